# Optimizing a Trainium2 kernel written in Bass

```python
import math
import jax, jax.numpy as jnp
from jax import lax
import numpy as np

D_MODEL = 1024
BATCH = 16
SEQ = 2048
DEPTH = 4

HEAD_DIM = 64
ROT_DIM = HEAD_DIM // 4
ROPE_THETA = 500000.0
BLOCK = 128
NORM_EPS = 1e-6

RWKV_HEADS = 6
RWKV_WIDTH = RWKV_HEADS * HEAD_DIM
DECAY_RANK = 64
ICL_RANK = 64
GN_EPS = 64e-5
DECAY_SCALE = math.exp(-0.5)

DSA_HEADS = 4
DSA_WIDTH = DSA_HEADS * HEAD_DIM
IDX_HEADS = 8
IDX_DIM = 32
IDX_ROT_DIM = IDX_DIM // 4
DSA_TOPK = 256

DIL_WINDOWS = (128, 512, 2048)
DIL_RATES = (1, 4, 16)
DIL_GROUPS = 3
DIL_HEADS_PER_GROUP = 2
DIL_HEADS = DIL_GROUPS * DIL_HEADS_PER_GROUP
DIL_WIDTH = DIL_HEADS * HEAD_DIM

MIX_WIDTH = RWKV_WIDTH + DSA_WIDTH + DIL_WIDTH
PLE_DIM = 256

RWKV_COLS = (RWKV_WIDTH, RWKV_WIDTH, RWKV_WIDTH, RWKV_WIDTH, DECAY_RANK, ICL_RANK)
RWKV_IN_WIDTH = 4 * RWKV_WIDTH + DECAY_RANK + ICL_RANK
DSA_COLS = (DSA_WIDTH, HEAD_DIM, HEAD_DIM, IDX_HEADS * IDX_DIM, IDX_DIM, IDX_HEADS, DSA_WIDTH)
DSA_IN_WIDTH = 2 * DSA_WIDTH + 2 * HEAD_DIM + IDX_HEADS * IDX_DIM + IDX_DIM + IDX_HEADS
DIL_COLS = (DIL_WIDTH, DIL_WIDTH, DIL_WIDTH, DIL_WIDTH)
DIL_IN_WIDTH = 4 * DIL_WIDTH
IN_COLS = RWKV_IN_WIDTH + DSA_IN_WIDTH + DIL_IN_WIDTH

kernel_name = 'hymba_rwkv7_dsa_dilated_trunk'


def rms_norm(x, g):
    xf = x.astype(jnp.float32)
    y = xf * lax.rsqrt(jnp.mean(xf * xf, axis=-1, keepdims=True) + NORM_EPS)
    return (y * g.astype(jnp.float32)).astype(x.dtype)


def split_cols(z, widths):
    out, off = [], 0
    for w in widths:
        out.append(z[..., off:off + w])
        off += w
    return out


def rope_tables(seq, rot_dim):
    half = rot_dim // 2
    inv = ROPE_THETA ** (-jnp.arange(half, dtype=jnp.float32) / half)
    ang = jnp.arange(seq, dtype=jnp.float32)[:, None] * inv[None, :]
    return jnp.cos(ang), jnp.sin(ang)


def partial_rope(x, cos, sin):
    half = cos.shape[-1]
    c = cos[None, :, None, :].astype(x.dtype)
    s = sin[None, :, None, :].astype(x.dtype)
    x1, x2, rest = x[..., :half], x[..., half:2 * half], x[..., 2 * half:]
    return jnp.concatenate([x1 * c - x2 * s, x2 * c + x1 * s, rest], axis=-1)


def token_shift_mix(z, mu):
    prev = jnp.pad(z, ((0, 0), (1, 0), (0, 0)))[:, :-1]
    return z + (prev - z) * mu


def rwkv7_branch(z, mu, w0, w_up, a0, a_up, k_k, k_a, r_k, ln_g, ln_b):
    B, S, _ = z.shape
    f32 = jnp.float32
    z = token_shift_mix(z, mu)
    r, k, v, g, wd, ad = split_cols(z, RWKV_COLS)
    heads = lambda t: t.astype(f32).reshape(B, S, RWKV_HEADS, HEAD_DIM)
    per_head = lambda t: t.astype(f32).reshape(RWKV_HEADS, HEAD_DIM)
    w = jnp.exp(-DECAY_SCALE * jax.nn.sigmoid((w0 + jnp.tanh(wd) @ w_up).astype(f32)))
    a = jax.nn.sigmoid((a0 + ad @ a_up).astype(f32))
    r, k, v, w, a = heads(r), heads(k), heads(v), heads(w), heads(a)
    kk = k * per_head(k_k)
    kk = kk / jnp.maximum(jnp.linalg.norm(kk, axis=-1, keepdims=True), 1e-12)
    k = k * (1.0 + (a - 1.0) * per_head(k_a))

    def step(state, inp):
        r_t, w_t, k_t, v_t, kk_t, a_t = inp
        sa = jnp.einsum('bhvk,bhk->bhv', state, -kk_t)
        state = (state * w_t[:, :, None, :]
                 + sa[..., None] * (kk_t * a_t)[:, :, None, :]
                 + v_t[..., None] * k_t[:, :, None, :])
        return state, jnp.einsum('bhvk,bhk->bhv', state, r_t)

    seq_first = lambda t: jnp.moveaxis(t, 1, 0)
    s0 = jnp.zeros((B, RWKV_HEADS, HEAD_DIM, HEAD_DIM), f32)
    _, y = lax.scan(step, s0, (seq_first(r), seq_first(w), seq_first(k), seq_first(v), seq_first(kk), seq_first(a)))
    y = jnp.moveaxis(y, 0, 1)
    mean = jnp.mean(y, axis=-1, keepdims=True)
    var = jnp.mean(jnp.square(y - mean), axis=-1, keepdims=True)
    y = ((y - mean) * lax.rsqrt(var + GN_EPS)).reshape(B, S, RWKV_WIDTH) * ln_g.astype(f32) + ln_b.astype(f32)
    bonus = jnp.sum(r * k * r_k.astype(f32), axis=-1, keepdims=True) * v
    y = y + bonus.reshape(B, S, RWKV_WIDTH)
    return (y * jax.nn.silu(g.astype(f32))).astype(z.dtype)


def dsa_branch(z, cos, sin, icos, isin):
    B, S, _ = z.shape
    f32 = jnp.float32
    q, k, v, iq, ik, iw, g = split_cols(z, DSA_COLS)
    q = partial_rope(q.reshape(B, S, DSA_HEADS, HEAD_DIM), cos, sin)
    k = partial_rope(k[:, :, None, :], cos, sin)[:, :, 0]
    iq = partial_rope(iq.reshape(B, S, IDX_HEADS, IDX_DIM), icos, isin)
    ik = partial_rope(ik[:, :, None, :], icos, isin)[:, :, 0]
    iw = iw * (IDX_HEADS ** -0.5)
    topk = min(DSA_TOPK, S // 4)
    nb = S // BLOCK
    key_pos = jnp.arange(S)
    gather = jax.vmap(lambda table, idx: table[idx])

    def block_fn(args):
        qb, iqb, iwb, start = args
        qpos = start + jnp.arange(BLOCK)
        sc = jnp.einsum('bqhd,bsd->bqhs', iqb, ik).astype(f32) * (IDX_DIM ** -0.5)
        sc = jnp.einsum('bqhs,bqh->bqs', jax.nn.relu(sc), iwb.astype(f32))
        sc = jnp.where((key_pos[None, :] <= qpos[:, None])[None], sc, -jnp.inf)
        _, idx = lax.top_k(sc, topk)
        valid = idx <= qpos[None, :, None]
        ks = gather(k, idx)
        vs = gather(v, idx)
        att = jnp.einsum('bqhd,bqkd->bqhk', qb, ks).astype(f32) * (HEAD_DIM ** -0.5)
        att = jnp.where(valid[:, :, None, :], att, -jnp.inf)
        probs = jax.nn.softmax(att, axis=-1).astype(vs.dtype)
        return jnp.einsum('bqhk,bqkd->bqhd', probs, vs)

    blocks = lambda t: jnp.moveaxis(t.reshape(B, nb, BLOCK, *t.shape[2:]), 1, 0)
    out = lax.map(block_fn, (blocks(q), blocks(iq), blocks(iw), jnp.arange(nb) * BLOCK))
    out = jnp.moveaxis(out, 0, 1).reshape(B, S, DSA_WIDTH)
    return out * jax.nn.silu(g)


def banded_attention(q, k, v, max_steps):
    N, n, H, dh = q.shape
    nb = -(-n // BLOCK)
    pad = nb * BLOCK - n
    to_blocks = lambda t: jnp.pad(t, ((0, 0), (0, pad), (0, 0), (0, 0))).reshape(N, nb, BLOCK, H, dh)
    qb, kb, vb = to_blocks(q), to_blocks(k), to_blocks(v)
    prev = lambda t: jnp.concatenate([jnp.zeros_like(t[:, :1]), t[:, :-1]], axis=1)
    kc = jnp.concatenate([prev(kb), kb], axis=2)
    vc = jnp.concatenate([prev(vb), vb], axis=2)
    s = jnp.einsum('nbqhd,nbkhd->nbhqk', qb, kc).astype(jnp.float32) * (HEAD_DIM ** -0.5)
    qi = jnp.arange(BLOCK)[:, None] + BLOCK
    kj = jnp.arange(2 * BLOCK)[None, :]
    dist = qi - kj
    band = (dist >= 0) & (dist <= max_steps)
    has_prev = (jnp.arange(nb)[:, None, None] > 0) | (kj[None] >= BLOCK)
    mask = band[None] & has_prev
    s = jnp.where(mask[None, :, None], s, -jnp.inf)
    m = jnp.max(s, axis=-1, keepdims=True)
    e = jnp.exp(s - m)
    l = jnp.sum(e, axis=-1, keepdims=True)
    o = jnp.einsum('nbhqk,nbkhd->nbqhd', (e / l).astype(v.dtype), vc)
    lse = (m + jnp.log(l))[..., 0]
    o = o.reshape(N, nb * BLOCK, H, dh)[:, :n]
    lse = jnp.moveaxis(lse, 2, 3).reshape(N, nb * BLOCK, H)[:, :n]
    return o, lse


def dilated_group(q, k, v, rate, steps):
    B, S, H, dh = q.shape
    n = S // rate
    to_res = lambda t: jnp.swapaxes(t.reshape(B, n, rate, H, dh), 1, 2).reshape(B * rate, n, H, dh)
    o, lse = banded_attention(to_res(q), to_res(k), to_res(v), steps)
    o = jnp.swapaxes(o.reshape(B, rate, n, H, dh), 1, 2).reshape(B, S, H, dh)
    lse = jnp.swapaxes(lse.reshape(B, rate, n, H), 1, 2).reshape(B, S, H)
    return o, lse


def dilated_branch(z, cos, sin):
    B, S, _ = z.shape
    q, k, v, g = split_cols(z, DIL_COLS)
    heads = lambda t: t.reshape(B, S, DIL_HEADS, HEAD_DIM)
    q = partial_rope(heads(q), cos, sin)
    k = partial_rope(heads(k), cos, sin)
    v = heads(v)
    outs, lses = [], []
    for gi in range(DIL_GROUPS):
        hs = slice(gi * DIL_HEADS_PER_GROUP, (gi + 1) * DIL_HEADS_PER_GROUP)
        o, lse = dilated_group(q[:, :, hs], k[:, :, hs], v[:, :, hs], DIL_RATES[gi], DIL_WINDOWS[gi] // DIL_RATES[gi])
        outs.append(o)
        lses.append(lse)
    o = jnp.stack(outs, axis=2)
    alpha = jax.nn.softmax(jnp.stack(lses, axis=2), axis=2)
    o = (o * alpha[..., None].astype(o.dtype)).reshape(B, S, DIL_WIDTH)
    return o * jax.nn.silu(g)


def setup_inputs(seed: int = 0) -> dict:
    key = jax.random.key(seed)
    ks = jax.random.split(key, 20)
    f32 = jnp.float32
    nrm = lambda kk, shape, scale: scale * jax.random.normal(kk, shape, f32)
    L = DEPTH
    return {
        'x': nrm(ks[0], (BATCH, SEQ, D_MODEL), 1.0),
        'p': nrm(ks[1], (DEPTH, BATCH, SEQ, PLE_DIM), 1.0),
        'norm_g': 1.0 + nrm(ks[2], (L, D_MODEL), 0.05),
        'w_in': nrm(ks[3], (L, D_MODEL, IN_COLS), D_MODEL ** -0.5),
        'tshift_mu': jax.random.uniform(ks[4], (L, RWKV_IN_WIDTH), f32),
        'rwkv_w0': jax.random.uniform(ks[5], (L, RWKV_WIDTH), f32, -6.0, 1.0),
        'rwkv_w_up': nrm(ks[6], (L, DECAY_RANK, RWKV_WIDTH), DECAY_RANK ** -0.5),
        'rwkv_a0': nrm(ks[7], (L, RWKV_WIDTH), 0.5),
        'rwkv_a_up': nrm(ks[8], (L, ICL_RANK, RWKV_WIDTH), ICL_RANK ** -0.5),
        'rwkv_k_k': 0.85 + nrm(ks[9], (L, RWKV_WIDTH), 0.05),
        'rwkv_k_a': 1.0 + nrm(ks[10], (L, RWKV_WIDTH), 0.05),
        'rwkv_r_k': nrm(ks[11], (L, RWKV_HEADS, HEAD_DIM), 0.1),
        'rwkv_ln_g': 1.0 + nrm(ks[12], (L, RWKV_WIDTH), 0.05),
        'rwkv_ln_b': nrm(ks[13], (L, RWKV_WIDTH), 0.01),
        'w_out': nrm(ks[14], (L, MIX_WIDTH, D_MODEL), 0.5 * MIX_WIDTH ** -0.5),
        'ple_norm_g': 1.0 + nrm(ks[15], (L, D_MODEL), 0.05),
        'ple_w_gate': nrm(ks[16], (L, D_MODEL, D_MODEL), D_MODEL ** -0.5),
        'ple_w_proj': nrm(ks[17], (L, PLE_DIM, D_MODEL), 0.5 * PLE_DIM ** -0.5),
        'final_norm_g': 1.0 + nrm(ks[18], (D_MODEL,), 0.05),
    }


def reference(x, p, norm_g, w_in, tshift_mu, rwkv_w0, rwkv_w_up, rwkv_a0, rwkv_a_up, rwkv_k_k, rwkv_k_a,
              rwkv_r_k, rwkv_ln_g, rwkv_ln_b, w_out, ple_norm_g, ple_w_gate, ple_w_proj, final_norm_g):
    S = x.shape[1]
    cos, sin = rope_tables(S, ROT_DIM)
    icos, isin = rope_tables(S, IDX_ROT_DIM)
    for i in range(DEPTH):
        h = rms_norm(x, norm_g[i])
        z = h @ w_in[i]
        z_a, z_b, z_c = split_cols(z, (RWKV_IN_WIDTH, DSA_IN_WIDTH, DIL_IN_WIDTH))
        y_a = rwkv7_branch(z_a, tshift_mu[i], rwkv_w0[i], rwkv_w_up[i], rwkv_a0[i], rwkv_a_up[i],
                           rwkv_k_k[i], rwkv_k_a[i], rwkv_r_k[i], rwkv_ln_g[i], rwkv_ln_b[i])
        y_b = dsa_branch(z_b, cos, sin, icos, isin)
        y_c = dilated_branch(z_c, cos, sin)
        y = jnp.concatenate([y_a, y_b, y_c], axis=-1)
        x = x + y @ w_out[i]
        gate = jax.nn.sigmoid(rms_norm(x, ple_norm_g[i]) @ ple_w_gate[i])
        x = x + gate * (p[i] @ ple_w_proj[i])
    return rms_norm(x, final_norm_g)
```

```python
import numpy as np
import concourse.bass as bass
import concourse.mybir as mybir
from concourse.bass_utils import run_bass_kernel_spmd

F32 = mybir.dt.float32
BF16 = mybir.dt.bfloat16
AF = mybir.ActivationFunctionType
ALU = mybir.AluOpType
AX = mybir.AxisListType


class Buf:
    __slots__ = ("name", "w", "r", "excl")

    def __init__(self, name, excl=False):
        self.name = name
        self.excl = excl
        self.w = None
        self.r = {}


class Prog:
    ENGS = ("pe", "act", "dve", "pool", "sp")
    NDMA = 6

    def __init__(self, nc, stack=None, marks=None, dry=False):
        self.nc = nc
        self.stack = stack
        self.marks = marks
        self.dry = dry
        self.waited = set()
        self.rank = {}
        self.val_of = {}
        self.sems = {}
        self.engobj = {"pe": nc.tensor, "act": nc.scalar, "dve": nc.vector, "pool": nc.gpsimd, "sp": nc.sync}
        self.streams = {e: [] for e in self.ENGS}
        self.count = {e: 0 for e in self.ENGS}
        self.known = {e: {} for e in self.ENGS}
        self.dma_cnt = {}
        self.dma_rr = {e: 0 for e in self.ENGS}
        self.semkeys = set()
        self.bufs = {}
        self.bank_last = {}
        self.bank_rows = {}
        self.since_barrier = 100
        self.epoch = 0
        self.ccount = {}

    def buf(self, name, excl=False):
        return Buf(name, excl)

    def _need(self, eng, dep, waits):
        if dep is None:
            return
        key, val = dep[0], dep[1]
        if self.known[eng].get(key, 0) >= val:
            return
        self.known[eng][key] = val
        self.waited.add((key, val))
        for i, (k, v) in enumerate(waits):
            if k == key:
                waits[i] = (k, max(v, val))
                return
        waits.append((key, val))

    def _deps(self, eng, reads, writes, tag):
        waits = []
        for b in reads:
            if b.w is not None:
                self._need(eng, b.w, waits)
        for b in writes:
            if b.w is not None:
                if b.w[2] == tag and tag != "dma":
                    pass
                else:
                    self._need(eng, b.w, waits)
            for key, (val, reng) in b.r.items():
                if reng == tag and tag != "dma":
                    continue
                self._need(eng, (key, val), waits)
        return waits

    def _mark(self, done, eng, reads, writes):
        key, val = done
        for b in reads:
            old = b.r.get(key)
            if old is None or old[0] < val:
                b.r[key] = (val, eng)
        for b in writes:
            b.w = (key, val, eng)
            b.r = {}

    def op(self, eng, fn, reads=(), writes=(), acc=None, rows=None):
        assert eng in ("pe", "act", "dve", "pool")
        ex = [b_ for b_ in reads if b_.excl]
        if ex:
            reads = [b_ for b_ in reads if not b_.excl]
            writes = list(writes) + ex
        waits = self._deps(eng, reads, writes, eng)
        if eng == "pe":
            for wb_ in writes:
                lastr = self.bank_rows.get(id(wb_))
                if lastr is not None and rows is not None and lastr[0] is not None and lastr[0] != rows:
                    self._need("pe", lastr[1], waits)
                pk_ = "c_pe_%d" % self.epoch
                self.bank_rows[id(wb_)] = (rows, (pk_, self.ccount.get(pk_, 0) + 1))
        if eng == "pe":
            is_start = True if acc is None else acc[1]
            for wb_ in writes:
                last = self.bank_last.get(id(wb_))
                if is_start and last:
                    self._need("pe", last, waits)
                    self.bank_last[id(wb_)] = None
                if acc is not None:
                    pk = "c_pe_%d" % self.epoch
                    self.bank_last[id(wb_)] = (pk, self.ccount.get(pk, 0) + 1)
        key = "c_%s_%d" % (eng, self.epoch)
        self.ccount[key] = self.ccount.get(key, 0) + 1
        self.count[eng] = self.ccount[key]
        self.semkeys.add(key)
        done = (key, self.ccount[key])
        self._emit(eng, waits, fn, (key, 1))
        self._mark(done, eng, reads, writes)

    def dma(self, fn, reads=(), writes=(), q="sp"):
        if q == "sp" and writes and self.since_barrier < 12:
            self.since_barrier += 1
            self._dma(fn, reads, writes, q)
        return self._dma(fn, reads, writes, q)

    def _dma(self, fn, reads=(), writes=(), q="sp"):
        waits = self._deps(q, reads, writes, "dma")
        j = self.dma_rr[q] % self.NDMA
        self.dma_rr[q] += 1
        key = "d_%s_%d" % (q, j)
        self.semkeys.add(key)
        prev = self.dma_cnt.get(key, 0)
        if prev > 0:
            self._need(q, (key, prev * 16), waits)
        self.dma_cnt[key] = prev + 1
        done = (key, (prev + 1) * 16)
        self._emit(q, waits, fn, (key, 16))
        self._mark(done, "dma", reads, writes)
        return done

    def wait_all(self, eng, dones):
        waits = []
        for d in dones:
            self._need(eng, d, waits)
        self._emit(eng, waits, None, None)

    def _sem(self, key):
        s = self.sems.get(key)
        if s is None:
            s = self.stack.enter_context(self.nc.semaphore(key))
            self.sems[key] = s
        return s

    def _emit(self, eng, waits, fn, inc):
        if self.dry:
            return
        e = self.engobj[eng]
        tw = []
        for (k, v) in waits:
            if k.startswith("c_") and self.marks is not None:
                v = self.val_of[(k, v)]
            tw.append((k, v))
        attach = fn is not None and inc[0].startswith("c_") and len(tw) > 0
        for (k, v) in (tw[:-1] if attach else tw):
            e.wait_ge(self._sem(k), v)
        if fn is None:
            return
        ins = fn(e)
        if attach:
            ins._wait_ge(self._sem(tw[-1][0]), tw[-1][1])
        if inc[0].startswith("c_") and self.marks is not None:
            idx = self.ccount[inc[0]]
            if (inc[0], idx) in self.marks:
                self.rank[inc[0]] = self.rank.get(inc[0], 0) + 1
                self.val_of[(inc[0], idx)] = self.rank[inc[0]]
                ins.then_inc(self._sem(inc[0]), 1)
        else:
            ins.then_inc(self._sem(inc[0]), inc[1])

    def new_epoch(self):
        self.epoch += 1

    def barrier(self):
        dones = [(k_, c) for k_, c in self.ccount.items() if c > 0]
        dones += [(k, c * 16) for k, c in self.dma_cnt.items()]
        for e in self.ENGS:
            self.wait_all(e, dones)
        self.since_barrier = 0
        if getattr(self, "settle", None) is not None:
            fn = self.settle
            self.settle = None
            for _ in range(self.NDMA):
                d = fn()
                self.since_barrier = 0
                for e in self.ENGS:
                    self.wait_all(e, [d])
            self.settle = fn


from contextlib import ExitStack
import math

D = 1024
HD = 64
RW = 384
RWIN = 1664
INC = 4136
NORM_EPS = 1e-6
GN_EPS = 64e-5
DECAY_SCALE = math.exp(-0.5)
NEG = -30000.0
SKIP_PHASES = ()


class K:
    pass


def build(NB, S, DEPTH, dbg=()):
    _, waited = _build(NB, S, DEPTH, dbg, None, True)
    nc, _ = _build(NB, S, DEPTH, dbg, waited, False)
    return nc


def _build(NB, S, DEPTH, dbg, marks, dry):
    T = NB * S
    NT = T // 128
    TS = S // 128
    nc = bass.Bass("TRN2", target_bir_lowering=False)
    k = K()
    k.nc = nc
    k.uid = 0
    k.skip = SKIP_PHASES

    def sbt(name, shape, dt):
        k.uid += 1
        return nc.sbuf_tensor("%s_u%d" % (name, k.uid), shape, dt)
    k.sbt = sbt
    k.fills = {}

    def fill(v):
        if v not in k.fills:
            k.fills[v] = nc.gpsimd.to_reg(float(v))
        return k.fills[v]
    k.fill = fill

    def din(name, shape, dt=F32):
        return nc.dram_tensor(name, list(shape), dt, kind="ExternalInput").ap()

    def dscr(name, shape, dt=F32):
        kind = "ExternalOutput" if name in dbg else "Internal"
        return nc.dram_tensor(name, list(shape), dt, kind=kind).ap()

    x_in = din("x", [T, D])
    p_in = din("p", [DEPTH, T, 256])
    norm_g = din("norm_g", [DEPTH, 128, 8])
    w_in = din("w_in", [DEPTH, D, INC])
    w_out = din("w_out", [DEPTH, D, D])
    ple_g = din("ple_norm_g", [DEPTH, 128, 8])
    ple_wg = din("ple_w_gate", [DEPTH, D, D])
    ple_wp = din("ple_w_proj", [DEPTH, 256, D])
    fin_g = din("final_norm_g", [1, D])
    rw_in = {"mu": din("rw_mu", [DEPTH, 128, 13]), "par": din("rw_par", [DEPTH, 128, 5, 3]), "w_up": din("rw_w_up", [DEPTH, 64, 384]),
             "a_up": din("rw_a_up", [DEPTH, 64, 384]), "ln_g": din("rw_ln_g", [DEPTH, 1, 384]), "ln_b": din("rw_ln_b", [DEPTH, 1, 384])}
    rope = din("rope", [S, 24])
    out = nc.dram_tensor("out", [T, D], F32, kind="ExternalOutput").ap()

    xres = dscr("xres", [T, D])
    zr = dscr("zr", [T, RWIN])
    dq = dscr("dq", [T, 384], BF16)
    di = dscr("di", [T, 288], BF16)
    iwd = dscr("iwd", [T, 8])
    gates = dscr("gates", [T, 640], BF16)
    cq = dscr("cq", [T, 384], BF16)
    ck = dscr("ck", [T, 384], BF16)
    cv = dscr("cv", [T, 384], BF16)
    ymix = dscr("ymix", [T, D], BF16)
    oaug = dscr("oaug", [T, 3, 2, 80])

    with ExitStack() as st:
        P = Prog(nc, st, marks=marks, dry=dry)
        k.P = P

        def sb(stack, name, shape, dt):
            t = stack.enter_context(k.sbt(name, list(shape), dt))
            return t, P.buf(name)

        psb = []
        for i in range(8):
            t = st.enter_context(nc.psum_tensor("psb%d" % i, [128, 512], F32))
            psb.append((t, P.buf("psb%d" % i, excl=True)))
        identf, b_identf = sb(st, "identf", [128, 128], F32)
        ident, b_ident = sb(st, "ident", [128, 128], BF16)
        ropet, b_ropet = sb(st, "ropet", [128, TS, 24], F32)
        P.op("pool", lambda e: e.memset(identf[:], 0.0), writes=[b_identf])
        P.op("pool", lambda e: e.affine_select(out=identf[:], in_=identf[:], pattern=[[-1, 128]],
                                               compare_op=ALU.not_equal, fill=k.fill(1.0), base=0,
                                               channel_multiplier=1), reads=[b_identf], writes=[b_identf])
        P.op("dve", lambda e: e.tensor_copy(ident[:], identf[:]), reads=[b_identf], writes=[b_ident])
        P.dma(lambda e: e.dma_start(out=ropet[:], in_=rope.rearrange("(t p) c -> p t c", p=128)), writes=[b_ropet])
        k.psb, k.ident, k.b_ident, k.identf, k.b_identf = psb, ident, b_ident, identf, b_identf
        dummy, b_dummy = sb(st, "dummyt", [128, 64], F32)
        P.settle_unused = lambda: P.dma(lambda e: e.dma_start(out=dummy[:], in_=rope[0:128, 0:16].bitcast(F32) if False else x_in[0:128, 0:64]), writes=[b_dummy])

        def rms_stats(eng_sq, xt, b_xt, junk, b_junk, ss, b_ss, rs, b_rs):
            P.op("act", lambda e: e.activation(out=junk, in_=xt, func=AF.Square, accum_out=ss),
                 reads=[b_xt], writes=[b_junk, b_ss])
            P.op("dve", lambda e: e.tensor_scalar(out=rs, in0=ss, scalar1=1.0 / D, scalar2=NORM_EPS,
                                                  op0=ALU.mult, op1=ALU.add), reads=[b_ss], writes=[b_rs])
            P.op("act", lambda e: e.activation(out=rs, in_=rs, func=AF.Sqrt), reads=[b_rs], writes=[b_rs])
            P.op("dve", lambda e: e.reciprocal(out=rs, in_=rs), reads=[b_rs], writes=[b_rs])

        def norm_transpose(xt, b_xt, rs, b_rs, hs, b_hs, gfm, b_gfm, dst_fn, b_dst, pa, pb):
            P.op("act", lambda e: e.activation(out=hs, in_=xt, func=AF.Copy, scale=rs),
                 reads=[b_xt, b_rs], writes=[b_hs])
            for half, (pt, b_pt) in enumerate((pa, pb)):
                tpv = pt[:].rearrange("p (c t) -> p c t", c=4)
                for c in range(4):
                    cc = half * 4 + c
                    P.op("pe", lambda e, c=c, cc=cc, tpv=tpv: e.transpose(out=tpv[:, c, :], in_=hs[:, cc * 128:(cc + 1) * 128],
                                                                          identity=identf[:]),
                         reads=[b_hs, b_identf], writes=[b_pt])
                P.op("dve", lambda e, tpv=tpv, half=half: e.tensor_tensor(
                    out=dst_fn(half), in0=tpv, in1=gfm[:, half * 4:half * 4 + 4].unsqueeze(2).to_broadcast([128, 4, 128]),
                    op=ALU.mult), reads=[b_pt, b_gfm], writes=[b_dst])

        def rope_apply(v, nh, half, ti, tmp, b_tmp, b_v, tab_off):
            c = ropet[:, ti, tab_off:tab_off + half].unsqueeze(1).to_broadcast([128, nh, half])
            s = ropet[:, ti, tab_off + half:tab_off + 2 * half].unsqueeze(1).to_broadcast([128, nh, half])
            x1 = v[:, :, 0:half]
            x2 = v[:, :, half:2 * half]
            tv = [tmp[:, j, 0:nh * half].rearrange("p (h d) -> p h d", h=nh) for j in range(4)]
            eng = "dve"
            P.op(eng, lambda e: e.tensor_tensor(out=tv[0], in0=x1, in1=c, op=ALU.mult), reads=[b_v, b_ropet], writes=[b_tmp])
            P.op(eng, lambda e: e.tensor_tensor(out=tv[1], in0=x2, in1=s, op=ALU.mult), reads=[b_v, b_ropet], writes=[b_tmp])
            P.op(eng, lambda e: e.tensor_tensor(out=tv[2], in0=x2, in1=c, op=ALU.mult), reads=[b_v, b_ropet], writes=[b_tmp])
            P.op(eng, lambda e: e.tensor_tensor(out=tv[3], in0=x1, in1=s, op=ALU.mult), reads=[b_v, b_ropet], writes=[b_tmp])
            P.op(eng, lambda e: e.tensor_tensor(out=x1, in0=tv[0], in1=tv[1], op=ALU.subtract), reads=[b_tmp], writes=[b_v])
            P.op(eng, lambda e: e.tensor_tensor(out=x2, in0=tv[2], in1=tv[3], op=ALU.add), reads=[b_tmp], writes=[b_v])

        for l in range(DEPTH):
            xsrc = x_in if l == 0 else xres
            if l > 0:
                P.new_epoch()
            with ExitStack() as ph:
                hT, b_hT = sb(ph, "hT", [128, 8, T], BF16)
                gfm, b_gfm = sb(ph, "gfm", [128, 8], F32)
                P.dma(lambda e: e.dma_start(out=gfm[:], in_=norm_g[l]), writes=[b_gfm])
                xts = [sb(ph, "xt%d" % j, [128, D], F32) for j in range(2)]
                hss = [sb(ph, "hs%d" % j, [128, D], F32) for j in range(2)]
                sss = [sb(ph, "ss%d" % j, [128, 1], F32) for j in range(2)]
                rss = [sb(ph, "rs%d" % j, [128, 1], F32) for j in range(2)]
                for i in range(NT):
                    (xt, b_xt), (hs, b_hs), (ss, b_ss), (rs, b_rs) = xts[i % 2], hss[i % 2], sss[i % 2], rss[i % 2]
                    P.dma(lambda e, i=i, xt=xt: e.dma_start(out=xt[:], in_=xsrc[i * 128:(i + 1) * 128, :]), writes=[b_xt])
                    rms_stats("act", xt[:], b_xt, hs[:], b_hs, ss[:], b_ss, rs[:], b_rs)
                    norm_transpose(xt[:], b_xt, rs[:], b_rs, hs[:], b_hs, gfm, b_gfm,
                                   lambda half, i=i: hT[:, half * 4:half * 4 + 4, i * 128:(i + 1) * 128], b_hT,
                                   psb[(i % 2) * 2], psb[(i % 2) * 2 + 1])
                groups = [(0, 512, "zr"), (512, 512, "zr"), (1024, 512, "zr"), (1536, 128, "zr"),
                          (1664, 384, "dq"), (2048, 296, "di"), (2344, 256, "g0"),
                          (2600, 384, "cq"), (2984, 384, "ck"), (3368, 384, "cv"), (3752, 384, "g1")]
                wbs = [sb(ph, "wb%d" % j, [128, 8, 512], BF16) for j in range(2)]
                st32 = [sb(ph, "st32_%d" % j, [128, 512], F32) for j in range(4)]
                stb = [sb(ph, "stb_%d" % j, [128, 512], BF16) for j in range(4)]
                rtmp = [sb(ph, "rtmp%d" % j, [128, 4, 64], F32) for j in range(4)]
                cnt = 0
                for gi, (c0, n, kind) in enumerate(groups):
                    wb, b_wb = wbs[gi % 2]
                    P.dma(lambda e, wb=wb, c0=c0, n=n: e.dma_start(
                        out=wb[:, :, 0:n], in_=w_in[l, :, c0:c0 + n].rearrange("(c p) n -> p c n", p=128)),
                        writes=[b_wb], q="pool")
                    for i in range(NT):
                        ps, b_ps = psb[4 + (cnt % 4)]
                        s32, b_s32 = st32[cnt % 4]
                        s16, b_s16 = stb[cnt % 4]
                        rt, b_rt = rtmp[cnt % 4]
                        cnt += 1
                        for c in range(8):
                            P.op("pe", lambda e, c=c, ps=ps, wb=wb, i=i, n=n: e.matmul(
                                ps[:, 0:n], lhsT=hT[:, c, i * 128:(i + 1) * 128], rhs=wb[:, c, 0:n],
                                start=(c == 0), stop=(c == 7)), reads=[b_hT, b_wb], writes=[b_ps], acc=(b_ps, c == 0))
                        rows = slice(i * 128, (i + 1) * 128)
                        ti = i % TS
                        if kind == "zr":
                            ev = "act" if (cnt % 2) else "dve"
                            if ev == "act":
                                P.op("act", lambda e, s32=s32, ps=ps, n=n: e.activation(out=s32[:, 0:n], in_=ps[:, 0:n], func=AF.Copy),
                                     reads=[b_ps], writes=[b_s32])
                            else:
                                P.op("dve", lambda e, s32=s32, ps=ps, n=n: e.tensor_copy(s32[:, 0:n], ps[:, 0:n]),
                                     reads=[b_ps], writes=[b_s32])
                            P.dma(lambda e, s32=s32, rows=rows, c0=c0, n=n: e.dma_start(out=zr[rows, c0:c0 + n], in_=s32[:, 0:n]),
                                  reads=[b_s32], q="act")
                        elif kind in ("g0", "g1"):
                            goff = 0 if kind == "g0" else 256
                            P.op("act", lambda e, s16=s16, ps=ps, n=n: e.activation(out=s16[:, 0:n], in_=ps[:, 0:n], func=AF.Silu),
                                 reads=[b_ps], writes=[b_s16])
                            P.dma(lambda e, s16=s16, rows=rows, goff=goff, n=n: e.dma_start(out=gates[rows, goff:goff + n], in_=s16[:, 0:n]),
                                  reads=[b_s16], q="act")
                        elif kind == "cv":
                            P.op("dve", lambda e, s16=s16, ps=ps, n=n: e.tensor_copy(s16[:, 0:n], ps[:, 0:n]),
                                 reads=[b_ps], writes=[b_s16])
                            P.dma(lambda e, s16=s16, rows=rows, n=n: e.dma_start(out=cv[rows, :], in_=s16[:, 0:n]),
                                  reads=[b_s16], q="act")
                        else:
                            P.op("act", lambda e, s32=s32, ps=ps, n=n: e.activation(out=s32[:, 0:n], in_=ps[:, 0:n], func=AF.Copy),
                                 reads=[b_ps], writes=[b_s32])
                            if kind == "dq":
                                v = s32[:, 0:320].rearrange("p (h d) -> p h d", h=5)
                                rope_apply(v, 5, 8, ti, rt, b_rt, b_s32, 0)
                                dst, nn = dq, 384
                            elif kind == "di":
                                v = s32[:, 0:288].rearrange("p (h d) -> p h d", h=9)
                                rope_apply(v, 9, 4, ti, rt, b_rt, b_s32, 16)
                                dst, nn = di, 288
                                P.dma(lambda e, s32=s32, rows=rows: e.dma_start(out=iwd[rows, :], in_=s32[:, 288:296]),
                                      reads=[b_s32], q="act")
                            else:
                                v = s32[:, 0:384].rearrange("p (h d) -> p h d", h=6)
                                rope_apply(v, 6, 8, ti, rt, b_rt, b_s32, 0)
                                dst, nn = (cq if kind == "cq" else ck), 384
                            P.op("dve", lambda e, s16=s16, s32=s32, nn=nn: e.tensor_copy(s16[:, 0:nn], s32[:, 0:nn]),
                                 reads=[b_s32], writes=[b_s16])
                            P.dma(lambda e, s16=s16, rows=rows, dst=dst, nn=nn: e.dma_start(out=dst[rows, :], in_=s16[:, 0:nn]),
                                  reads=[b_s16], q="act")
                P.barrier()

            MIXERS(k, l, locals())
            P.barrier()

            with ExitStack() as ph:
                wo, b_wo = sb(ph, "wo", [128, 8, D], BF16)
                wg, b_wg = sb(ph, "wg", [128, 8, D], BF16)
                wp, b_wp = sb(ph, "wp", [128, 2, D], BF16)
                gfm, b_gfm = sb(ph, "gfm2", [128, 8], F32)
                P.dma(lambda e: e.dma_start(out=wo[:], in_=w_out[l].rearrange("(c p) n -> p c n", p=128)), writes=[b_wo], q="pool")
                P.dma(lambda e: e.dma_start(out=wg[:], in_=ple_wg[l].rearrange("(c p) n -> p c n", p=128)), writes=[b_wg], q="pool")
                P.dma(lambda e: e.dma_start(out=wp[:], in_=ple_wp[l].rearrange("(c p) n -> p c n", p=128)), writes=[b_wp], q="pool")
                P.dma(lambda e: e.dma_start(out=gfm[:], in_=ple_g[l]), writes=[b_gfm])
                last = (l == DEPTH - 1)
                if last:
                    fg, b_fg = sb(ph, "fg", [128, D], F32)
                    P.dma(lambda e: e.dma_start(out=fg[:], in_=fin_g.to_broadcast([128, D])), writes=[b_fg])
                xts = [sb(ph, "oxt%d" % j, [128, D], F32) for j in range(2)]
                yts = [sb(ph, "oyt%d" % j, [128, D], BF16) for j in range(2)]
                pts = [sb(ph, "opt%d" % j, [128, 256], F32) for j in range(2)]
                yTs = [sb(ph, "oyT%d" % j, [128, 8, 128], BF16) for j in range(2)]
                hTs = [sb(ph, "ohT%d" % j, [128, 8, 128], BF16) for j in range(2)]
                pTs = [sb(ph, "opT%d" % j, [128, 2, 128], BF16) for j in range(2)]
                hss = [sb(ph, "ohs%d" % j, [128, D], F32) for j in range(2)]
                x1s = [sb(ph, "ox1%d" % j, [128, D], F32) for j in range(2)]
                gss = [sb(ph, "ogs%d" % j, [128, D], F32) for j in range(2)]
                sss = [sb(ph, "oss%d" % j, [128, 1], F32) for j in range(2)]
                rss = [sb(ph, "ors%d" % j, [128, 1], F32) for j in range(2)]
                odone = []
                for i in range(NT):
                    j = i % 2
                    (xt, b_xt), (yt, b_yt), (pt, b_pt), (yT, b_yT), (hT2, b_hT2), (pT, b_pT) = xts[j], yts[j], pts[j], yTs[j], hTs[j], pTs[j]
                    (hs, b_hs), (x1, b_x1), (gs, b_gs), (ss, b_ss), (rs, b_rs) = hss[j], x1s[j], gss[j], sss[j], rss[j]
                    rows = slice(i * 128, (i + 1) * 128)
                    P.dma(lambda e, xt=xt, rows=rows: e.dma_start(out=xt[:], in_=xsrc[rows, :]), writes=[b_xt])
                    P.dma(lambda e, yt=yt, rows=rows: e.dma_start(out=yt[:], in_=ymix[rows, :]), writes=[b_yt])
                    P.dma(lambda e, pt=pt, rows=rows: e.dma_start(out=pt[:], in_=p_in[l, rows, :]), writes=[b_pt])
                    tb, b_tb = psb[0]
                    tbv = tb[:].bitcast(BF16).rearrange("p (c t) -> p c t", t=128)
                    for c in range(8):
                        P.op("pe", lambda e, c=c, yt=yt, tbv=tbv: e.transpose(out=tbv[:, c, :], in_=yt[:, c * 128:(c + 1) * 128], identity=ident[:]),
                             reads=[b_yt, b_ident], writes=[b_tb])
                    P.op("dve", lambda e, yT=yT, tbv=tbv: e.tensor_copy(yT[:], tbv[:, 0:8, :]), reads=[b_tb], writes=[b_yT])
                    for hf in range(2):
                        ps, b_ps = psb[2 + hf]
                        for c in range(8):
                            P.op("pe", lambda e, c=c, ps=ps, yT=yT, hf=hf: e.matmul(ps[:], lhsT=yT[:, c, :], rhs=wo[:, c, hf * 512:(hf + 1) * 512],
                                                                                start=(c == 0), stop=(c == 7)),
                                 reads=[b_yT, b_wo], writes=[b_ps], acc=(b_ps, c == 0))
                        P.op("dve", lambda e, ps=ps, x1=x1, xt=xt, hf=hf: e.tensor_tensor(out=x1[:, hf * 512:(hf + 1) * 512], in0=ps[:],
                                                                                      in1=xt[:, hf * 512:(hf + 1) * 512], op=ALU.add),
                             reads=[b_ps, b_xt], writes=[b_x1])
                    rms_stats("act", x1[:], b_x1, hs[:], b_hs, ss[:], b_ss, rs[:], b_rs)
                    norm_transpose(x1[:], b_x1, rs[:], b_rs, hs[:], b_hs, gfm, b_gfm,
                                   lambda half, hT2=hT2: hT2[:, half * 4:half * 4 + 4, :], b_hT2, psb[4], psb[5])
                    tb2, b_tb2 = psb[1]
                    tb2v = tb2[:].rearrange("p (c t) -> p c t", t=128)
                    for c in range(2):
                        P.op("pe", lambda e, c=c, pt=pt, tb2v=tb2v: e.transpose(out=tb2v[:, c, :], in_=pt[:, c * 128:(c + 1) * 128], identity=identf[:]),
                             reads=[b_pt, b_identf], writes=[b_tb2])
                    P.op("dve", lambda e, pT=pT, tb2v=tb2v: e.tensor_copy(pT[:], tb2v[:, 0:2, :]), reads=[b_tb2], writes=[b_pT])
                    for hf in range(2):
                        psg, b_psg = psb[6]
                        psp, b_psp = psb[7]
                        cols = slice(hf * 512, (hf + 1) * 512)
                        for c in range(8):
                            P.op("pe", lambda e, c=c, psg=psg, hT2=hT2, cols=cols: e.matmul(psg[:], lhsT=hT2[:, c, :], rhs=wg[:, c, cols],
                                                                                       start=(c == 0), stop=(c == 7)),
                                 reads=[b_hT2, b_wg], writes=[b_psg], acc=(b_psg, c == 0))
                        for c in range(2):
                            P.op("pe", lambda e, c=c, psp=psp, pT=pT, cols=cols: e.matmul(psp[:], lhsT=pT[:, c, :], rhs=wp[:, c, cols],
                                                                                     start=(c == 0), stop=(c == 1)),
                                 reads=[b_pT, b_wp], writes=[b_psp], acc=(b_psp, c == 0))
                        P.op("act", lambda e, gs=gs, psg=psg, cols=cols: e.activation(out=gs[:, cols], in_=psg[:], func=AF.Sigmoid),
                             reads=[b_psg], writes=[b_gs])
                        P.op("dve", lambda e, gs=gs, psp=psp, cols=cols: e.tensor_tensor(out=gs[:, cols], in0=psp[:], in1=gs[:, cols], op=ALU.mult),
                             reads=[b_psp, b_gs], writes=[b_gs])
                        P.op("dve", lambda e, gs=gs, x1=x1, cols=cols: e.tensor_tensor(out=x1[:, cols], in0=x1[:, cols], in1=gs[:, cols], op=ALU.add),
                             reads=[b_gs, b_x1], writes=[b_x1])
                    if not last:
                        odone.append(P.dma(lambda e, x1=x1, rows=rows: e.dma_start(out=xres[rows, :], in_=x1[:]), reads=[b_x1], q="act"))
                    else:
                        rms_stats("act", x1[:], b_x1, hs[:], b_hs, ss[:], b_ss, rs[:], b_rs)
                        P.op("dve", lambda e, hs=hs, x1=x1, rs=rs: e.scalar_tensor_tensor(out=hs[:], in0=x1[:], scalar=rs[:], in1=fg[:],
                                                                                       op0=ALU.mult, op1=ALU.mult),
                             reads=[b_x1, b_rs, b_fg], writes=[b_hs])
                        odone.append(P.dma(lambda e, hs=hs, rows=rows: e.dma_start(out=out[rows, :], in_=hs[:]), reads=[b_hs], q="act"))
                P.barrier()
        P.barrier()
    return nc, P.waited


DIL_RATES = (1, 4, 16)


def make_masks(k, st):
    nc, P = k.nc, k.P
    mf = st.enter_context(k.sbt("maskf", [128, 2, 128], F32)); b_mf = P.buf("maskf")
    mk = st.enter_context(k.sbt("maskb", [128, 2, 128], BF16)); b_mk = P.buf("maskb")
    P.op("pool", lambda e: e.memset(mf[:], 1.0), writes=[b_mf])
    P.op("pool", lambda e: e.affine_select(out=mf[:, 0, :], in_=mf[:, 0, :], pattern=[[1, 128]], compare_op=ALU.is_ge,
                                           fill=k.fill(0.0), base=0, channel_multiplier=-1), reads=[b_mf], writes=[b_mf])
    P.op("pool", lambda e: e.affine_select(out=mf[:, 1, :], in_=mf[:, 1, :], pattern=[[-1, 128]], compare_op=ALU.is_ge,
                                           fill=k.fill(0.0), base=0, channel_multiplier=1), reads=[b_mf], writes=[b_mf])
    P.op("dve", lambda e: e.tensor_copy(mk[:], mf[:]), reads=[b_mf], writes=[b_mk])
    k.mask, k.b_mask = mk, b_mk


def dilated_phase(k, l, env):
    nc, P = k.nc, k.P
    k.dbg_barrier = False
    NB, S, T, NT = env["NB"], env["S"], env["T"], env["NT"]
    cq, ck, cv, gates, ymix, oaug = env["cq"], env["ck"], env["cv"], env["gates"], env["ymix"], env["oaug"]
    psb, ident, b_ident = k.psb, k.ident, k.b_ident
    mask, b_mask = k.mask, k.b_mask
    with ExitStack() as ph:
        def sb(name, shape, dt):
            return ph.enter_context(k.sbt(name, list(shape), dt)), P.buf(name)
        NBMAX = S // 128
        sets = []
        for j in range(2):
            sets.append(dict(
                qr=sb("d_qr%d" % j, [128, NBMAX, 128], BF16), kr=sb("d_kr%d" % j, [128, NBMAX, 128], BF16),
                vr=sb("d_vr%d" % j, [128, NBMAX, 128], BF16),
                qT=sb("d_qT%d" % j, [64, 2, NBMAX, 128], BF16), kT=sb("d_kT%d" % j, [64, 2, NBMAX, 128], BF16),
                va=sb("d_va%d" % j, [128, NBMAX, 2, 65], BF16), oo=sb("d_oo%d" % j, [128, NBMAX, 2, 80], F32)))
            va, b_va = sets[j]["va"]
            P.op("pool", lambda e, va=va: e.memset(va[:], 1.0), writes=[b_va])
            oo_, b_oo_ = sets[j]["oo"]
            P.op("pool", lambda e, oo_=oo_: e.memset(oo_[:], 0.0), writes=[b_oo_])
        pts = [sb("d_pt%d" % j, [128, 2, 2, 128], BF16) for j in range(2)]
        qfs = [sb("d_qf%d" % j, [128, 2, 128], F32) for j in range(2)]
        cnt = 0
        for b in range(NB):
            for g, rate in enumerate(DIL_RATES):
                n = S // rate
                nb = n // 128
                cols = slice(g * 128, (g + 1) * 128)
                for rho in range(rate):
                    s_ = sets[cnt % 2]
                    cnt += 1
                    (qr, b_qr), (kr, b_kr), (vr, b_vr) = s_["qr"], s_["kr"], s_["vr"]
                    (qT, b_qT), (kT, b_kT), (va, b_va), (oo, b_oo) = s_["qT"], s_["kT"], s_["va"], s_["oo"]

                    def rows(d):
                        return d[b * S:(b + 1) * S, :].rearrange("(j i r) c -> r i j c", i=128, r=rate)[rho]
                    for j0 in range(0, nb, 4):
                        j1 = min(nb, j0 + 4)
                        P.dma(lambda e, qr=qr, src=rows(cq)[:, j0:j1, cols], j0=j0, j1=j1: e.dma_start(out=qr[:, j0:j1, :], in_=src), writes=[b_qr])
                        P.dma(lambda e, kr=kr, src=rows(ck)[:, j0:j1, cols], j0=j0, j1=j1: e.dma_start(out=kr[:, j0:j1, :], in_=src), writes=[b_kr])
                        P.dma(lambda e, vr=vr, src=rows(cv)[:, j0:j1, cols], j0=j0, j1=j1: e.dma_start(out=vr[:, j0:j1, :], in_=src), writes=[b_vr])
                    if k.dbg_barrier:
                        P.barrier()
                    P.op("dve", lambda e, va=va, vr=vr: e.tensor_copy(
                        va[:, 0:nb, :, 0:64], vr[:, 0:nb, :].rearrange("p j (h d) -> p j h d", h=2)), reads=[b_vr], writes=[b_va])
                    for j in range(nb):
                        tb, b_tb = psb[j % 2]
                        tbv = tb[:].bitcast(BF16).rearrange("p (c t) -> p c t", t=128)
                        for h in range(2):
                            P.op("pe", lambda e, tbv=tbv, qr=qr, j=j, h=h: e.transpose(out=tbv[0:64, h, :], in_=qr[:, j, h * 64:(h + 1) * 64], identity=ident[:]),
                                 reads=[b_qr, b_ident], writes=[b_tb])
                            P.op("pe", lambda e, tbv=tbv, kr=kr, j=j, h=h: e.transpose(out=tbv[0:64, 2 + h, :], in_=kr[:, j, h * 64:(h + 1) * 64], identity=ident[:]),
                                 reads=[b_kr, b_ident], writes=[b_tb])
                        P.op("dve", lambda e, tbv=tbv, qT=qT, j=j: e.tensor_copy(qT[:, :, j, :], tbv[0:64, 0:2, :]), reads=[b_tb], writes=[b_qT])
                        P.op("dve", lambda e, tbv=tbv, kT=kT, j=j: e.tensor_copy(kT[:, :, j, :], tbv[0:64, 2:4, :]), reads=[b_tb], writes=[b_kT])
                    for j in range(nb):
                        sp_, b_sp = psb[2 + (j % 2)]
                        spv = sp_[:].rearrange("p (h c q) -> p h c q", h=2, c=2)
                        pt, b_pt = pts[j % 2]
                        ncp = 2 if j > 0 else 1
                        for h in range(2):
                            for c in range(ncp):
                                jj = j - c
                                P.op("pe", lambda e, spv=spv, h=h, c=c, jj=jj, j=j, kT=kT, qT=qT: e.matmul(
                                    spv[:, h, c, :], lhsT=kT[:, h, jj, :], rhs=qT[:, h, j, :], start=True, stop=True),
                                    reads=[b_kT, b_qT], writes=[b_sp])
                        P.op("act", lambda e, pt=pt, spv=spv, ncp=ncp: e.activation(out=pt[:, :, 0:ncp, :], in_=spv[:, :, 0:ncp, :], func=AF.Exp, scale=0.125),
                             reads=[b_sp], writes=[b_pt])
                        P.op("dve", lambda e, pt=pt, ncp=ncp: e.tensor_tensor(
                            out=pt[:, :, 0:ncp, :], in0=pt[:, :, 0:ncp, :], in1=mask[:, 0:ncp, :].unsqueeze(1).to_broadcast([128, 2, ncp, 128]), op=ALU.mult),
                            reads=[b_pt, b_mask], writes=[b_pt])
                        for h in range(2):
                            op_, b_op = psb[4 + (j % 2) * 2 + h]
                            for c in range(ncp):
                                jj = j - c
                                P.op("pe", lambda e, op_=op_, h=h, c=c, jj=jj, pt=pt, va=va, ncp=ncp: e.matmul(
                                    op_[:, 0:65], lhsT=pt[:, h, c, :], rhs=va[:, jj, h, :], start=(c == 0), stop=(c == ncp - 1)),
                                    reads=[b_pt, b_va], writes=[b_op], acc=(b_op, c == 0))
                            P.op("dve", lambda e, oo=oo, op_=op_, j=j, h=h: e.tensor_copy(oo[:, j, h, 0:65], op_[:, 0:65]), reads=[b_op], writes=[b_oo])
                    dst = oaug[b * S:(b + 1) * S].rearrange("(j i r) g h d -> r i j g h d", i=128, r=rate)[rho][:, :, g, :, :]
                    P.dma(lambda e, oo=oo, dst=dst: e.dma_start(out=dst, in_=oo[:, 0:nb, :, :]), reads=[b_oo], q="act")
        P.barrier()
        ots = [sb("d_ot%d" % j, [128, 3, 2, 80], F32) for j in range(2)]
        gts = [sb("d_gt%d" % j, [128, 384], BF16) for j in range(2)]
        lls = [sb("d_ll%d" % j, [128, 2], F32) for j in range(2)]
        yys = [sb("d_yy%d" % j, [128, 3, 2, 64], F32) for j in range(2)]
        ybs = [sb("d_yb%d" % j, [128, 384], BF16) for j in range(2)]
        for i in range(NT):
            (ot, b_ot), (gt, b_gt), (ll, b_ll), (yy, b_yy), (yb, b_yb) = ots[i % 2], gts[i % 2], lls[i % 2], yys[i % 2], ybs[i % 2]
            rows = slice(i * 128, (i + 1) * 128)
            P.dma(lambda e, ot=ot, rows=rows: e.dma_start(out=ot[:], in_=oaug[rows]), writes=[b_ot])
            P.dma(lambda e, gt=gt, rows=rows: e.dma_start(out=gt[:], in_=gates[rows, 256:640]), writes=[b_gt])
            P.op("dve", lambda e, ll=ll, ot=ot: e.tensor_tensor(out=ll[:], in0=ot[:, 0, :, 64], in1=ot[:, 1, :, 64], op=ALU.add), reads=[b_ot], writes=[b_ll])
            P.op("dve", lambda e, ll=ll, ot=ot: e.tensor_tensor(out=ll[:], in0=ll[:], in1=ot[:, 2, :, 64], op=ALU.add), reads=[b_ot, b_ll], writes=[b_ll])
            P.op("dve", lambda e, ll=ll: e.reciprocal(out=ll[:], in_=ll[:]), reads=[b_ll], writes=[b_ll])
            for g in range(3):
                P.op("dve", lambda e, yy=yy, ot=ot, ll=ll, g=g: e.tensor_tensor(
                    out=yy[:, g, :, :], in0=ot[:, g, :, 0:64], in1=ll[:].unsqueeze(2).to_broadcast([128, 2, 64]), op=ALU.mult),
                    reads=[b_ot, b_ll], writes=[b_yy])
            P.op("dve", lambda e, yb=yb, yy=yy, gt=gt: e.tensor_tensor(out=yb[:], in0=yy[:].rearrange("p g h d -> p (g h d)"), in1=gt[:], op=ALU.mult),
                 reads=[b_yy, b_gt], writes=[b_yb])
            P.dma(lambda e, yb=yb, rows=rows: e.dma_start(out=ymix[rows, 640:1024], in_=yb[:]), reads=[b_yb], q="act")


def MIXERS(k, l, env):
    nc, P = k.nc, k.P
    ymix, NT = env["ymix"], env["NT"]
    if l == 0:
        make_masks(k, env["st"])
    skip = k.skip
    if "dil" not in skip:
        dilated_phase(k, l, env)
        P.barrier()
    if "dsa" not in skip:
        dsa_phase(k, l, env)
        P.barrier()
    if "rwkv" not in skip:
        rwkv_phase(k, l, env)


def mixer_inputs(inputs, DEPTH):
    f = lambda a: np.ascontiguousarray(np.asarray(a, dtype=np.float32))
    L = DEPTH

    def fm(v, nchunk):
        return np.ascontiguousarray(f(v).reshape(L, nchunk, 128).transpose(0, 2, 1))
    par = np.stack([fm(inputs["rwkv_w0"], 3), fm(inputs["rwkv_a0"], 3), fm(inputs["rwkv_k_k"], 3), fm(inputs["rwkv_k_a"], 3),
                    fm(f(inputs["rwkv_r_k"]).reshape(L, 384), 3)], axis=2)
    return {
        "rw_mu": fm(inputs["tshift_mu"], 13),
        "rw_par": np.ascontiguousarray(par),
        "rw_w_up": f(inputs["rwkv_w_up"]),
        "rw_a_up": f(inputs["rwkv_a_up"]),
        "rw_ln_g": f(inputs["rwkv_ln_g"]).reshape(L, 1, 384),
        "rw_ln_b": f(inputs["rwkv_ln_b"]).reshape(L, 1, 384),
    }


NIT = 14


def dsa_phase(k, l, env):
    nc, P = k.nc, k.P
    NB, S, T, NT, TS = env["NB"], env["S"], env["T"], env["NT"], env["TS"]
    dq, di, iwd, gates, ymix = env["dq"], env["di"], env["iwd"], env["gates"], env["ymix"]
    psb, ident, b_ident, identf, b_identf = k.psb, k.ident, k.b_ident, k.identf, k.b_identf
    TOPK = min(256, S // 4)
    with ExitStack() as ph:
        def sb(name, shape, dt):
            return ph.enter_context(k.sbt(name, list(shape), dt)), P.buf(name)
        qT, b_qT = sb("s_qT", [64, TS, 4, 128], BF16)
        kT, b_kT = sb("s_kT", [64, TS, 128], BF16)
        iqT, b_iqT = sb("s_iqT", [32, 8, TS, 128], BF16)
        ikT, b_ikT = sb("s_ikT", [32, TS, 128], BF16)
        va, b_va = sb("s_va", [128, TS, 65], BF16)
        iw, b_iw = sb("s_iw", [128, TS, 8], F32)
        id4, b_id4 = sb("s_id4", [128, 4, 128], BF16)
        P.op("pool", lambda e: e.memset(va[:], 1.0), writes=[b_va])
        for h in range(4):
            P.op("dve", lambda e, h=h: e.tensor_copy(id4[:, h, :], ident[:]), reads=[b_ident], writes=[b_id4])
        qds = [sb("s_qd%d" % j, [128, 384], BF16) for j in range(2)]
        ids = [sb("s_id%d" % j, [128, 288], BF16) for j in range(2)]
        diags = [sb("s_diag%d" % j, [128, 8, 128], BF16) for j in range(2)]
        rsb = [sb("s_r%d" % j, [128, 512], BF16) for j in range(3)]
        scs = [sb("s_sc%d" % j, [128, S], F32) for j in range(4)]
        junk, b_junk = sb("s_junk", [128, S], F32)
        junk2, b_junk2 = sb("s_junk2", [128, S], BF16)
        ssums = [sb("s_ssum%d" % j, [128, 1], F32) for j in range(4)]
        sgs = [sb("s_sg%d" % j, [128, 1], F32) for j in range(4)]
        cbs = [sb("s_cb%d" % j, [128, 1], F32) for j in range(4)]
        biass = [sb("s_bias%d" % j, [128, S], BF16) for j in range(8)]
        ams = [sb("s_am%d" % j, [128, 1], F32) for j in range(4)]
        thrs = [sb("s_thr%d" % j, [128, 1], F32) for j in range(4)]
        cnts = [sb("s_cnt%d" % j, [128, 1], F32) for j in range(4)]
        tmp1s = [sb("s_tmp1%d" % j, [128, 1], F32) for j in range(4)]
        ptas = [sb("s_pta%d" % j, [128, TS, 512], BF16) for j in range(2)]
        ost = [sb("s_o%d" % j, [128, 65], F32) for j in range(2)]
        rl = [sb("s_rl%d" % j, [128, 1], F32) for j in range(2)]
        yb = [sb("s_yb%d" % j, [128, 256], F32) for j in range(2)]
        gt = [sb("s_gt%d" % j, [128, 256], BF16) for j in range(2)]
        yo = [sb("s_yo%d" % j, [128, 256], BF16) for j in range(2)]
        for b in range(NB):
            for t in range(TS):
                rows = slice(b * S + t * 128, b * S + (t + 1) * 128)
                (qd, b_qd), (idt, b_idt) = qds[t % 2], ids[t % 2]
                P.dma(lambda e, qd=qd, rows=rows: e.dma_start(out=qd[:], in_=dq[rows, :]), writes=[b_qd])
                P.dma(lambda e, idt=idt, rows=rows: e.dma_start(out=idt[:], in_=di[rows, :]), writes=[b_idt])
                P.dma(lambda e, t=t, rows=rows: e.dma_start(out=iw[:, t, :], in_=iwd[rows, :]), writes=[b_iw])
                P.op("dve", lambda e, t=t: e.tensor_scalar(out=iw[:, t, :], in0=iw[:, t, :], scalar1=8.0 ** -0.5, scalar2=None, op0=ALU.mult), reads=[b_iw], writes=[b_iw])
                tx, b_tx = psb[0]
                ty, b_ty = psb[1]
                txv = tx[:].bitcast(BF16).rearrange("p (c t) -> p c t", t=128)
                tyv = ty[:].bitcast(BF16).rearrange("p (c t) -> p c t", t=128)
                for h in range(5):
                    P.op("pe", lambda e, h=h, qd=qd, txv=txv: e.transpose(out=txv[0:64, h, :], in_=qd[:, h * 64:(h + 1) * 64], identity=ident[:]),
                         reads=[b_qd, b_ident], writes=[b_tx])
                P.op("pe", lambda e, idt=idt, txv=txv: e.transpose(out=txv[0:32, 5, :], in_=idt[:, 256:288], identity=ident[:]),
                     reads=[b_idt, b_ident], writes=[b_tx])
                for h in range(8):
                    P.op("pe", lambda e, h=h, idt=idt, tyv=tyv: e.transpose(out=tyv[0:32, h, :], in_=idt[:, h * 32:(h + 1) * 32], identity=ident[:]),
                         reads=[b_idt, b_ident], writes=[b_ty])
                P.op("dve", lambda e, t=t, txv=txv: e.tensor_copy(qT[:, t, :, :], txv[0:64, 0:4, :]), reads=[b_tx], writes=[b_qT])
                P.op("dve", lambda e, t=t, txv=txv: e.tensor_copy(kT[:, t, :], txv[0:64, 4, :]), reads=[b_tx], writes=[b_kT])
                P.op("dve", lambda e, t=t, txv=txv: e.tensor_copy(ikT[:, t, :], txv[0:32, 5, :]), reads=[b_tx], writes=[b_ikT])
                P.op("dve", lambda e, t=t, tyv=tyv: e.tensor_copy(iqT[:, :, t, :], tyv[0:32, 0:8, :]), reads=[b_ty], writes=[b_iqT])
                P.op("dve", lambda e, t=t, qd=qd: e.tensor_copy(va[:, t, 0:64], qd[:, 320:384]), reads=[b_qd], writes=[b_va])
            rcnt = 0

            def tile_vars(j):
                nk = 128 * (j + 1)
                rows = slice(b * S + j * 128, b * S + (j + 1) * 128)
                dcols = slice(nk - 128, nk)
                return nk, rows, dcols

            def bisect_gen(j):
                nk, rows, dcols = tile_vars(j)
                (sc, b_sc), (bias, b_bias), (am, b_am) = scs[j % 4], biass[j % 8], ams[j % 4]
                (thr, b_thr), (cntt, b_cnt), (tmp1, b_tmp1) = thrs[j % 4], cnts[j % 4], tmp1s[j % 4]
                P.op("dve", lambda e: e.tensor_reduce(out=am[:], in_=sc[:, 0:nk], axis=AX.X, op=ALU.max, apply_absolute_value=True),
                     reads=[b_sc], writes=[b_am])
                yield
                P.op("dve", lambda e: e.tensor_scalar(out=am[:], in0=am[:], scalar1=1e-20, scalar2=None, op0=ALU.add), reads=[b_am], writes=[b_am])
                yield
                P.op("dve", lambda e: e.reciprocal(out=am[:], in_=am[:]), reads=[b_am], writes=[b_am])
                P.op("pool", lambda e: e.affine_select(out=sc[:, dcols], in_=sc[:, dcols], pattern=[[-1, 128]], compare_op=ALU.is_ge,
                                                       fill=k.fill(-1e30), base=0, channel_multiplier=1), reads=[b_sc], writes=[b_sc])
                yield
                P.op("dve", lambda e: e.tensor_scalar(out=sc[:, 0:nk], in0=sc[:, 0:nk], scalar1=am[:], scalar2=None, op0=ALU.mult),
                     reads=[b_sc, b_am], writes=[b_sc])
                P.op("dve", lambda e: e.memset(thr[:], 0.0), writes=[b_thr])
                yield
                if j % 2 == 0:
                    for it in range(NIT):
                        step = 2.0 ** -(it + 1)
                        P.op("dve", lambda e: e.tensor_scalar(out=junk[:, 0:nk], in0=sc[:, 0:nk], scalar1=thr[:], scalar2=0.0, op0=ALU.is_ge, op1=ALU.add,
                                                              accum_out=cntt[:]), reads=[b_sc, b_thr], writes=[b_junk, b_cnt])
                        yield
                        P.op("dve", lambda e: e.tensor_scalar(out=tmp1[:], in0=cntt[:], scalar1=float(TOPK), scalar2=2.0 * step, op0=ALU.is_ge, op1=ALU.mult),
                             reads=[b_cnt], writes=[b_tmp1])
                        yield
                        P.op("dve", lambda e: e.scalar_tensor_tensor(out=thr[:], in0=tmp1[:], scalar=-step, in1=thr[:], op0=ALU.add, op1=ALU.add),
                             reads=[b_tmp1, b_thr], writes=[b_thr])
                        yield
                    P.op("dve", lambda e: e.tensor_scalar(out=thr[:], in0=thr[:], scalar1=-(2.0 ** -NIT), scalar2=None, op0=ALU.add), reads=[b_thr], writes=[b_thr])
                else:
                    (ssum, b_ssum), (sg, b_sg), (cb, b_cb) = ssums[j % 4], sgs[j % 4], cbs[j % 4]
                    P.op("pool", lambda e: e.memset(cb[:], float(-(2 * TOPK - nk) + 0.5)), writes=[b_cb])
                    for it in range(NIT):
                        step = 2.0 ** -(it + 1)
                        P.op("act", lambda e: e.activation(out=junk2[:, 0:nk], in_=sc[:, 0:nk], func=AF.Sign, bias=thr[:], accum_out=ssum[:]),
                             reads=[b_sc, b_thr], writes=[b_junk2, b_ssum])
                        yield
                        P.op("act", lambda e: e.activation(out=sg[:], in_=ssum[:], func=AF.Sign, bias=cb[:]), reads=[b_ssum, b_cb], writes=[b_sg])
                        yield
                        P.op("act", lambda e, step=step: e.activation(out=thr[:], in_=sg[:], func=AF.Identity, scale=-step, bias=thr[:]),
                             reads=[b_sg, b_thr], writes=[b_thr])
                        yield
                    P.op("dve", lambda e: e.tensor_scalar(out=thr[:], in0=thr[:], scalar1=-1.0, scalar2=-(2.0 ** -NIT), op0=ALU.mult, op1=ALU.add), reads=[b_thr], writes=[b_thr])
                yield
                P.op("dve", lambda e: e.tensor_scalar(out=bias[:, 0:nk], in0=sc[:, 0:nk], scalar1=thr[:], scalar2=NEG, op0=ALU.is_lt, op1=ALU.mult),
                     reads=[b_sc, b_thr], writes=[b_bias])

            def stage1_gen(grp):
              nonlocal rcnt
              for j in grp:
                nk, rows, dcols = tile_vars(j)
                (diag, b_diag), (sc, b_sc), (bias, b_bias) = diags[j % 2], scs[j % 4], biass[j % 8]
                if nk > TOPK:
                    for h in range(8):
                        P.op("act", lambda e, h=h, j=j: e.activation(out=diag[:, h, :], in_=identf[:], func=AF.Copy, scale=iw[:, j, h:h + 1]),
                             reads=[b_identf, b_iw], writes=[b_diag])
                    for c0 in range(0, nk, 512):
                        n = min(512, nk - c0)
                        acc, b_acc = psb[2]
                        for h in range(8):
                            rp, b_rp = psb[rcnt % 2]
                            rs_, b_rs = rsb[rcnt % 3]
                            rcnt += 1
                            ikv = ikT[:].rearrange("p t k -> p (t k)")
                            P.op("pe", lambda e, rp=rp, h=h, j=j, c0=c0, n=n, ikv=ikv: e.matmul(rp[:, 0:n], lhsT=iqT[:, h, j, :], rhs=ikv[:, c0:c0 + n], start=True, stop=True),
                                 reads=[b_iqT, b_ikT], writes=[b_rp])
                            if True:
                                P.op("act", lambda e, rp=rp, rs_=rs_, n=n: e.activation(out=rs_[:, 0:n], in_=rp[:, 0:n], func=AF.Relu, scale=32.0 ** -0.5),
                                     reads=[b_rp], writes=[b_rs])
                            else:
                                P.op("dve", lambda e, rp=rp, rs_=rs_, n=n: e.tensor_scalar(out=rs_[:, 0:n], in0=rp[:, 0:n], scalar1=32.0 ** -0.5, scalar2=0.0,
                                                                                          op0=ALU.mult, op1=ALU.max), reads=[b_rp], writes=[b_rs])
                            P.op("pe", lambda e, acc=acc, h=h, rs_=rs_, n=n: e.matmul(acc[:, 0:n], lhsT=diag[:, h, :], rhs=rs_[:, 0:n], start=(h == 0), stop=(h == 7)),
                                 reads=[b_diag, b_rs], writes=[b_acc], acc=(b_acc, h == 0))
                            if h < 7:
                                yield
                        P.op("act", lambda e, acc=acc, c0=c0, n=n: e.activation(out=sc[:, c0:c0 + n], in_=acc[:, 0:n], func=AF.Copy), reads=[b_acc], writes=[b_sc])
                        yield
                else:
                    P.op("pool", lambda e, nk=nk: e.memset(bias[:, 0:nk], 0.0), writes=[b_bias])
                    P.op("pool", lambda e, dcols=dcols: e.affine_select(out=bias[:, dcols], in_=bias[:, dcols], pattern=[[-1, 128]], compare_op=ALU.is_ge,
                                                                        fill=k.fill(NEG), base=0, channel_multiplier=1), reads=[b_bias], writes=[b_bias])
            def stage2_gen(grp):
              gens = [bisect_gen(j) for j in grp if 128 * (j + 1) > TOPK]
              while gens:
                  for g in list(gens):
                      try:
                          next(g)
                      except StopIteration:
                          gens.remove(g)
                  yield
            def stage3_gen(grp):
              for j in grp:
                nk, rows, dcols = tile_vars(j)
                (bias, b_bias), (pta, b_pta) = biass[j % 8], ptas[j % 2]
                for kb in range(j + 1):
                    sp_, b_sp = psb[3 + (kb % 2)]
                    P.op("pe", lambda e, sp_=sp_, kb=kb, j=j: e.matmul(sp_[:], lhsT=kT[:, kb, :], rhs=qT[:, j, :, :].rearrange("p h q -> p (h q)"), start=True, stop=False),
                         reads=[b_kT, b_qT], writes=[b_sp], acc=(b_sp, True))
                    P.op("pe", lambda e, sp_=sp_, kb=kb: e.matmul(sp_[:], lhsT=bias[:, kb * 128:(kb + 1) * 128], rhs=id4[:].rearrange("p h q -> p (h q)"), start=False, stop=True),
                         reads=[b_bias, b_id4], writes=[b_sp], acc=(b_sp, False))
                    P.op("act", lambda e, sp_=sp_, kb=kb: e.activation(out=pta[:, kb, :], in_=sp_[:], func=AF.Exp, scale=0.125), reads=[b_sp], writes=[b_pta])
                    yield
                (ybt, b_ybt), (gtt, b_gtt), (yot, b_yot) = yb[j % 2], gt[j % 2], yo[j % 2]
                P.dma(lambda e, gtt=gtt, rows=rows: e.dma_start(out=gtt[:], in_=gates[rows, 0:256]), writes=[b_gtt])
                for h in range(4):
                    op_, b_op = psb[5 + (h % 2)]
                    (ot, b_ot), (rlt, b_rlt) = ost[h % 2], rl[h % 2]
                    for kb in range(j + 1):
                        P.op("pe", lambda e, op_=op_, kb=kb, h=h, j=j: e.matmul(op_[:, 0:65], lhsT=pta[:, kb, h * 128:(h + 1) * 128], rhs=va[:, kb, :], start=(kb == 0), stop=(kb == j)),
                             reads=[b_pta, b_va], writes=[b_op], acc=(b_op, kb == 0))
                    P.op("dve", lambda e, op_=op_, ot=ot: e.tensor_copy(ot[:], op_[:, 0:65]), reads=[b_op], writes=[b_ot])
                    P.op("dve", lambda e, ot=ot, rlt=rlt: e.reciprocal(out=rlt[:], in_=ot[:, 64:65]), reads=[b_ot], writes=[b_rlt])
                    P.op("dve", lambda e, ot=ot, rlt=rlt, ybt=ybt, h=h: e.tensor_scalar(out=ybt[:, h * 64:(h + 1) * 64], in0=ot[:, 0:64], scalar1=rlt[:], scalar2=None, op0=ALU.mult),
                         reads=[b_ot, b_rlt], writes=[b_ybt])
                    yield
                P.op("pool", lambda e, ybt=ybt, gtt=gtt, yot=yot: e.tensor_tensor(out=yot[:], in0=ybt[:], in1=gtt[:], op=ALU.mult), reads=[b_ybt, b_gtt], writes=[b_yot])
                P.dma(lambda e, yot=yot, rows=rows: e.dma_start(out=ymix[rows, 384:640], in_=yot[:]), reads=[b_yot], q="act")

            def chain(*gs):
                for g_ in gs:
                    yield from g_

            def lockstep(gl):
                gl = [g_ for g_ in gl if g_ is not None]
                while gl:
                    for g_ in list(gl):
                        try:
                            next(g_)
                        except StopIteration:
                            gl.remove(g_)
            groups = [list(range(j0, min(TS, j0 + 4))) for j0 in range(0, TS, 4)]
            lockstep([chain(stage1_gen(groups[0]), stage2_gen(groups[0]))])
            for gi in range(len(groups)):
                nxt = chain(stage1_gen(groups[gi + 1]), stage2_gen(groups[gi + 1])) if gi + 1 < len(groups) else None
                lockstep([stage3_gen(groups[gi]), nxt])


def rwkv_phase(k, l, env):
    nc, P = k.nc, k.P
    NB, S, T, NT, TS = env["NB"], env["S"], env["T"], env["NT"], env["TS"]
    zr, ymix = env["zr"], env["ymix"]
    rin = env["rw_in"]
    psb, ident, b_ident, identf, b_identf = k.psb, k.ident, k.b_ident, k.identf, k.b_identf
    CH = 512
    NCH = S // CH
    with ExitStack() as ph:
        def sb(name, shape, dt):
            return ph.enter_context(k.sbt(name, list(shape), dt)), P.buf(name)
        mu, b_mu = sb("r_mu", [128, 13], F32)
        omu, b_omu = sb("r_omu", [128, 13], F32)
        par, b_par = sb("r_par", [128, 5, 3], F32)
        omka, b_omka = sb("r_omka", [128, 3], F32)
        wup, b_wup = sb("r_wup", [128, 384], BF16)
        aup, b_aup = sb("r_aup", [128, 384], BF16)
        lng, b_lng = sb("r_lng", [128, 384], F32)
        lnb, b_lnb = sb("r_lnb", [128, 384], F32)
        P.dma(lambda e: e.dma_start(out=mu[:], in_=rin["mu"][l]), writes=[b_mu])
        P.dma(lambda e: e.dma_start(out=par[:], in_=rin["par"][l]), writes=[b_par])
        P.dma(lambda e: e.dma_start(out=wup[0:64, :], in_=rin["w_up"][l]), writes=[b_wup], q="pool")
        P.dma(lambda e: e.dma_start(out=aup[64:128, :], in_=rin["a_up"][l]), writes=[b_aup], q="pool")
        P.dma(lambda e: e.dma_start(out=lng[:], in_=rin["ln_g"][l].to_broadcast([128, 384])), writes=[b_lng])
        P.dma(lambda e: e.dma_start(out=lnb[:], in_=rin["ln_b"][l].to_broadcast([128, 384])), writes=[b_lnb])
        P.op("dve", lambda e: e.tensor_scalar(out=omu[:], in0=mu[:], scalar1=-1.0, scalar2=1.0, op0=ALU.mult, op1=ALU.add), reads=[b_mu], writes=[b_omu])
        P.op("dve", lambda e: e.tensor_scalar(out=omka[:], in0=par[:, 3, :], scalar1=-1.0, scalar2=1.0, op0=ALU.mult, op1=ALU.add), reads=[b_par], writes=[b_omka])
        bones, b_bones = sb("r_bones", [128, 128], BF16)
        ind2, b_ind2 = sb("r_ind2", [128, 2], BF16)
        rmask, b_rmask = sb("r_rmask", [128, CH], F32)
        cmk, b_cmk = sb("r_cmk", [128, 4, 128], F32)
        rowm, b_rowm = sb("r_rowm", [128, 2], F32)
        P.op("pool", lambda e: e.memset(bones[:], 0.0), writes=[b_bones])
        P.op("pool", lambda e: e.memset(bones[0:64, 0:64], 1.0), writes=[b_bones])
        P.op("pool", lambda e: e.memset(bones[64:128, 64:128], 1.0), writes=[b_bones])
        P.op("pool", lambda e: e.memset(ind2[:], 0.0), writes=[b_ind2])
        P.op("pool", lambda e: e.memset(ind2[0:64, 0:1], 1.0), writes=[b_ind2])
        P.op("pool", lambda e: e.memset(ind2[64:128, 1:2], 1.0), writes=[b_ind2])
        P.op("pool", lambda e: e.memset(rmask[:], 1.0), writes=[b_rmask])
        P.op("pool", lambda e: e.memset(rmask[:].rearrange("p (c t) -> p c t", t=64)[:, :, 0:1], 0.0), writes=[b_rmask])
        P.op("pool", lambda e: e.memset(rowm[:], 0.0), writes=[b_rowm])
        P.op("pool", lambda e: e.memset(rowm[0:64, 0:1], 1.0), writes=[b_rowm])
        P.op("pool", lambda e: e.memset(rowm[64:128, 1:2], 1.0), writes=[b_rowm])
        P.op("pool", lambda e: e.memset(cmk[:], 0.0), writes=[b_cmk])
        for cblk in range(2):
            ps_ = slice(cblk * 64, cblk * 64 + 64)
            for kind in range(3):
                P.op("pool", lambda e, ps_=ps_, kind=kind: e.memset(cmk[ps_, kind, ps_], 1.0), writes=[b_cmk])
        P.op("pool", lambda e: e.affine_select(out=cmk[:, 0, :], in_=cmk[:, 0, :], pattern=[[1, 128]], compare_op=ALU.is_gt, fill=k.fill(0.0), base=0, channel_multiplier=-1), reads=[b_cmk], writes=[b_cmk])
        P.op("pool", lambda e: e.affine_select(out=cmk[:, 1, :], in_=cmk[:, 1, :], pattern=[[1, 128]], compare_op=ALU.is_ge, fill=k.fill(0.0), base=0, channel_multiplier=-1), reads=[b_cmk], writes=[b_cmk])
        P.op("pool", lambda e: e.affine_select(out=cmk[:, 2, :], in_=cmk[:, 2, :], pattern=[[-1, 128]], compare_op=ALU.is_gt, fill=k.fill(0.0), base=0, channel_multiplier=1), reads=[b_cmk], writes=[b_cmk])
        zts = [sb("r_zt%d" % j, [128, RWIN], F32) for j in range(2)]
        zT, b_zT = sb("r_zT", [128, 13, CH + 1], F32)
        zs, b_zs = sb("r_zs", [128, 13, CH], F32)
        W = {}
        for nm in ("lw", "a", "kk", "kp", "be", "cw", "ep", "en", "epv", "ee", "t1", "t2"):
            W[nm] = sb("r_w_" + nm, [128, CH], F32)
        twd, b_twd = sb("r_twd", [128, CH], BF16)
        adb, b_adb = sb("r_adb", [128, CH], BF16)
        kk2, b_kk2 = sb("r_kk2", [128, CH], BF16)
        Oset = []
        for par_ in range(2):
            O_ = {}
            for nm in ("At", "Bt", "Kt", "Rt", "Bh", "Kh", "vb", "sg", "rk"):
                O_[nm] = sb("r_o%d_%s" % (par_, nm), [128, 3, CH], BF16)
            Oset.append(O_)
        wcs = [sb("r_wc%d" % j, [128, 3, CH // 64], F32) for j in range(2)]
        PAIR = []
        for mm in range(3):
            pr = dict(tokT=sb("r_tokT%d" % mm, [128, 4, 128], BF16), khc=sb("r_khc%d" % mm, [128, 2, 128], BF16), bhc=sb("r_bhc%d" % mm, [128, 2, 128], BF16),
                      atc=sb("r_atc%d" % mm, [128, 2, 128], BF16), rtc=sb("r_rtc%d" % mm, [128, 2, 128], BF16), upc=sb("r_upc%d" % mm, [128, 2, 128], BF16),
                      Hs=sb("r_Hs%d" % mm, [128, 64], F32), Hb=sb("r_Hb%d" % mm, [128, 3, 64], BF16))
            for nm in ("atc", "rtc"):
                t_, b_ = pr[nm]
                P.op("pool", lambda e, t_=t_: e.memset(t_[:], 0.0), writes=[b_])
            PAIR.append(pr)
        HEAD = []
        for hi in range(6):
            HEAD.append(dict(M4=sb("r_M4_%d" % hi, [128, 4, 128], BF16), Mrbc=sb("r_Mrbc%d" % hi, [128, 2, 128], BF16),
                             PP=[sb("r_PP%d_%d" % (hi, j), [128, 2, 128], BF16) for j in range(2)], X=sb("r_X%d" % hi, [128, 128], BF16),
                             P1=sb("r_P1_%d" % hi, [128, 64], BF16), Vp=sb("r_Vp%d" % hi, [128, 64], F32), G=sb("r_G%d" % hi, [128, 64], BF16)))
        TAIL = []
        for j in range(2):
            TAIL.append(dict(ytile=sb("r_ytile%d" % j, [128, 384], F32), ysq=sb("r_ysq%d" % j, [128, 64], F32), st1=sb("r_st1%d" % j, [128, 6], F32),
                             st2=sb("r_st2%d" % j, [128, 6], F32), bsc=sb("r_bsc%d" % j, [128, 6], F32), yout=sb("r_yout%d" % j, [128, 384], BF16),
                             vtok=sb("r_vtok%d" % j, [128, 3, 128], BF16), sgtok=sb("r_sgtok%d" % j, [128, 3, 128], BF16),
                             mean=sb("r_mean%d" % j, [128, 6], F32), rstd=sb("r_rstd%d" % j, [128, 6], F32), btmp=sb("r_btmp%d" % j, [128, 384], F32)))

        def w_(nm):
            return W[nm]

        def prep_gen(b, tc):
            r0 = b * S + tc * CH
            Oc = Oset[tc % 2]
            wcc, b_wcc = wcs[tc % 2]
            if tc == 0:
                P.op("pool", lambda e: e.memset(zT[:, :, 0:1], 0.0), writes=[b_zT])
            else:
                P.op("dve", lambda e: e.tensor_copy(zT[:, :, 0:1], zT[:, :, CH:CH + 1]), reads=[b_zT], writes=[b_zT])
            def load(tt):
                P.dma(lambda e: e.dma_start(out=zts[tt % 2][0][:, :], in_=zr[r0 + tt * 128:r0 + (tt + 1) * 128, :]), writes=[zts[tt % 2][1]])
            load(0)
            cnt = 0
            for tt in range(4):
                if tt + 1 < 4:
                    load(tt + 1)
                for c0 in range(0, 13, 2):
                    ncc = min(2, 13 - c0)
                    tb, b_tb = psb[6 + cnt % 2]
                    cnt += 1
                    tbv = tb[:, 256:512].rearrange("p (c t) -> p c t", t=128)
                    for c in range(ncc):
                        P.op("pe", lambda e, tbv=tbv, c=c, c0=c0, tt=tt: e.transpose(out=tbv[:, c, :], in_=zts[tt % 2][0][:, (c0 + c) * 128:(c0 + c + 1) * 128], identity=identf[:]),
                             reads=[zts[tt % 2][1], b_identf], writes=[b_tb])
                    eng = "act" if cnt % 2 else "dve"
                    if eng == "act":
                        P.op("act", lambda e, tbv=tbv, c0=c0, ncc=ncc, tt=tt: e.activation(out=zT[:, c0:c0 + ncc, 1 + tt * 128:1 + (tt + 1) * 128], in_=tbv[:, 0:ncc, :], func=AF.Copy),
                             reads=[b_tb], writes=[b_zT])
                    else:
                        P.op("dve", lambda e, tbv=tbv, c0=c0, ncc=ncc, tt=tt: e.tensor_copy(zT[:, c0:c0 + ncc, 1 + tt * 128:1 + (tt + 1) * 128], tbv[:, 0:ncc, :]),
                             reads=[b_tb], writes=[b_zT])
                    yield
            for c in range(13):
                if c % 2:
                    P.op("dve", lambda e, c=c: e.tensor_scalar(out=zs[:, c, :], in0=zT[:, c, 1:CH + 1], scalar1=omu[:, c:c + 1], scalar2=None, op0=ALU.mult),
                         reads=[b_zT, b_omu], writes=[b_zs])
                else:
                    P.op("act", lambda e, c=c: e.activation(out=zs[:, c, :], in_=zT[:, c, 1:CH + 1], func=AF.Copy, scale=omu[:, c:c + 1]),
                         reads=[b_zT, b_omu], writes=[b_zs])
                P.op("dve", lambda e, c=c: e.scalar_tensor_tensor(out=zs[:, c, :], in0=zT[:, c, 0:CH], scalar=mu[:, c:c + 1], in1=zs[:, c, :], op0=ALU.mult, op1=ALU.add),
                     reads=[b_zT, b_mu, b_zs], writes=[b_zs])
                yield
            P.op("act", lambda e: e.activation(out=twd[0:64, :], in_=zs[0:64, 12, :], func=AF.Tanh), reads=[b_zs], writes=[b_twd])
            P.op("dve", lambda e: e.tensor_copy(adb[64:128, :], zs[64:128, 12, :]), reads=[b_zs], writes=[b_adb])
            for m in range(3):
                fs = slice(m * 128, (m + 1) * 128)
                (lw, b_lw), (a_, b_a), (kk, b_kk), (kp, b_kp), (be, b_be), (cw, b_cw) = w_("lw"), w_("a"), w_("kk"), w_("kp"), w_("be"), w_("cw")
                (ep, b_ep), (en, b_en), (epv, b_epv), (ee, b_ee), (t1, b_t1), (t2, b_t2) = w_("ep"), w_("en"), w_("epv"), w_("ee"), w_("t1"), w_("t2")
                r_v, k_v, v_v, g_v = zs[:, m, :], zs[:, 3 + m, :], zs[:, 6 + m, :], zs[:, 9 + m, :]
                for hfx in range(2):
                    hs = slice(hfx * 256, (hfx + 1) * 256)
                    pw, b_pw = psb[6]
                    pa, b_pa = psb[7]
                    P.op("pe", lambda e, fs=fs, pw=pw, hs=hs: e.matmul(pw[:, 256:512], lhsT=wup[0:64, fs], rhs=twd[0:64, hs], start=True, stop=True), reads=[b_wup, b_twd], writes=[b_pw], rows="lo")
                    P.op("pe", lambda e, fs=fs, pa=pa, hs=hs: e.matmul(pa[:, 256:512], lhsT=aup[64:128, fs], rhs=adb[64:128, hs], start=True, stop=True), reads=[b_aup, b_adb], writes=[b_pa], rows="hi")
                    P.op("act", lambda e, lw=lw, pw=pw, m=m, hs=hs: e.activation(out=lw[:, hs], in_=pw[:, 256:512], func=AF.Sigmoid, bias=par[:, 0, m:m + 1]), reads=[b_pw, b_par], writes=[b_lw])
                    P.op("act", lambda e, a_=a_, pa=pa, m=m, hs=hs: e.activation(out=a_[:, hs], in_=pa[:, 256:512], func=AF.Sigmoid, bias=par[:, 1, m:m + 1]), reads=[b_pa, b_par], writes=[b_a])
                    yield
                P.op("act", lambda e, lw=lw: e.activation(out=lw[:], in_=lw[:], func=AF.Copy, scale=-DECAY_SCALE), reads=[b_lw], writes=[b_lw])
                yield
                P.op("dve", lambda e, kk=kk, k_v=k_v, m=m: e.tensor_scalar(out=kk[:], in0=k_v, scalar1=par[:, 2, m:m + 1], scalar2=None, op0=ALU.mult), reads=[b_zs, b_par], writes=[b_kk])
                P.op("dve", lambda e, kk=kk: e.tensor_tensor(out=kk2[:], in0=kk[:], in1=kk[:], op=ALU.mult), reads=[b_kk], writes=[b_kk2])
                yield
                for hfx in range(2):
                    hs = slice(hfx * 256, (hfx + 1) * 256)
                    pn, b_pn = psb[6 + hfx]
                    P.op("pe", lambda e, pn=pn, hs=hs: e.matmul(pn[:, 256:512], lhsT=bones[:], rhs=kk2[:, hs], start=True, stop=True), reads=[b_bones, b_kk2], writes=[b_pn])
                    P.op("act", lambda e, t1=t1, pn=pn, hs=hs: e.activation(out=t1[:, hs], in_=pn[:, 256:512], func=AF.Sqrt), reads=[b_pn], writes=[b_t1])
                yield
                P.op("dve", lambda e, t1=t1: e.tensor_scalar(out=t1[:], in0=t1[:], scalar1=1e-12, scalar2=None, op0=ALU.max), reads=[b_t1], writes=[b_t1])
                P.op("dve", lambda e, t1=t1: e.reciprocal(out=t1[:], in_=t1[:]), reads=[b_t1], writes=[b_t1])
                P.op("dve", lambda e, kk=kk, t1=t1: e.tensor_tensor(out=kk[:], in0=kk[:], in1=t1[:], op=ALU.mult), reads=[b_kk, b_t1], writes=[b_kk])
                yield
                P.op("dve", lambda e, kp=kp, a_=a_, m=m: e.tensor_scalar(out=kp[:], in0=a_[:], scalar1=par[:, 3, m:m + 1], scalar2=omka[:, m:m + 1], op0=ALU.mult, op1=ALU.add),
                     reads=[b_a, b_par, b_omka], writes=[b_kp])
                P.op("dve", lambda e, kp=kp, k_v=k_v: e.tensor_tensor(out=kp[:], in0=kp[:], in1=k_v, op=ALU.mult), reads=[b_kp, b_zs], writes=[b_kp])
                P.op("dve", lambda e, be=be, kk=kk, a_=a_: e.tensor_tensor(out=be[:], in0=kk[:], in1=a_[:], op=ALU.mult), reads=[b_kk, b_a], writes=[b_be])
                yield
                P.op("dve", lambda e, cw=cw, lw=lw: e.tensor_tensor_scan(out=cw[:], data0=rmask[:], data1=lw[:], initial=0.0, op0=ALU.mult, op1=ALU.add),
                     reads=[b_rmask, b_lw], writes=[b_cw])
                yield
                P.op("act", lambda e, ep=ep, cw=cw: e.activation(out=ep[:], in_=cw[:], func=AF.Exp), reads=[b_cw], writes=[b_ep])
                P.op("act", lambda e, en=en, cw=cw: e.activation(out=en[:], in_=cw[:], func=AF.Exp, scale=-1.0), reads=[b_cw], writes=[b_en])
                P.op("dve", lambda e, t2=t2, cw=cw, lw=lw: e.tensor_tensor(out=t2[:], in0=cw[:], in1=lw[:], op=ALU.subtract), reads=[b_cw, b_lw], writes=[b_t2])
                P.op("act", lambda e, epv=epv, t2=t2: e.activation(out=epv[:], in_=t2[:], func=AF.Exp), reads=[b_t2], writes=[b_epv])
                yield
                cw3 = cw[:].rearrange("p (c t) -> p c t", t=64)
                P.op("dve", lambda e, t2=t2, cw3=cw3: e.tensor_tensor(out=t2[:].rearrange("p (c t) -> p c t", t=64), in0=cw3[:, :, 63:64].to_broadcast([128, CH // 64, 64]), in1=cw3, op=ALU.subtract),
                     reads=[b_cw], writes=[b_t2])
                P.op("act", lambda e, ee=ee, t2=t2: e.activation(out=ee[:], in_=t2[:], func=AF.Exp), reads=[b_t2], writes=[b_ee])
                yield
                P.op("dve", lambda e, ep=ep, m=m: e.tensor_copy(wcc[:, m, :], ep[:].rearrange("p (c t) -> p c t", t=64)[:, :, 63]), reads=[b_ep], writes=[b_wcc])
                P.op("dve", lambda e, kk=kk, epv=epv, m=m: e.scalar_tensor_tensor(out=Oc["At"][0][:, m, :], in0=kk[:], scalar=-1.0, in1=epv[:], op0=ALU.mult, op1=ALU.mult),
                     reads=[b_kk, b_epv], writes=[Oc["At"][1]])
                P.op("dve", lambda e, be=be, en=en, m=m: e.tensor_tensor(out=Oc["Bt"][0][:, m, :], in0=be[:], in1=en[:], op=ALU.mult), reads=[b_be, b_en], writes=[Oc["Bt"][1]])
                P.op("dve", lambda e, kp=kp, en=en, m=m: e.tensor_tensor(out=Oc["Kt"][0][:, m, :], in0=kp[:], in1=en[:], op=ALU.mult), reads=[b_kp, b_en], writes=[Oc["Kt"][1]])
                yield
                P.op("dve", lambda e, r_v=r_v, ep=ep, m=m: e.tensor_tensor(out=Oc["Rt"][0][:, m, :], in0=r_v, in1=ep[:], op=ALU.mult), reads=[b_zs, b_ep], writes=[Oc["Rt"][1]])
                P.op("dve", lambda e, be=be, ee=ee, m=m: e.tensor_tensor(out=Oc["Bh"][0][:, m, :], in0=be[:], in1=ee[:], op=ALU.mult), reads=[b_be, b_ee], writes=[Oc["Bh"][1]])
                yield
                P.op("dve", lambda e, kp=kp, ee=ee, m=m: e.tensor_tensor(out=Oc["Kh"][0][:, m, :], in0=kp[:], in1=ee[:], op=ALU.mult), reads=[b_kp, b_ee], writes=[Oc["Kh"][1]])
                P.op("act", lambda e, v_v=v_v, m=m: e.activation(out=Oc["vb"][0][:, m, :], in_=v_v, func=AF.Copy), reads=[b_zs], writes=[Oc["vb"][1]])
                yield
                P.op("act", lambda e, g_v=g_v, m=m: e.activation(out=Oc["sg"][0][:, m, :], in_=g_v, func=AF.Silu), reads=[b_zs], writes=[Oc["sg"][1]])
                P.op("dve", lambda e, t1=t1, r_v=r_v, kp=kp: e.tensor_tensor(out=t1[:], in0=r_v, in1=kp[:], op=ALU.mult), reads=[b_zs, b_kp], writes=[b_t1])
                P.op("dve", lambda e, t1=t1, m=m: e.tensor_scalar(out=Oc["rk"][0][:, m, :], in0=t1[:], scalar1=par[:, 4, m:m + 1], scalar2=None, op0=ALU.mult),
                     reads=[b_t1, b_par], writes=[Oc["rk"][1]])

        for b in range(NB):
            for pr in PAIR:
                P.op("pool", lambda e, t_=pr["Hs"][0]: e.memset(t_[:], 0.0), writes=[pr["Hs"][1]])
                P.op("pool", lambda e, t_=pr["Hb"][0]: e.memset(t_[:], 0.0), writes=[pr["Hb"][1]])
            spi = 0
            for _ in prep_gen(b, 0):
                pass
            for tc in range(NCH):
                r0 = b * S + tc * CH
                O = Oset[tc % 2]
                wc, b_wc = wcs[tc % 2]
                bg = [prep_gen(b, tc + 1)] if tc + 1 < NCH else []
                Lc = locals()
                fns = {}
                for sp in range(CH // 128):
                    fns[sp] = RWKV_SPAN_FNS(k, Lc, sp, spi + sp)
                def lockstep(gl):
                    while gl:
                        for g in list(gl) + list(bg):
                            try:
                                next(g)
                            except StopIteration:
                                (gl if g in gl else bg).remove(g)
                for sp in range(CH // 128):
                    gl = [fns[sp][0](0, 0), fns[sp][0](1, 1)]
                    if sp > 0:
                        gl.append(fns[sp - 1][1](2, 0))
                    lockstep(gl)
                    if sp > 0:
                        fns[sp - 1][2]()
                    lockstep([fns[sp][0](2, 0), fns[sp][1](0, 0), fns[sp][1](1, 1)])
                lockstep([fns[CH // 128 - 1][1](2, 0)])
                fns[CH // 128 - 1][2]()
                for g in bg:
                    for _ in g:
                        pass
                spi += CH // 128


def RWKV_SPAN_FNS(k, L, sp, spi):
    nc, P = k.nc, k.P
    psb, ident, b_ident = k.psb, k.ident, k.b_ident
    O, wc, b_wc = L["O"], L["wc"], L["b_wc"]
    cmk, b_cmk, rowm, b_rowm, ind2, b_ind2 = L["cmk"], L["b_cmk"], L["rowm"], L["b_rowm"], L["ind2"], L["b_ind2"]
    lng, b_lng, lnb, b_lnb = L["lng"], L["b_lng"], L["lnb"], L["b_lnb"]
    PAIR, HEAD, TAIL = L["PAIR"], L["HEAD"], L["TAIL"]
    ymix = L["ymix"]
    r0 = L["r0"]
    ts = slice(sp * 128, (sp + 1) * 128)
    rows = slice(r0 + sp * 128, r0 + (sp + 1) * 128)
    T_ = TAIL[spi % 2]
    (ytile, b_ytile), (ysq, b_ysq), (st1, b_st1), (st2, b_st2), (bsc, b_bsc) = T_["ytile"], T_["ysq"], T_["st1"], T_["st2"], T_["bsc"]
    (yout, b_yout), (vtok, b_vtok), (sgtok, b_sgtok), (mean, b_mean), (rstd, b_rstd), (btmp, b_btmp) = T_["yout"], T_["vtok"], T_["sgtok"], T_["mean"], T_["rstd"], T_["btmp"]
    s_in, s_mid, s_out = (2 * spi) % 3, (2 * spi + 1) % 3, (2 * spi + 2) % 3

    def fm(nm, m, hb=slice(0, 128)):
        t_, b_ = O[nm]
        return t_[hb, m, ts], b_

    def head_pre(m, hh, slot):
        pr = PAIR[m]
        (tokT, b_tokT) = pr["tokT"]
        hb = slice(hh * 64, hh * 64 + 64)
        rg = "hi" if hh else "lo"
        d_ = HEAD[2 * m + hh]
        At, b_At = fm("At", m, hb)
        Bt, b_Bt = fm("Bt", m, hb)
        Kt, b_Kt = fm("Kt", m, hb)
        Rt, b_Rt = fm("Rt", m, hb)
        bk, b_bk = psb[1 + 2 * slot + hh]
        bkv = bk[:].rearrange("p (c t) -> p c t", t=128)
        for i_, (lh, rh, bl, br) in enumerate(((Bt, At, b_Bt, b_At), (Kt, At, b_Kt, b_At), (Bt, Rt, b_Bt, b_Rt), (At, Bt, b_At, b_Bt))):
            P.op("pe", lambda e, i_=i_, lh=lh, rh=rh: e.matmul(bkv[:, i_, :], lhsT=lh, rhs=rh, start=True, stop=True), reads=[bl, br], writes=[b_bk], rows=rg)
        yield
        (M4t, b_M4t) = d_["M4"]
        (pp0, b_pp0) = d_["PP"][0]
        P.op("dve", lambda e: e.tensor_tensor(out=M4t[:, 0:2, :], in0=bkv[:, 0:2, :], in1=cmk[:, 0:1, :].to_broadcast([128, 2, 128]), op=ALU.mult),
             reads=[b_bk, b_cmk], writes=[b_M4t])
        P.op("dve", lambda e: e.tensor_tensor(out=pp0[:, 1, :], in0=bkv[:, 3, :], in1=cmk[:, 2, :], op=ALU.mult), reads=[b_bk, b_cmk], writes=[b_pp0])
        P.op("dve", lambda e: e.tensor_tensor(out=M4t[:, 2, :], in0=bkv[:, 2, :], in1=cmk[:, 1, :], op=ALU.mult), reads=[b_bk, b_cmk], writes=[b_M4t])
        N0, Mak, Mrb, Mrk = M4t[:, 0, :], M4t[:, 1, :], M4t[:, 2, :], M4t[:, 3, :]
        yield
        P.op("dve", lambda e: e.tensor_copy(pp0[:, 0, :], N0), reads=[b_M4t], writes=[b_pp0])
        (X, b_X) = d_["X"]
        P.op("dve", lambda e: e.tensor_tensor(out=X[:], in0=N0, in1=ident[:], op=ALU.add), reads=[b_M4t, b_ident], writes=[b_X])
        P.op("pe", lambda e: e.matmul(bk[:, 384:512], lhsT=Kt, rhs=Rt, start=True, stop=True), reads=[b_Kt, b_Rt], writes=[b_bk], rows=rg)
        (mrbc, b_mrbc) = d_["Mrbc"]
        for c in range(2):
            P.op("dve", lambda e, c=c: e.tensor_scalar(out=mrbc[:, c, :], in0=Mrb, scalar1=rowm[:, c:c + 1], scalar2=None, op0=ALU.mult),
                 reads=[b_M4t, b_rowm], writes=[b_mrbc])
        yield
        P.op("dve", lambda e: e.tensor_tensor(out=M4t[:, 3, :], in0=bk[:, 384:512], in1=cmk[:, 1, :], op=ALU.mult), reads=[b_bk, b_cmk], writes=[b_M4t])
        for i_ in range(5):
            (pc, b_pc) = d_["PP"][i_ % 2]
            (pn_, b_pn) = d_["PP"][(i_ + 1) % 2]
            p3v = bk[:, 0:256].rearrange("p (c t) -> p c t", t=128)
            P.op("pe", lambda e, pc=pc: e.matmul(p3v[:, 0, :], lhsT=pc[:, 1, :], rhs=pc[:, 0, :], start=True, stop=True), reads=[b_pc], writes=[b_bk])
            P.op("pe", lambda e, pc=pc: e.matmul(p3v[:, 1, :], lhsT=pc[:, 0, :], rhs=pc[:, 1, :], start=True, stop=True), reads=[b_pc], writes=[b_bk])
            yield
            P.op("act", lambda e, pn_=pn_: e.activation(out=pn_[:], in_=p3v, func=AF.Copy), reads=[b_bk], writes=[b_pn])
            yield
            P.op("pe", lambda e, pn_=pn_: e.matmul(bk[:, 256:384], lhsT=pn_[:, 1, :], rhs=X[:], start=True, stop=True), reads=[b_pn, b_X], writes=[b_bk])
            yield
            P.op("dve", lambda e: e.tensor_tensor(out=X[:], in0=bk[:, 256:384], in1=X[:], op=ALU.add), reads=[b_bk, b_X], writes=[b_X])
            yield
        (P1, b_P1), (Vp, b_Vp) = d_["P1"], d_["Vp"]
        P.op("pe", lambda e: e.matmul(bk[:, 0:64], lhsT=Mak, rhs=tokT[:, 0, hb], start=True, stop=True), reads=[b_M4t, b_tokT], writes=[b_bk])
        yield
        P.op("act", lambda e: e.activation(out=P1[:], in_=bk[:, 0:64], func=AF.Copy), reads=[b_bk], writes=[b_P1])
        yield
        P.op("pe", lambda e: e.matmul(bk[:, 64:128], lhsT=X[:], rhs=P1[:], start=True, stop=True), reads=[b_X, b_P1], writes=[b_bk])
        yield
        P.op("act", lambda e: e.activation(out=Vp[:], in_=bk[:, 64:128], func=AF.Copy), reads=[b_bk], writes=[b_Vp])

    def pre_gen(m, slot):
        pr = PAIR[m]
        (tokT, b_tokT), (khc, b_khc), (bhc, b_bhc), (atc, b_atc), (rtc, b_rtc) = pr["tokT"], pr["khc"], pr["bhc"], pr["atc"], pr["rtc"]
        (upc, b_upc), (Hs, b_Hs), (Hb, b_Hb) = pr["upc"], pr["Hs"], pr["Hb"]
        tb, b_tb = psb[0]
        tbv = tb[:].bitcast(BF16).rearrange("p (c t) -> p c t", t=128)[:, 4 * slot:4 * slot + 4, :]
        for i_, nm in enumerate(("vb", "Kh", "Bh", "sg")):
            src, b_src = fm(nm, m)
            P.op("pe", lambda e, i_=i_, src=src: e.transpose(out=tbv[:, i_, :], in_=src, identity=ident[:]), reads=[b_src, b_ident], writes=[b_tb])
        rk_src, b_rk = fm("rk", m)
        pbs, b_pbs = psb[1 + 2 * slot]
        P.op("pe", lambda e: e.matmul(pbs[:, 384:386], lhsT=rk_src, rhs=ind2[:], start=True, stop=True), reads=[b_rk, b_ind2], writes=[b_pbs])
        yield
        P.op("act", lambda e: e.activation(out=tokT[:], in_=tbv, func=AF.Copy), reads=[b_tb], writes=[b_tokT])
        P.op("dve", lambda e: e.tensor_copy(bsc[:, 2 * m:2 * m + 2], pbs[:, 384:386]), reads=[b_pbs], writes=[b_bsc])
        yield
        P.op("dve", lambda e: e.tensor_copy(vtok[:, m, :], tokT[:, 0, :]), reads=[b_tokT], writes=[b_vtok])
        P.op("dve", lambda e: e.tensor_copy(sgtok[:, m, :], tokT[:, 3, :]), reads=[b_tokT], writes=[b_sgtok])
        at_src, b_at = fm("At", m)
        rt_src, b_rt = fm("Rt", m)
        for c in range(2):
            P.op("dve", lambda e, c=c: e.tensor_scalar(out=khc[:, c, :], in0=tokT[:, 1, :], scalar1=rowm[:, c:c + 1], scalar2=None, op0=ALU.mult),
                 reads=[b_tokT, b_rowm], writes=[b_khc])
            P.op("dve", lambda e, c=c: e.tensor_scalar(out=bhc[:, c, :], in0=tokT[:, 2, :], scalar1=rowm[:, c:c + 1], scalar2=None, op0=ALU.mult),
                 reads=[b_tokT, b_rowm], writes=[b_bhc])
            cs = slice(c * 64, (c + 1) * 64)
            P.op("dve", lambda e, c=c, cs=cs: e.tensor_copy(atc[:, c, cs], at_src[:, cs]), reads=[b_at], writes=[b_atc])
            P.op("dve", lambda e, c=c, cs=cs: e.tensor_copy(rtc[:, c, cs], rt_src[:, cs]), reads=[b_rt], writes=[b_rtc])
        hg = [head_pre(m, 0, slot), head_pre(m, 1, slot)]
        while hg:
            for g in list(hg):
                try:
                    next(g)
                except StopIteration:
                    hg.remove(g)
            yield

    def serial_gen(m, slot):
        b5 = slot * 256
        b67 = slot * 128
        pr = PAIR[m]
        (tokT, b_tokT), (khc, b_khc), (bhc, b_bhc), (atc, b_atc), (rtc, b_rtc) = pr["tokT"], pr["khc"], pr["bhc"], pr["atc"], pr["rtc"]
        (upc, b_upc), (Hs, b_Hs), (Hb, b_Hb) = pr["upc"], pr["Hs"], pr["Hb"]
        slots = (s_in, s_mid, s_out)
        p5, b_p5 = psb[5]
        for c in range(2):
            cg = sp * 2 + c
            sl_i, sl_o = slots[c], slots[c + 1]
            for hh in range(2):
                hb = slice(hh * 64, hh * 64 + 64)
                gp, b_gp = psb[6 + hh]
                P.op("pe", lambda e, gp=gp, hb=hb, c=c, sl_i=sl_i: e.matmul(gp[:, b67 + 64:b67 + 128], lhsT=atc[hb, c, :], rhs=Hb[hb, sl_i, :], start=True, stop=True),
                     reads=[b_atc, b_Hb], writes=[b_gp], rows=("hi" if hh else "lo"))
            yield
            for hh in range(2):
                (G, b_G) = HEAD[2 * m + hh]["G"]
                gp, b_gp = psb[6 + hh]
                P.op("act", lambda e, G=G, gp=gp: e.activation(out=G[:], in_=gp[:, b67 + 64:b67 + 128], func=AF.Copy), reads=[b_gp], writes=[b_G])
            yield
            for hh in range(2):
                (G, b_G), (X, b_X) = HEAD[2 * m + hh]["G"], HEAD[2 * m + hh]["X"]
                P.op("pe", lambda e, hh=hh, X=X, G=G: e.matmul(p5[:, b5 + hh * 64:b5 + (hh + 1) * 64], lhsT=X[:], rhs=G[:], start=True, stop=True), reads=[b_X, b_G], writes=[b_p5])
            yield
            for hh in range(2):
                (Vp, b_Vp) = HEAD[2 * m + hh]["Vp"]
                P.op("dve", lambda e, hh=hh, Vp=Vp, c=c: e.tensor_tensor(out=upc[:, c, hh * 64:(hh + 1) * 64], in0=p5[:, b5 + hh * 64:b5 + (hh + 1) * 64], in1=Vp[:], op=ALU.add),
                     reads=[b_p5, b_Vp], writes=[b_upc])
            yield
            P.op("pe", lambda e, c=c: e.matmul(p5[:, b5 + 128:b5 + 256], lhsT=khc[:, c, :], rhs=tokT[:, 0, :], start=True, stop=False), reads=[b_khc, b_tokT], writes=[b_p5], acc=(b_p5, True))
            P.op("pe", lambda e, c=c: e.matmul(p5[:, b5 + 128:b5 + 256], lhsT=bhc[:, c, :], rhs=upc[:, c, :], start=False, stop=True), reads=[b_bhc, b_upc], writes=[b_p5], acc=(b_p5, False))
            yield
            for hh in range(2):
                hb = slice(hh * 64, hh * 64 + 64)
                P.op("dve", lambda e, hb=hb, hh=hh, cg=cg: e.scalar_tensor_tensor(
                    out=Hs[hb, :], in0=Hs[hb, :], scalar=wc[hb, m, cg:cg + 1], in1=p5[hb, b5 + 128 + hh * 64:b5 + 128 + (hh + 1) * 64], op0=ALU.mult, op1=ALU.add),
                    reads=[b_Hs, b_wc, b_p5], writes=[b_Hs])
            yield
            P.op("act", lambda e, sl_o=sl_o: e.activation(out=Hb[:, sl_o, :], in_=Hs[:], func=AF.Copy), reads=[b_Hs], writes=[b_Hb])
            yield
        for hh in range(2):
            hb = slice(hh * 64, hh * 64 + 64)
            idx = 2 * m + hh
            py, b_py = psb[6 + hh]
            d_ = HEAD[idx]
            (M4t, b_M4t), (mrbc, b_mrbc) = d_["M4"], d_["Mrbc"]
            P.op("pe", lambda e, py=py, M4t=M4t, hb=hb: e.matmul(py[:, b67:b67 + 64], lhsT=M4t[:, 3, :], rhs=tokT[:, 0, hb], start=True, stop=False), reads=[b_M4t, b_tokT], writes=[b_py], acc=(b_py, True))
            for c in range(2):
                P.op("pe", lambda e, py=py, hb=hb, c=c: e.matmul(py[:, b67:b67 + 64], lhsT=rtc[hb, c, :], rhs=Hb[hb, slots[c], :], start=False, stop=False),
                     reads=[b_rtc, b_Hb], writes=[b_py], acc=(b_py, False), rows=("hi" if hh else "lo"))
                P.op("pe", lambda e, py=py, hb=hb, c=c, mrbc=mrbc: e.matmul(py[:, b67:b67 + 64], lhsT=mrbc[:, c, :], rhs=upc[:, c, hb], start=False, stop=(c == 1)),
                     reads=[b_mrbc, b_upc], writes=[b_py], acc=(b_py, False))
        yield
        for hh in range(2):
            idx = 2 * m + hh
            py, b_py = psb[6 + hh]
            P.op("act", lambda e, py=py, idx=idx: e.activation(out=ytile[:, idx * 64:(idx + 1) * 64], in_=py[:, b67:b67 + 64], func=AF.Copy, accum_out=st1[:, idx:idx + 1]),
                 reads=[b_py], writes=[b_ytile, b_st1])
            P.op("act", lambda e, py=py, idx=idx: e.activation(out=ysq[:], in_=py[:, b67:b67 + 64], func=AF.Square, accum_out=st2[:, idx:idx + 1]),
                 reads=[b_py], writes=[b_ysq, b_st2])

    def tail():
        P.op("dve", lambda e: e.tensor_scalar(out=mean[:], in0=st1[:], scalar1=1.0 / 64, scalar2=None, op0=ALU.mult), reads=[b_st1], writes=[b_mean])
        P.op("dve", lambda e: e.tensor_tensor(out=rstd[:], in0=mean[:], in1=mean[:], op=ALU.mult), reads=[b_mean], writes=[b_rstd])
        P.op("dve", lambda e: e.scalar_tensor_tensor(out=rstd[:], in0=st2[:], scalar=1.0 / 64, in1=rstd[:], op0=ALU.mult, op1=ALU.subtract), reads=[b_st2, b_rstd], writes=[b_rstd])
        P.op("dve", lambda e: e.tensor_scalar(out=rstd[:], in0=rstd[:], scalar1=GN_EPS, scalar2=None, op0=ALU.add), reads=[b_rstd], writes=[b_rstd])
        P.op("act", lambda e: e.activation(out=rstd[:], in_=rstd[:], func=AF.Sqrt), reads=[b_rstd], writes=[b_rstd])
        P.op("dve", lambda e: e.reciprocal(out=rstd[:], in_=rstd[:]), reads=[b_rstd], writes=[b_rstd])
        for idx in range(6):
            P.op("dve", lambda e, idx=idx: e.tensor_scalar(out=ytile[:, idx * 64:(idx + 1) * 64], in0=ytile[:, idx * 64:(idx + 1) * 64], scalar1=mean[:, idx:idx + 1],
                                                           scalar2=rstd[:, idx:idx + 1], op0=ALU.subtract, op1=ALU.mult), reads=[b_ytile, b_mean, b_rstd], writes=[b_ytile])
        P.op("dve", lambda e: e.tensor_tensor(out=ytile[:], in0=ytile[:], in1=lng[:], op=ALU.mult), reads=[b_ytile, b_lng], writes=[b_ytile])
        P.op("dve", lambda e: e.tensor_tensor(out=ytile[:], in0=ytile[:], in1=lnb[:], op=ALU.add), reads=[b_ytile, b_lnb], writes=[b_ytile])
        vt6 = vtok[:].rearrange("p m (h v) -> p (m h) v", h=2)
        P.op("dve", lambda e: e.tensor_tensor(out=btmp[:].rearrange("p (i v) -> p i v", v=64), in0=vt6, in1=bsc[:].unsqueeze(2).to_broadcast([128, 6, 64]), op=ALU.mult),
             reads=[b_vtok, b_bsc], writes=[b_btmp])
        P.op("dve", lambda e: e.tensor_tensor(out=ytile[:], in0=ytile[:], in1=btmp[:], op=ALU.add), reads=[b_ytile, b_btmp], writes=[b_ytile])
        P.op("dve", lambda e: e.tensor_tensor(out=yout[:], in0=ytile[:], in1=sgtok[:].rearrange("p m f -> p (m f)"), op=ALU.mult), reads=[b_ytile, b_sgtok], writes=[b_yout])
        P.dma(lambda e: e.dma_start(out=ymix[rows, 0:384], in_=yout[:]), reads=[b_yout], q="act")

    return pre_gen, serial_gen, tail


_CACHE = {}


def rope_table(S):
    def tab(rot):
        half = rot // 2
        inv = (500000.0 ** (-np.arange(half, dtype=np.float32) / half)).astype(np.float32)
        ang = np.arange(S, dtype=np.float32)[:, None] * inv[None, :]
        return np.cos(ang).astype(np.float32), np.sin(ang).astype(np.float32)
    c64, s64 = tab(16)
    c32, s32 = tab(8)
    return np.ascontiguousarray(np.concatenate([c64, s64, c32, s32], axis=1).astype(np.float32))


def fm8(v):
    L = v.shape[0]
    return np.ascontiguousarray(v.reshape(L, 8, 128).transpose(0, 2, 1))


def make_in_maps(inputs, NB, S, DEPTH, ncores):
    f = lambda a: np.ascontiguousarray(np.asarray(a, dtype=np.float32))
    x = f(inputs["x"])
    p = f(inputs["p"])
    maps = []
    shared = {
        "norm_g": fm8(f(inputs["norm_g"])),
        "w_in": f(inputs["w_in"]),
        "w_out": f(inputs["w_out"]),
        "ple_norm_g": fm8(f(inputs["ple_norm_g"])),
        "ple_w_gate": f(inputs["ple_w_gate"]),
        "ple_w_proj": f(inputs["ple_w_proj"]),
        "final_norm_g": f(inputs["final_norm_g"]).reshape(1, D),
        "rope": rope_table(S),
    }
    shared.update(mixer_inputs(inputs, DEPTH))
    for c in range(ncores):
        m = dict(shared)
        m["x"] = np.ascontiguousarray(x[c * NB:(c + 1) * NB].reshape(NB * S, D))
        m["p"] = np.ascontiguousarray(p[:, c * NB:(c + 1) * NB].reshape(DEPTH, NB * S, 256))
        maps.append(m)
    return maps


def run(inputs, NB, S, DEPTH, ncores, dbg=()):
    key = (NB, S, DEPTH, tuple(dbg))
    if key not in _CACHE:
        _CACHE[key] = build(NB, S, DEPTH, dbg)
    nc = _CACHE[key]
    maps = make_in_maps(inputs, NB, S, DEPTH, ncores)
    res = run_bass_kernel_spmd(nc, maps, core_ids=list(range(ncores)))
    return res.results


def kernel(**inputs):
    B, S, _ = inputs["x"].shape
    DEPTH = inputs["w_in"].shape[0]
    ncores = 8
    NB = B // ncores
    results = run(inputs, NB, S, DEPTH, ncores)
    out = np.concatenate([r["out"].reshape(NB, S, D) for r in results], axis=0)
    return out.astype(np.float32)
```

```python
import numpy as np
import concourse.bass as bass
import concourse.mybir as mybir
from concourse.bass_utils import run_bass_kernel_spmd

F32 = mybir.dt.float32
BF16 = mybir.dt.bfloat16
AF = mybir.ActivationFunctionType
ALU = mybir.AluOpType
AX = mybir.AxisListType


class Buf:
    __slots__ = ("name", "w", "r", "excl")

    def __init__(self, name, excl=False):
        self.name = name
        self.excl = excl
        self.w = None
        self.r = {}


class Prog:
    ENGS = ("pe", "act", "dve", "pool", "sp")
    NDMA = 8

    def __init__(self, nc, stack=None, marks=None, dry=False):
        self.nc = nc
        self.stack = stack
        self.marks = marks
        self.dry = dry
        self.waited = set()
        self.rank = {}
        self.val_of = {}
        self.sems = {}
        self.engobj = {"pe": nc.tensor, "act": nc.scalar, "dve": nc.vector, "pool": nc.gpsimd, "sp": nc.sync}
        self.streams = {e: [] for e in self.ENGS}
        self.count = {e: 0 for e in self.ENGS}
        self.known = {e: {} for e in self.ENGS}
        self.dma_cnt = {}
        self.dma_rr = {e: 0 for e in self.ENGS}
        self.semkeys = set()
        self.bufs = {}
        self.bank_last = {}
        self.bank_rows = {}
        self.since_barrier = 100
        self.epoch = 0
        self.ccount = {}

    def buf(self, name, excl=False):
        return Buf(name, excl)

    def _need(self, eng, dep, waits):
        if dep is None:
            return
        key, val = dep[0], dep[1]
        if self.known[eng].get(key, 0) >= val:
            return
        self.known[eng][key] = val
        self.waited.add((key, val))
        for i, (k, v) in enumerate(waits):
            if k == key:
                waits[i] = (k, max(v, val))
                return
        waits.append((key, val))

    def _deps(self, eng, reads, writes, tag):
        waits = []
        for b in reads:
            if b.w is not None:
                self._need(eng, b.w, waits)
        for b in writes:
            if b.w is not None:
                if b.w[2] == tag and tag != "dma":
                    pass
                else:
                    self._need(eng, b.w, waits)
            for key, (val, reng) in b.r.items():
                if reng == tag and tag != "dma":
                    continue
                self._need(eng, (key, val), waits)
        return waits

    def _mark(self, done, eng, reads, writes):
        key, val = done
        for b in reads:
            old = b.r.get(key)
            if old is None or old[0] < val:
                b.r[key] = (val, eng)
        for b in writes:
            b.w = (key, val, eng)
            b.r = {}

    def op(self, eng, fn, reads=(), writes=(), acc=None, rows=None):
        assert eng in ("pe", "act", "dve", "pool")
        ex = [b_ for b_ in reads if b_.excl]
        if ex:
            reads = [b_ for b_ in reads if not b_.excl]
            writes = list(writes) + ex
        waits = self._deps(eng, reads, writes, eng)
        if eng == "pe":
            for wb_ in writes:
                lastr = self.bank_rows.get(id(wb_))
                if lastr is not None and rows is not None and lastr[0] is not None and lastr[0] != rows:
                    self._need("pe", lastr[1], waits)
                pk_ = "c_pe_%d" % self.epoch
                self.bank_rows[id(wb_)] = (rows, (pk_, self.ccount.get(pk_, 0) + 1))
        if eng == "pe":
            is_start = True if acc is None else acc[1]
            for wb_ in writes:
                last = self.bank_last.get(id(wb_))
                if is_start and last:
                    self._need("pe", last, waits)
                    self.bank_last[id(wb_)] = None
                if acc is not None:
                    pk = "c_pe_%d" % self.epoch
                    self.bank_last[id(wb_)] = (pk, self.ccount.get(pk, 0) + 1)
        key = "c_%s_%d" % (eng, self.epoch)
        self.ccount[key] = self.ccount.get(key, 0) + 1
        self.count[eng] = self.ccount[key]
        self.semkeys.add(key)
        done = (key, self.ccount[key])
        self._emit(eng, waits, fn, (key, 1))
        self._mark(done, eng, reads, writes)

    def dma(self, fn, reads=(), writes=(), q="sp"):
        if q == "sp" and writes and self.since_barrier < 12:
            self.since_barrier += 1
            self._dma(fn, reads, writes, q)
        return self._dma(fn, reads, writes, q)

    def _dma(self, fn, reads=(), writes=(), q="sp"):
        waits = self._deps(q, reads, writes, "dma")
        j = self.dma_rr[q] % self.NDMA
        self.dma_rr[q] += 1
        key = "d_%s_%d" % (q, j)
        self.semkeys.add(key)
        prev = self.dma_cnt.get(key, 0)
        if prev > 0:
            self._need(q, (key, prev * 16), waits)
        self.dma_cnt[key] = prev + 1
        done = (key, (prev + 1) * 16)
        self._emit(q, waits, fn, (key, 16))
        self._mark(done, "dma", reads, writes)
        return done

    def wait_all(self, eng, dones):
        waits = []
        for d in dones:
            self._need(eng, d, waits)
        self._emit(eng, waits, None, None)

    def _sem(self, key):
        s = self.sems.get(key)
        if s is None:
            s = self.stack.enter_context(self.nc.semaphore(key))
            self.sems[key] = s
        return s

    def _emit(self, eng, waits, fn, inc):
        if self.dry:
            return
        e = self.engobj[eng]
        tw = []
        for (k, v) in waits:
            if k.startswith("c_") and self.marks is not None:
                v = self.val_of[(k, v)]
            tw.append((k, v))
        attach = fn is not None and inc[0].startswith("c_") and len(tw) > 0
        for (k, v) in (tw[:-1] if attach else tw):
            e.wait_ge(self._sem(k), v)
        if fn is None:
            return
        ins = fn(e)
        if attach:
            ins._wait_ge(self._sem(tw[-1][0]), tw[-1][1])
        if inc[0].startswith("c_") and self.marks is not None:
            idx = self.ccount[inc[0]]
            if (inc[0], idx) in self.marks:
                self.rank[inc[0]] = self.rank.get(inc[0], 0) + 1
                self.val_of[(inc[0], idx)] = self.rank[inc[0]]
                ins.then_inc(self._sem(inc[0]), 1)
        else:
            ins.then_inc(self._sem(inc[0]), inc[1])

    def new_epoch(self):
        self.epoch += 1

    def barrier(self):
        dones = [(k_, c) for k_, c in self.ccount.items() if c > 0]
        dones += [(k, c * 16) for k, c in self.dma_cnt.items()]
        for e in self.ENGS:
            self.wait_all(e, dones)
        self.since_barrier = 0
        if getattr(self, "settle", None) is not None:
            fn = self.settle
            self.settle = None
            for _ in range(self.NDMA):
                d = fn()
                self.since_barrier = 0
                for e in self.ENGS:
                    self.wait_all(e, [d])
            self.settle = fn


from contextlib import ExitStack
import math

D = 1024
HD = 64
RW = 384
RWIN = 1664
INC = 4136
NORM_EPS = 1e-6
GN_EPS = 64e-5
DECAY_SCALE = math.exp(-0.5)
NEG = -30000.0
SKIP_PHASES = ()


class K:
    pass


def build(NB, S, DEPTH, dbg=()):
    _, waited = _build(NB, S, DEPTH, dbg, None, True)
    nc, _ = _build(NB, S, DEPTH, dbg, waited, False)
    return nc


def _build(NB, S, DEPTH, dbg, marks, dry):
    T = NB * S
    NT = T // 128
    TS = S // 128
    nc = bass.Bass("TRN2", target_bir_lowering=False)
    k = K()
    k.nc = nc
    k.uid = 0
    k.skip = SKIP_PHASES

    def sbt(name, shape, dt):
        k.uid += 1
        return nc.sbuf_tensor("%s_u%d" % (name, k.uid), shape, dt)
    k.sbt = sbt
    k.fills = {}

    def fill(v):
        if v not in k.fills:
            k.fills[v] = nc.gpsimd.to_reg(float(v))
        return k.fills[v]
    k.fill = fill

    def din(name, shape, dt=F32):
        return nc.dram_tensor(name, list(shape), dt, kind="ExternalInput").ap()

    def dscr(name, shape, dt=F32):
        kind = "ExternalOutput" if name in dbg else "Internal"
        return nc.dram_tensor(name, list(shape), dt, kind=kind).ap()

    x_in = din("x", [T, D])
    p_in = din("p", [DEPTH, T, 256])
    norm_g = din("norm_g", [DEPTH, 128, 8])
    w_in = din("w_in", [DEPTH, D, INC])
    w_out = din("w_out", [DEPTH, D, D])
    ple_g = din("ple_norm_g", [DEPTH, 128, 8])
    ple_wg = din("ple_w_gate", [DEPTH, D, D])
    ple_wp = din("ple_w_proj", [DEPTH, 256, D])
    fin_g = din("final_norm_g", [1, D])
    rw_in = {"mu": din("rw_mu", [DEPTH, 128, 13]), "par": din("rw_par", [DEPTH, 128, 5, 3]), "w_up": din("rw_w_up", [DEPTH, 64, 384]),
             "a_up": din("rw_a_up", [DEPTH, 64, 384]), "ln_g": din("rw_ln_g", [DEPTH, 1, 384]), "ln_b": din("rw_ln_b", [DEPTH, 1, 384])}
    rope = din("rope", [S, 24])
    out = nc.dram_tensor("out", [T, D], F32, kind="ExternalOutput").ap()

    xres = dscr("xres", [T, D])
    zr = dscr("zr", [T, RWIN])
    dq = dscr("dq", [T, 384], BF16)
    di = dscr("di", [T, 288], BF16)
    iwd = dscr("iwd", [T, 8])
    gates = dscr("gates", [T, 640], BF16)
    cq = dscr("cq", [T, 384], BF16)
    ck = dscr("ck", [T, 384], BF16)
    cv = dscr("cv", [T, 384], BF16)
    ymix = dscr("ymix", [T, D], BF16)
    oaug = dscr("oaug", [T, 3, 2, 80])

    with ExitStack() as st:
        P = Prog(nc, st, marks=marks, dry=dry)
        k.P = P

        def sb(stack, name, shape, dt):
            t = stack.enter_context(k.sbt(name, list(shape), dt))
            return t, P.buf(name)

        psb = []
        for i in range(8):
            t = st.enter_context(nc.psum_tensor("psb%d" % i, [128, 512], F32))
            psb.append((t, P.buf("psb%d" % i, excl=True)))
        identf, b_identf = sb(st, "identf", [128, 128], F32)
        ident, b_ident = sb(st, "ident", [128, 128], BF16)
        ropet, b_ropet = sb(st, "ropet", [128, TS, 24], F32)
        P.op("pool", lambda e: e.memset(identf[:], 0.0), writes=[b_identf])
        P.op("pool", lambda e: e.affine_select(out=identf[:], in_=identf[:], pattern=[[-1, 128]],
                                               compare_op=ALU.not_equal, fill=k.fill(1.0), base=0,
                                               channel_multiplier=1), reads=[b_identf], writes=[b_identf])
        P.op("dve", lambda e: e.tensor_copy(ident[:], identf[:]), reads=[b_identf], writes=[b_ident])
        P.dma(lambda e: e.dma_start(out=ropet[:], in_=rope.rearrange("(t p) c -> p t c", p=128)), writes=[b_ropet])
        k.psb, k.ident, k.b_ident, k.identf, k.b_identf = psb, ident, b_ident, identf, b_identf
        dummy, b_dummy = sb(st, "dummyt", [128, 64], F32)
        P.settle_unused = lambda: P.dma(lambda e: e.dma_start(out=dummy[:], in_=rope[0:128, 0:16].bitcast(F32) if False else x_in[0:128, 0:64]), writes=[b_dummy])

        def rms_stats(eng_sq, xt, b_xt, junk, b_junk, ss, b_ss, rs, b_rs):
            P.op("act", lambda e: e.activation(out=junk, in_=xt, func=AF.Square, accum_out=ss),
                 reads=[b_xt], writes=[b_junk, b_ss])
            P.op("dve", lambda e: e.tensor_scalar(out=rs, in0=ss, scalar1=1.0 / D, scalar2=NORM_EPS,
                                                  op0=ALU.mult, op1=ALU.add), reads=[b_ss], writes=[b_rs])
            P.op("act", lambda e: e.activation(out=rs, in_=rs, func=AF.Sqrt), reads=[b_rs], writes=[b_rs])
            P.op("dve", lambda e: e.reciprocal(out=rs, in_=rs), reads=[b_rs], writes=[b_rs])

        def norm_transpose(xt, b_xt, rs, b_rs, hs, b_hs, gfm, b_gfm, dst_fn, b_dst, pa, pb):
            P.op("act", lambda e: e.activation(out=hs, in_=xt, func=AF.Copy, scale=rs),
                 reads=[b_xt, b_rs], writes=[b_hs])
            for half, (pt, b_pt) in enumerate((pa, pb)):
                tpv = pt[:].rearrange("p (c t) -> p c t", c=4)
                for c in range(4):
                    cc = half * 4 + c
                    P.op("pe", lambda e, c=c, cc=cc, tpv=tpv: e.transpose(out=tpv[:, c, :], in_=hs[:, cc * 128:(cc + 1) * 128],
                                                                          identity=identf[:]),
                         reads=[b_hs, b_identf], writes=[b_pt])
                P.op("dve", lambda e, tpv=tpv, half=half: e.tensor_tensor(
                    out=dst_fn(half), in0=tpv, in1=gfm[:, half * 4:half * 4 + 4].unsqueeze(2).to_broadcast([128, 4, 128]),
                    op=ALU.mult), reads=[b_pt, b_gfm], writes=[b_dst])

        def rope_apply(v, nh, half, ti, tmp, b_tmp, b_v, tab_off):
            c = ropet[:, ti, tab_off:tab_off + half].unsqueeze(1).to_broadcast([128, nh, half])
            s = ropet[:, ti, tab_off + half:tab_off + 2 * half].unsqueeze(1).to_broadcast([128, nh, half])
            x1 = v[:, :, 0:half]
            x2 = v[:, :, half:2 * half]
            tv = [tmp[:, j, 0:nh * half].rearrange("p (h d) -> p h d", h=nh) for j in range(4)]
            eng = "dve"
            P.op(eng, lambda e: e.tensor_tensor(out=tv[0], in0=x1, in1=c, op=ALU.mult), reads=[b_v, b_ropet], writes=[b_tmp])
            P.op(eng, lambda e: e.tensor_tensor(out=tv[1], in0=x2, in1=s, op=ALU.mult), reads=[b_v, b_ropet], writes=[b_tmp])
            P.op(eng, lambda e: e.tensor_tensor(out=tv[2], in0=x2, in1=c, op=ALU.mult), reads=[b_v, b_ropet], writes=[b_tmp])
            P.op(eng, lambda e: e.tensor_tensor(out=tv[3], in0=x1, in1=s, op=ALU.mult), reads=[b_v, b_ropet], writes=[b_tmp])
            P.op(eng, lambda e: e.tensor_tensor(out=x1, in0=tv[0], in1=tv[1], op=ALU.subtract), reads=[b_tmp], writes=[b_v])
            P.op(eng, lambda e: e.tensor_tensor(out=x2, in0=tv[2], in1=tv[3], op=ALU.add), reads=[b_tmp], writes=[b_v])

        for l in range(DEPTH):
            xsrc = x_in if l == 0 else xres
            if l > 0:
                P.new_epoch()
            with ExitStack() as ph:
                hT, b_hT = sb(ph, "hT", [128, 8, T], BF16)
                gfm, b_gfm = sb(ph, "gfm", [128, 8], F32)
                P.dma(lambda e: e.dma_start(out=gfm[:], in_=norm_g[l]), writes=[b_gfm])
                xts = [sb(ph, "xt%d" % j, [128, D], F32) for j in range(2)]
                hss = [sb(ph, "hs%d" % j, [128, D], F32) for j in range(2)]
                sss = [sb(ph, "ss%d" % j, [128, 1], F32) for j in range(2)]
                rss = [sb(ph, "rs%d" % j, [128, 1], F32) for j in range(2)]
                for i in range(NT):
                    (xt, b_xt), (hs, b_hs), (ss, b_ss), (rs, b_rs) = xts[i % 2], hss[i % 2], sss[i % 2], rss[i % 2]
                    P.dma(lambda e, i=i, xt=xt: e.dma_start(out=xt[:], in_=xsrc[i * 128:(i + 1) * 128, :]), writes=[b_xt])
                    rms_stats("act", xt[:], b_xt, hs[:], b_hs, ss[:], b_ss, rs[:], b_rs)
                    norm_transpose(xt[:], b_xt, rs[:], b_rs, hs[:], b_hs, gfm, b_gfm,
                                   lambda half, i=i: hT[:, half * 4:half * 4 + 4, i * 128:(i + 1) * 128], b_hT,
                                   psb[(i % 2) * 2], psb[(i % 2) * 2 + 1])
                groups = [(0, 512, "zr"), (512, 512, "zr"), (1024, 512, "zr"), (1536, 128, "zr"),
                          (1664, 384, "dq"), (2048, 296, "di"), (2344, 256, "g0"),
                          (2600, 384, "cq"), (2984, 384, "ck"), (3368, 384, "cv"), (3752, 384, "g1")]
                wbs = [sb(ph, "wb%d" % j, [128, 8, 512], BF16) for j in range(2)]
                st32 = [sb(ph, "st32_%d" % j, [128, 512], F32) for j in range(4)]
                stb = [sb(ph, "stb_%d" % j, [128, 512], BF16) for j in range(4)]
                rtmp = [sb(ph, "rtmp%d" % j, [128, 4, 64], F32) for j in range(4)]
                cnt = 0
                for gi, (c0, n, kind) in enumerate(groups):
                    wb, b_wb = wbs[gi % 2]
                    P.dma(lambda e, wb=wb, c0=c0, n=n: e.dma_start(
                        out=wb[:, :, 0:n], in_=w_in[l, :, c0:c0 + n].rearrange("(c p) n -> p c n", p=128)),
                        writes=[b_wb], q="pool")
                    for i in range(NT):
                        ps, b_ps = psb[4 + (cnt % 4)]
                        s32, b_s32 = st32[cnt % 4]
                        s16, b_s16 = stb[cnt % 4]
                        rt, b_rt = rtmp[cnt % 4]
                        cnt += 1
                        for c in range(8):
                            P.op("pe", lambda e, c=c, ps=ps, wb=wb, i=i, n=n: e.matmul(
                                ps[:, 0:n], lhsT=hT[:, c, i * 128:(i + 1) * 128], rhs=wb[:, c, 0:n],
                                start=(c == 0), stop=(c == 7)), reads=[b_hT, b_wb], writes=[b_ps], acc=(b_ps, c == 0))
                        rows = slice(i * 128, (i + 1) * 128)
                        ti = i % TS
                        if kind == "zr":
                            ev = "act" if (cnt % 2) else "dve"
                            if ev == "act":
                                P.op("act", lambda e, s32=s32, ps=ps, n=n: e.activation(out=s32[:, 0:n], in_=ps[:, 0:n], func=AF.Copy),
                                     reads=[b_ps], writes=[b_s32])
                            else:
                                P.op("dve", lambda e, s32=s32, ps=ps, n=n: e.tensor_copy(s32[:, 0:n], ps[:, 0:n]),
                                     reads=[b_ps], writes=[b_s32])
                            P.dma(lambda e, s32=s32, rows=rows, c0=c0, n=n: e.dma_start(out=zr[rows, c0:c0 + n], in_=s32[:, 0:n]),
                                  reads=[b_s32], q="act")
                        elif kind in ("g0", "g1"):
                            goff = 0 if kind == "g0" else 256
                            P.op("act", lambda e, s16=s16, ps=ps, n=n: e.activation(out=s16[:, 0:n], in_=ps[:, 0:n], func=AF.Silu),
                                 reads=[b_ps], writes=[b_s16])
                            P.dma(lambda e, s16=s16, rows=rows, goff=goff, n=n: e.dma_start(out=gates[rows, goff:goff + n], in_=s16[:, 0:n]),
                                  reads=[b_s16], q="act")
                        elif kind == "cv":
                            P.op("dve", lambda e, s16=s16, ps=ps, n=n: e.tensor_copy(s16[:, 0:n], ps[:, 0:n]),
                                 reads=[b_ps], writes=[b_s16])
                            P.dma(lambda e, s16=s16, rows=rows, n=n: e.dma_start(out=cv[rows, :], in_=s16[:, 0:n]),
                                  reads=[b_s16], q="act")
                        else:
                            P.op("act", lambda e, s32=s32, ps=ps, n=n: e.activation(out=s32[:, 0:n], in_=ps[:, 0:n], func=AF.Copy),
                                 reads=[b_ps], writes=[b_s32])
                            if kind == "dq":
                                v = s32[:, 0:320].rearrange("p (h d) -> p h d", h=5)
                                rope_apply(v, 5, 8, ti, rt, b_rt, b_s32, 0)
                                dst, nn = dq, 384
                            elif kind == "di":
                                v = s32[:, 0:288].rearrange("p (h d) -> p h d", h=9)
                                rope_apply(v, 9, 4, ti, rt, b_rt, b_s32, 16)
                                dst, nn = di, 288
                                P.dma(lambda e, s32=s32, rows=rows: e.dma_start(out=iwd[rows, :], in_=s32[:, 288:296]),
                                      reads=[b_s32], q="act")
                            else:
                                v = s32[:, 0:384].rearrange("p (h d) -> p h d", h=6)
                                rope_apply(v, 6, 8, ti, rt, b_rt, b_s32, 0)
                                dst, nn = (cq if kind == "cq" else ck), 384
                            P.op("dve", lambda e, s16=s16, s32=s32, nn=nn: e.tensor_copy(s16[:, 0:nn], s32[:, 0:nn]),
                                 reads=[b_s32], writes=[b_s16])
                            P.dma(lambda e, s16=s16, rows=rows, dst=dst, nn=nn: e.dma_start(out=dst[rows, :], in_=s16[:, 0:nn]),
                                  reads=[b_s16], q="act")
                P.barrier()

            MIXERS(k, l, locals())
            P.barrier()

            with ExitStack() as ph:
                wo, b_wo = sb(ph, "wo", [128, 8, D], BF16)
                wg, b_wg = sb(ph, "wg", [128, 8, D], BF16)
                wp, b_wp = sb(ph, "wp", [128, 2, D], BF16)
                gfm, b_gfm = sb(ph, "gfm2", [128, 8], F32)
                P.dma(lambda e: e.dma_start(out=wo[:], in_=w_out[l].rearrange("(c p) n -> p c n", p=128)), writes=[b_wo], q="pool")
                P.dma(lambda e: e.dma_start(out=wg[:], in_=ple_wg[l].rearrange("(c p) n -> p c n", p=128)), writes=[b_wg], q="pool")
                P.dma(lambda e: e.dma_start(out=wp[:], in_=ple_wp[l].rearrange("(c p) n -> p c n", p=128)), writes=[b_wp], q="pool")
                P.dma(lambda e: e.dma_start(out=gfm[:], in_=ple_g[l]), writes=[b_gfm])
                last = (l == DEPTH - 1)
                if last:
                    fg, b_fg = sb(ph, "fg", [128, D], F32)
                    P.dma(lambda e: e.dma_start(out=fg[:], in_=fin_g.to_broadcast([128, D])), writes=[b_fg])
                xts = [sb(ph, "oxt%d" % j, [128, D], F32) for j in range(2)]
                yts = [sb(ph, "oyt%d" % j, [128, D], BF16) for j in range(2)]
                pts = [sb(ph, "opt%d" % j, [128, 256], F32) for j in range(2)]
                yTs = [sb(ph, "oyT%d" % j, [128, 8, 128], BF16) for j in range(2)]
                hTs = [sb(ph, "ohT%d" % j, [128, 8, 128], BF16) for j in range(2)]
                pTs = [sb(ph, "opT%d" % j, [128, 2, 128], BF16) for j in range(2)]
                hss = [sb(ph, "ohs%d" % j, [128, D], F32) for j in range(2)]
                x1s = [sb(ph, "ox1%d" % j, [128, D], F32) for j in range(2)]
                gss = [sb(ph, "ogs%d" % j, [128, D], F32) for j in range(2)]
                sss = [sb(ph, "oss%d" % j, [128, 1], F32) for j in range(2)]
                rss = [sb(ph, "ors%d" % j, [128, 1], F32) for j in range(2)]
                odone = []
                for i in range(NT):
                    j = i % 2
                    (xt, b_xt), (yt, b_yt), (pt, b_pt), (yT, b_yT), (hT2, b_hT2), (pT, b_pT) = xts[j], yts[j], pts[j], yTs[j], hTs[j], pTs[j]
                    (hs, b_hs), (x1, b_x1), (gs, b_gs), (ss, b_ss), (rs, b_rs) = hss[j], x1s[j], gss[j], sss[j], rss[j]
                    rows = slice(i * 128, (i + 1) * 128)
                    P.dma(lambda e, xt=xt, rows=rows: e.dma_start(out=xt[:], in_=xsrc[rows, :]), writes=[b_xt])
                    P.dma(lambda e, yt=yt, rows=rows: e.dma_start(out=yt[:], in_=ymix[rows, :]), writes=[b_yt])
                    P.dma(lambda e, pt=pt, rows=rows: e.dma_start(out=pt[:], in_=p_in[l, rows, :]), writes=[b_pt])
                    tb, b_tb = psb[0]
                    tbv = tb[:].bitcast(BF16).rearrange("p (c t) -> p c t", t=128)
                    for c in range(8):
                        P.op("pe", lambda e, c=c, yt=yt, tbv=tbv: e.transpose(out=tbv[:, c, :], in_=yt[:, c * 128:(c + 1) * 128], identity=ident[:]),
                             reads=[b_yt, b_ident], writes=[b_tb])
                    P.op("dve", lambda e, yT=yT, tbv=tbv: e.tensor_copy(yT[:], tbv[:, 0:8, :]), reads=[b_tb], writes=[b_yT])
                    for hf in range(2):
                        ps, b_ps = psb[2 + hf]
                        for c in range(8):
                            P.op("pe", lambda e, c=c, ps=ps, yT=yT, hf=hf: e.matmul(ps[:], lhsT=yT[:, c, :], rhs=wo[:, c, hf * 512:(hf + 1) * 512],
                                                                                start=(c == 0), stop=(c == 7)),
                                 reads=[b_yT, b_wo], writes=[b_ps], acc=(b_ps, c == 0))
                        P.op("dve", lambda e, ps=ps, x1=x1, xt=xt, hf=hf: e.tensor_tensor(out=x1[:, hf * 512:(hf + 1) * 512], in0=ps[:],
                                                                                      in1=xt[:, hf * 512:(hf + 1) * 512], op=ALU.add),
                             reads=[b_ps, b_xt], writes=[b_x1])
                    rms_stats("act", x1[:], b_x1, hs[:], b_hs, ss[:], b_ss, rs[:], b_rs)
                    norm_transpose(x1[:], b_x1, rs[:], b_rs, hs[:], b_hs, gfm, b_gfm,
                                   lambda half, hT2=hT2: hT2[:, half * 4:half * 4 + 4, :], b_hT2, psb[4], psb[5])
                    tb2, b_tb2 = psb[1]
                    tb2v = tb2[:].rearrange("p (c t) -> p c t", t=128)
                    for c in range(2):
                        P.op("pe", lambda e, c=c, pt=pt, tb2v=tb2v: e.transpose(out=tb2v[:, c, :], in_=pt[:, c * 128:(c + 1) * 128], identity=identf[:]),
                             reads=[b_pt, b_identf], writes=[b_tb2])
                    P.op("dve", lambda e, pT=pT, tb2v=tb2v: e.tensor_copy(pT[:], tb2v[:, 0:2, :]), reads=[b_tb2], writes=[b_pT])
                    for hf in range(2):
                        psg, b_psg = psb[6]
                        psp, b_psp = psb[7]
                        cols = slice(hf * 512, (hf + 1) * 512)
                        for c in range(8):
                            P.op("pe", lambda e, c=c, psg=psg, hT2=hT2, cols=cols: e.matmul(psg[:], lhsT=hT2[:, c, :], rhs=wg[:, c, cols],
                                                                                       start=(c == 0), stop=(c == 7)),
                                 reads=[b_hT2, b_wg], writes=[b_psg], acc=(b_psg, c == 0))
                        for c in range(2):
                            P.op("pe", lambda e, c=c, psp=psp, pT=pT, cols=cols: e.matmul(psp[:], lhsT=pT[:, c, :], rhs=wp[:, c, cols],
                                                                                     start=(c == 0), stop=(c == 1)),
                                 reads=[b_pT, b_wp], writes=[b_psp], acc=(b_psp, c == 0))
                        P.op("act", lambda e, gs=gs, psg=psg, cols=cols: e.activation(out=gs[:, cols], in_=psg[:], func=AF.Sigmoid),
                             reads=[b_psg], writes=[b_gs])
                        P.op("dve", lambda e, gs=gs, psp=psp, cols=cols: e.tensor_tensor(out=gs[:, cols], in0=psp[:], in1=gs[:, cols], op=ALU.mult),
                             reads=[b_psp, b_gs], writes=[b_gs])
                        P.op("dve", lambda e, gs=gs, x1=x1, cols=cols: e.tensor_tensor(out=x1[:, cols], in0=x1[:, cols], in1=gs[:, cols], op=ALU.add),
                             reads=[b_gs, b_x1], writes=[b_x1])
                    if not last:
                        odone.append(P.dma(lambda e, x1=x1, rows=rows: e.dma_start(out=xres[rows, :], in_=x1[:]), reads=[b_x1], q="act"))
                    else:
                        rms_stats("act", x1[:], b_x1, hs[:], b_hs, ss[:], b_ss, rs[:], b_rs)
                        P.op("dve", lambda e, hs=hs, x1=x1, rs=rs: e.scalar_tensor_tensor(out=hs[:], in0=x1[:], scalar=rs[:], in1=fg[:],
                                                                                       op0=ALU.mult, op1=ALU.mult),
                             reads=[b_x1, b_rs, b_fg], writes=[b_hs])
                        odone.append(P.dma(lambda e, hs=hs, rows=rows: e.dma_start(out=out[rows, :], in_=hs[:]), reads=[b_hs], q="act"))
                P.barrier()
        P.barrier()
    return nc, P.waited


DIL_RATES = (1, 4, 16)


def make_masks(k, st):
    nc, P = k.nc, k.P
    mf = st.enter_context(k.sbt("maskf", [128, 2, 128], F32)); b_mf = P.buf("maskf")
    mk = st.enter_context(k.sbt("maskb", [128, 2, 128], BF16)); b_mk = P.buf("maskb")
    P.op("pool", lambda e: e.memset(mf[:], 1.0), writes=[b_mf])
    P.op("pool", lambda e: e.affine_select(out=mf[:, 0, :], in_=mf[:, 0, :], pattern=[[1, 128]], compare_op=ALU.is_ge,
                                           fill=k.fill(0.0), base=0, channel_multiplier=-1), reads=[b_mf], writes=[b_mf])
    P.op("pool", lambda e: e.affine_select(out=mf[:, 1, :], in_=mf[:, 1, :], pattern=[[-1, 128]], compare_op=ALU.is_ge,
                                           fill=k.fill(0.0), base=0, channel_multiplier=1), reads=[b_mf], writes=[b_mf])
    P.op("dve", lambda e: e.tensor_copy(mk[:], mf[:]), reads=[b_mf], writes=[b_mk])
    k.mask, k.b_mask = mk, b_mk


def dilated_phase(k, l, env):
    nc, P = k.nc, k.P
    k.dbg_barrier = False
    NB, S, T, NT = env["NB"], env["S"], env["T"], env["NT"]
    cq, ck, cv, gates, ymix, oaug = env["cq"], env["ck"], env["cv"], env["gates"], env["ymix"], env["oaug"]
    psb, ident, b_ident = k.psb, k.ident, k.b_ident
    mask, b_mask = k.mask, k.b_mask
    with ExitStack() as ph:
        def sb(name, shape, dt):
            return ph.enter_context(k.sbt(name, list(shape), dt)), P.buf(name)
        NBMAX = S // 128
        sets = []
        for j in range(2):
            sets.append(dict(
                qr=sb("d_qr%d" % j, [128, NBMAX, 128], BF16), kr=sb("d_kr%d" % j, [128, NBMAX, 128], BF16),
                vr=sb("d_vr%d" % j, [128, NBMAX, 128], BF16),
                qT=sb("d_qT%d" % j, [64, 2, NBMAX, 128], BF16), kT=sb("d_kT%d" % j, [64, 2, NBMAX, 128], BF16),
                va=sb("d_va%d" % j, [128, NBMAX, 2, 65], BF16), oo=sb("d_oo%d" % j, [128, NBMAX, 2, 80], F32)))
            va, b_va = sets[j]["va"]
            P.op("pool", lambda e, va=va: e.memset(va[:], 1.0), writes=[b_va])
            oo_, b_oo_ = sets[j]["oo"]
            P.op("pool", lambda e, oo_=oo_: e.memset(oo_[:], 0.0), writes=[b_oo_])
        pts = [sb("d_pt%d" % j, [128, 2, 2, 128], BF16) for j in range(2)]
        qfs = [sb("d_qf%d" % j, [128, 2, 128], F32) for j in range(2)]
        cnt = 0
        for b in range(NB):
            for g, rate in enumerate(DIL_RATES):
                n = S // rate
                nb = n // 128
                cols = slice(g * 128, (g + 1) * 128)
                for rho in range(rate):
                    s_ = sets[cnt % 2]
                    cnt += 1
                    (qr, b_qr), (kr, b_kr), (vr, b_vr) = s_["qr"], s_["kr"], s_["vr"]
                    (qT, b_qT), (kT, b_kT), (va, b_va), (oo, b_oo) = s_["qT"], s_["kT"], s_["va"], s_["oo"]

                    def rows(d):
                        return d[b * S:(b + 1) * S, :].rearrange("(j i r) c -> r i j c", i=128, r=rate)[rho]
                    for j0 in range(0, nb, 4):
                        j1 = min(nb, j0 + 4)
                        P.dma(lambda e, qr=qr, src=rows(cq)[:, j0:j1, cols], j0=j0, j1=j1: e.dma_start(out=qr[:, j0:j1, :], in_=src), writes=[b_qr])
                        P.dma(lambda e, kr=kr, src=rows(ck)[:, j0:j1, cols], j0=j0, j1=j1: e.dma_start(out=kr[:, j0:j1, :], in_=src), writes=[b_kr])
                        P.dma(lambda e, vr=vr, src=rows(cv)[:, j0:j1, cols], j0=j0, j1=j1: e.dma_start(out=vr[:, j0:j1, :], in_=src), writes=[b_vr])
                    if k.dbg_barrier:
                        P.barrier()
                    P.op("dve", lambda e, va=va, vr=vr: e.tensor_copy(
                        va[:, 0:nb, :, 0:64], vr[:, 0:nb, :].rearrange("p j (h d) -> p j h d", h=2)), reads=[b_vr], writes=[b_va])
                    for j in range(nb):
                        tb, b_tb = psb[j % 2]
                        tbv = tb[:].bitcast(BF16).rearrange("p (c t) -> p c t", t=128)
                        for h in range(2):
                            P.op("pe", lambda e, tbv=tbv, qr=qr, j=j, h=h: e.transpose(out=tbv[0:64, h, :], in_=qr[:, j, h * 64:(h + 1) * 64], identity=ident[:]),
                                 reads=[b_qr, b_ident], writes=[b_tb])
                            P.op("pe", lambda e, tbv=tbv, kr=kr, j=j, h=h: e.transpose(out=tbv[0:64, 2 + h, :], in_=kr[:, j, h * 64:(h + 1) * 64], identity=ident[:]),
                                 reads=[b_kr, b_ident], writes=[b_tb])
                        P.op("dve", lambda e, tbv=tbv, qT=qT, j=j: e.tensor_copy(qT[:, :, j, :], tbv[0:64, 0:2, :]), reads=[b_tb], writes=[b_qT])
                        P.op("dve", lambda e, tbv=tbv, kT=kT, j=j: e.tensor_copy(kT[:, :, j, :], tbv[0:64, 2:4, :]), reads=[b_tb], writes=[b_kT])
                    for j in range(nb):
                        sp_, b_sp = psb[2 + (j % 2)]
                        spv = sp_[:].rearrange("p (h c q) -> p h c q", h=2, c=2)
                        pt, b_pt = pts[j % 2]
                        ncp = 2 if j > 0 else 1
                        for h in range(2):
                            for c in range(ncp):
                                jj = j - c
                                P.op("pe", lambda e, spv=spv, h=h, c=c, jj=jj, j=j, kT=kT, qT=qT: e.matmul(
                                    spv[:, h, c, :], lhsT=kT[:, h, jj, :], rhs=qT[:, h, j, :], start=True, stop=True),
                                    reads=[b_kT, b_qT], writes=[b_sp])
                        P.op("act", lambda e, pt=pt, spv=spv, ncp=ncp: e.activation(out=pt[:, :, 0:ncp, :], in_=spv[:, :, 0:ncp, :], func=AF.Exp, scale=0.125),
                             reads=[b_sp], writes=[b_pt])
                        P.op("dve", lambda e, pt=pt, ncp=ncp: e.tensor_tensor(
                            out=pt[:, :, 0:ncp, :], in0=pt[:, :, 0:ncp, :], in1=mask[:, 0:ncp, :].unsqueeze(1).to_broadcast([128, 2, ncp, 128]), op=ALU.mult),
                            reads=[b_pt, b_mask], writes=[b_pt])
                        for h in range(2):
                            op_, b_op = psb[4 + (j % 2) * 2 + h]
                            for c in range(ncp):
                                jj = j - c
                                P.op("pe", lambda e, op_=op_, h=h, c=c, jj=jj, pt=pt, va=va, ncp=ncp: e.matmul(
                                    op_[:, 0:65], lhsT=pt[:, h, c, :], rhs=va[:, jj, h, :], start=(c == 0), stop=(c == ncp - 1)),
                                    reads=[b_pt, b_va], writes=[b_op], acc=(b_op, c == 0))
                            P.op("dve", lambda e, oo=oo, op_=op_, j=j, h=h: e.tensor_copy(oo[:, j, h, 0:65], op_[:, 0:65]), reads=[b_op], writes=[b_oo])
                    dst = oaug[b * S:(b + 1) * S].rearrange("(j i r) g h d -> r i j g h d", i=128, r=rate)[rho][:, :, g, :, :]
                    P.dma(lambda e, oo=oo, dst=dst: e.dma_start(out=dst, in_=oo[:, 0:nb, :, :]), reads=[b_oo], q="act")
        P.barrier()
        ots = [sb("d_ot%d" % j, [128, 3, 2, 80], F32) for j in range(2)]
        gts = [sb("d_gt%d" % j, [128, 384], BF16) for j in range(2)]
        lls = [sb("d_ll%d" % j, [128, 2], F32) for j in range(2)]
        yys = [sb("d_yy%d" % j, [128, 3, 2, 64], F32) for j in range(2)]
        ybs = [sb("d_yb%d" % j, [128, 384], BF16) for j in range(2)]
        for i in range(NT):
            (ot, b_ot), (gt, b_gt), (ll, b_ll), (yy, b_yy), (yb, b_yb) = ots[i % 2], gts[i % 2], lls[i % 2], yys[i % 2], ybs[i % 2]
            rows = slice(i * 128, (i + 1) * 128)
            P.dma(lambda e, ot=ot, rows=rows: e.dma_start(out=ot[:], in_=oaug[rows]), writes=[b_ot])
            P.dma(lambda e, gt=gt, rows=rows: e.dma_start(out=gt[:], in_=gates[rows, 256:640]), writes=[b_gt])
            P.op("dve", lambda e, ll=ll, ot=ot: e.tensor_tensor(out=ll[:], in0=ot[:, 0, :, 64], in1=ot[:, 1, :, 64], op=ALU.add), reads=[b_ot], writes=[b_ll])
            P.op("dve", lambda e, ll=ll, ot=ot: e.tensor_tensor(out=ll[:], in0=ll[:], in1=ot[:, 2, :, 64], op=ALU.add), reads=[b_ot, b_ll], writes=[b_ll])
            P.op("dve", lambda e, ll=ll: e.reciprocal(out=ll[:], in_=ll[:]), reads=[b_ll], writes=[b_ll])
            for g in range(3):
                P.op("dve", lambda e, yy=yy, ot=ot, ll=ll, g=g: e.tensor_tensor(
                    out=yy[:, g, :, :], in0=ot[:, g, :, 0:64], in1=ll[:].unsqueeze(2).to_broadcast([128, 2, 64]), op=ALU.mult),
                    reads=[b_ot, b_ll], writes=[b_yy])
            P.op("dve", lambda e, yb=yb, yy=yy, gt=gt: e.tensor_tensor(out=yb[:], in0=yy[:].rearrange("p g h d -> p (g h d)"), in1=gt[:], op=ALU.mult),
                 reads=[b_yy, b_gt], writes=[b_yb])
            P.dma(lambda e, yb=yb, rows=rows: e.dma_start(out=ymix[rows, 640:1024], in_=yb[:]), reads=[b_yb], q="act")


def MIXERS(k, l, env):
    nc, P = k.nc, k.P
    ymix, NT = env["ymix"], env["NT"]
    if l == 0:
        make_masks(k, env["st"])
    skip = k.skip
    if "dil" not in skip:
        dilated_phase(k, l, env)
        P.barrier()
    if "dsa" not in skip:
        dsa_phase(k, l, env)
        P.barrier()
    if "rwkv" not in skip:
        rwkv_phase(k, l, env)


def mixer_inputs(inputs, DEPTH):
    f = lambda a: np.ascontiguousarray(np.asarray(a, dtype=np.float32))
    L = DEPTH

    def fm(v, nchunk):
        return np.ascontiguousarray(f(v).reshape(L, nchunk, 128).transpose(0, 2, 1))
    par = np.stack([fm(inputs["rwkv_w0"], 3), fm(inputs["rwkv_a0"], 3), fm(inputs["rwkv_k_k"], 3), fm(inputs["rwkv_k_a"], 3),
                    fm(f(inputs["rwkv_r_k"]).reshape(L, 384), 3)], axis=2)
    return {
        "rw_mu": fm(inputs["tshift_mu"], 13),
        "rw_par": np.ascontiguousarray(par),
        "rw_w_up": f(inputs["rwkv_w_up"]),
        "rw_a_up": f(inputs["rwkv_a_up"]),
        "rw_ln_g": f(inputs["rwkv_ln_g"]).reshape(L, 1, 384),
        "rw_ln_b": f(inputs["rwkv_ln_b"]).reshape(L, 1, 384),
    }


NIT = 14


def dsa_phase(k, l, env):
    nc, P = k.nc, k.P
    NB, S, T, NT, TS = env["NB"], env["S"], env["T"], env["NT"], env["TS"]
    dq, di, iwd, gates, ymix = env["dq"], env["di"], env["iwd"], env["gates"], env["ymix"]
    psb, ident, b_ident, identf, b_identf = k.psb, k.ident, k.b_ident, k.identf, k.b_identf
    TOPK = min(256, S // 4)
    with ExitStack() as ph:
        def sb(name, shape, dt):
            return ph.enter_context(k.sbt(name, list(shape), dt)), P.buf(name)
        qT, b_qT = sb("s_qT", [64, TS, 4, 128], BF16)
        kT, b_kT = sb("s_kT", [64, TS, 128], BF16)
        iqT, b_iqT = sb("s_iqT", [32, 8, TS, 128], BF16)
        ikT, b_ikT = sb("s_ikT", [32, TS, 128], BF16)
        va, b_va = sb("s_va", [128, TS, 65], BF16)
        iw, b_iw = sb("s_iw", [128, TS, 8], F32)
        id4, b_id4 = sb("s_id4", [128, 4, 128], BF16)
        P.op("pool", lambda e: e.memset(va[:], 1.0), writes=[b_va])
        for h in range(4):
            P.op("dve", lambda e, h=h: e.tensor_copy(id4[:, h, :], ident[:]), reads=[b_ident], writes=[b_id4])
        qds = [sb("s_qd%d" % j, [128, 384], BF16) for j in range(2)]
        ids = [sb("s_id%d" % j, [128, 288], BF16) for j in range(2)]
        diags = [sb("s_diag%d" % j, [128, 8, 128], BF16) for j in range(2)]
        rsb = [sb("s_r%d" % j, [128, 512], BF16) for j in range(3)]
        scs = [sb("s_sc%d" % j, [128, S], F32) for j in range(4)]
        junk, b_junk = sb("s_junk", [128, S], F32)
        junk2, b_junk2 = sb("s_junk2", [128, S], BF16)
        ssums = [sb("s_ssum%d" % j, [128, 1], F32) for j in range(4)]
        sgs = [sb("s_sg%d" % j, [128, 1], F32) for j in range(4)]
        cbs = [sb("s_cb%d" % j, [128, 1], F32) for j in range(4)]
        biass = [sb("s_bias%d" % j, [128, S], BF16) for j in range(8)]
        ams = [sb("s_am%d" % j, [128, 1], F32) for j in range(4)]
        thrs = [sb("s_thr%d" % j, [128, 1], F32) for j in range(4)]
        cnts = [sb("s_cnt%d" % j, [128, 1], F32) for j in range(4)]
        tmp1s = [sb("s_tmp1%d" % j, [128, 1], F32) for j in range(4)]
        ptas = [sb("s_pta%d" % j, [128, TS, 512], BF16) for j in range(2)]
        ost = [sb("s_o%d" % j, [128, 65], F32) for j in range(2)]
        rl = [sb("s_rl%d" % j, [128, 1], F32) for j in range(2)]
        yb = [sb("s_yb%d" % j, [128, 256], F32) for j in range(2)]
        gt = [sb("s_gt%d" % j, [128, 256], BF16) for j in range(2)]
        yo = [sb("s_yo%d" % j, [128, 256], BF16) for j in range(2)]
        for b in range(NB):
            for t in range(TS):
                rows = slice(b * S + t * 128, b * S + (t + 1) * 128)
                (qd, b_qd), (idt, b_idt) = qds[t % 2], ids[t % 2]
                P.dma(lambda e, qd=qd, rows=rows: e.dma_start(out=qd[:], in_=dq[rows, :]), writes=[b_qd])
                P.dma(lambda e, idt=idt, rows=rows: e.dma_start(out=idt[:], in_=di[rows, :]), writes=[b_idt])
                P.dma(lambda e, t=t, rows=rows: e.dma_start(out=iw[:, t, :], in_=iwd[rows, :]), writes=[b_iw])
                P.op("dve", lambda e, t=t: e.tensor_scalar(out=iw[:, t, :], in0=iw[:, t, :], scalar1=8.0 ** -0.5, scalar2=None, op0=ALU.mult), reads=[b_iw], writes=[b_iw])
                tx, b_tx = psb[0]
                ty, b_ty = psb[1]
                txv = tx[:].bitcast(BF16).rearrange("p (c t) -> p c t", t=128)
                tyv = ty[:].bitcast(BF16).rearrange("p (c t) -> p c t", t=128)
                for h in range(5):
                    P.op("pe", lambda e, h=h, qd=qd, txv=txv: e.transpose(out=txv[0:64, h, :], in_=qd[:, h * 64:(h + 1) * 64], identity=ident[:]),
                         reads=[b_qd, b_ident], writes=[b_tx])
                P.op("pe", lambda e, idt=idt, txv=txv: e.transpose(out=txv[0:32, 5, :], in_=idt[:, 256:288], identity=ident[:]),
                     reads=[b_idt, b_ident], writes=[b_tx])
                for h in range(8):
                    P.op("pe", lambda e, h=h, idt=idt, tyv=tyv: e.transpose(out=tyv[0:32, h, :], in_=idt[:, h * 32:(h + 1) * 32], identity=ident[:]),
                         reads=[b_idt, b_ident], writes=[b_ty])
                P.op("dve", lambda e, t=t, txv=txv: e.tensor_copy(qT[:, t, :, :], txv[0:64, 0:4, :]), reads=[b_tx], writes=[b_qT])
                P.op("dve", lambda e, t=t, txv=txv: e.tensor_copy(kT[:, t, :], txv[0:64, 4, :]), reads=[b_tx], writes=[b_kT])
                P.op("dve", lambda e, t=t, txv=txv: e.tensor_copy(ikT[:, t, :], txv[0:32, 5, :]), reads=[b_tx], writes=[b_ikT])
                P.op("dve", lambda e, t=t, tyv=tyv: e.tensor_copy(iqT[:, :, t, :], tyv[0:32, 0:8, :]), reads=[b_ty], writes=[b_iqT])
                P.op("dve", lambda e, t=t, qd=qd: e.tensor_copy(va[:, t, 0:64], qd[:, 320:384]), reads=[b_qd], writes=[b_va])
            rcnt = 0

            def tile_vars(j):
                nk = 128 * (j + 1)
                rows = slice(b * S + j * 128, b * S + (j + 1) * 128)
                dcols = slice(nk - 128, nk)
                return nk, rows, dcols

            def bisect_gen(j):
                nk, rows, dcols = tile_vars(j)
                (sc, b_sc), (bias, b_bias), (am, b_am) = scs[j % 4], biass[j % 8], ams[j % 4]
                (thr, b_thr), (cntt, b_cnt), (tmp1, b_tmp1) = thrs[j % 4], cnts[j % 4], tmp1s[j % 4]
                P.op("dve", lambda e: e.tensor_reduce(out=am[:], in_=sc[:, 0:nk], axis=AX.X, op=ALU.max, apply_absolute_value=True),
                     reads=[b_sc], writes=[b_am])
                yield
                P.op("dve", lambda e: e.tensor_scalar(out=am[:], in0=am[:], scalar1=1e-20, scalar2=None, op0=ALU.add), reads=[b_am], writes=[b_am])
                yield
                P.op("dve", lambda e: e.reciprocal(out=am[:], in_=am[:]), reads=[b_am], writes=[b_am])
                P.op("pool", lambda e: e.affine_select(out=sc[:, dcols], in_=sc[:, dcols], pattern=[[-1, 128]], compare_op=ALU.is_ge,
                                                       fill=k.fill(-1e30), base=0, channel_multiplier=1), reads=[b_sc], writes=[b_sc])
                yield
                P.op("dve", lambda e: e.tensor_scalar(out=sc[:, 0:nk], in0=sc[:, 0:nk], scalar1=am[:], scalar2=None, op0=ALU.mult),
                     reads=[b_sc, b_am], writes=[b_sc])
                P.op("dve", lambda e: e.memset(thr[:], 0.0), writes=[b_thr])
                yield
                if j % 2 == 0:
                    for it in range(NIT):
                        step = 2.0 ** -(it + 1)
                        P.op("dve", lambda e: e.tensor_scalar(out=junk[:, 0:nk], in0=sc[:, 0:nk], scalar1=thr[:], scalar2=0.0, op0=ALU.is_ge, op1=ALU.add,
                                                              accum_out=cntt[:]), reads=[b_sc, b_thr], writes=[b_junk, b_cnt])
                        yield
                        P.op("dve", lambda e: e.tensor_scalar(out=tmp1[:], in0=cntt[:], scalar1=float(TOPK), scalar2=2.0 * step, op0=ALU.is_ge, op1=ALU.mult),
                             reads=[b_cnt], writes=[b_tmp1])
                        yield
                        P.op("dve", lambda e: e.scalar_tensor_tensor(out=thr[:], in0=tmp1[:], scalar=-step, in1=thr[:], op0=ALU.add, op1=ALU.add),
                             reads=[b_tmp1, b_thr], writes=[b_thr])
                        yield
                    P.op("dve", lambda e: e.tensor_scalar(out=thr[:], in0=thr[:], scalar1=-(2.0 ** -NIT), scalar2=None, op0=ALU.add), reads=[b_thr], writes=[b_thr])
                else:
                    (ssum, b_ssum), (sg, b_sg), (cb, b_cb) = ssums[j % 4], sgs[j % 4], cbs[j % 4]
                    P.op("pool", lambda e: e.memset(cb[:], float(-(2 * TOPK - nk) + 0.5)), writes=[b_cb])
                    for it in range(NIT):
                        step = 2.0 ** -(it + 1)
                        P.op("act", lambda e: e.activation(out=junk2[:, 0:nk], in_=sc[:, 0:nk], func=AF.Sign, bias=thr[:], accum_out=ssum[:]),
                             reads=[b_sc, b_thr], writes=[b_junk2, b_ssum])
                        yield
                        P.op("act", lambda e: e.activation(out=sg[:], in_=ssum[:], func=AF.Sign, bias=cb[:]), reads=[b_ssum, b_cb], writes=[b_sg])
                        yield
                        P.op("act", lambda e, step=step: e.activation(out=thr[:], in_=sg[:], func=AF.Identity, scale=-step, bias=thr[:]),
                             reads=[b_sg, b_thr], writes=[b_thr])
                        yield
                    P.op("dve", lambda e: e.tensor_scalar(out=thr[:], in0=thr[:], scalar1=-1.0, scalar2=-(2.0 ** -NIT), op0=ALU.mult, op1=ALU.add), reads=[b_thr], writes=[b_thr])
                yield
                P.op("dve", lambda e: e.tensor_scalar(out=bias[:, 0:nk], in0=sc[:, 0:nk], scalar1=thr[:], scalar2=NEG, op0=ALU.is_lt, op1=ALU.mult),
                     reads=[b_sc, b_thr], writes=[b_bias])

            def stage1_gen(grp):
              nonlocal rcnt
              for j in grp:
                nk, rows, dcols = tile_vars(j)
                (diag, b_diag), (sc, b_sc), (bias, b_bias) = diags[j % 2], scs[j % 4], biass[j % 8]
                if nk > TOPK:
                    for h in range(8):
                        P.op("act", lambda e, h=h, j=j: e.activation(out=diag[:, h, :], in_=identf[:], func=AF.Copy, scale=iw[:, j, h:h + 1]),
                             reads=[b_identf, b_iw], writes=[b_diag])
                    for c0 in range(0, nk, 512):
                        n = min(512, nk - c0)
                        acc, b_acc = psb[2]
                        for h in range(8):
                            rp, b_rp = psb[rcnt % 2]
                            rs_, b_rs = rsb[rcnt % 3]
                            rcnt += 1
                            ikv = ikT[:].rearrange("p t k -> p (t k)")
                            P.op("pe", lambda e, rp=rp, h=h, j=j, c0=c0, n=n, ikv=ikv: e.matmul(rp[:, 0:n], lhsT=iqT[:, h, j, :], rhs=ikv[:, c0:c0 + n], start=True, stop=True),
                                 reads=[b_iqT, b_ikT], writes=[b_rp])
                            if True:
                                P.op("act", lambda e, rp=rp, rs_=rs_, n=n: e.activation(out=rs_[:, 0:n], in_=rp[:, 0:n], func=AF.Relu, scale=32.0 ** -0.5),
                                     reads=[b_rp], writes=[b_rs])
                            else:
                                P.op("dve", lambda e, rp=rp, rs_=rs_, n=n: e.tensor_scalar(out=rs_[:, 0:n], in0=rp[:, 0:n], scalar1=32.0 ** -0.5, scalar2=0.0,
                                                                                          op0=ALU.mult, op1=ALU.max), reads=[b_rp], writes=[b_rs])
                            P.op("pe", lambda e, acc=acc, h=h, rs_=rs_, n=n: e.matmul(acc[:, 0:n], lhsT=diag[:, h, :], rhs=rs_[:, 0:n], start=(h == 0), stop=(h == 7)),
                                 reads=[b_diag, b_rs], writes=[b_acc], acc=(b_acc, h == 0))
                            if h < 7:
                                yield
                        P.op("act", lambda e, acc=acc, c0=c0, n=n: e.activation(out=sc[:, c0:c0 + n], in_=acc[:, 0:n], func=AF.Copy), reads=[b_acc], writes=[b_sc])
                        yield
                else:
                    P.op("pool", lambda e, nk=nk: e.memset(bias[:, 0:nk], 0.0), writes=[b_bias])
                    P.op("pool", lambda e, dcols=dcols: e.affine_select(out=bias[:, dcols], in_=bias[:, dcols], pattern=[[-1, 128]], compare_op=ALU.is_ge,
                                                                        fill=k.fill(NEG), base=0, channel_multiplier=1), reads=[b_bias], writes=[b_bias])
            def stage2_gen(grp):
              gens = [bisect_gen(j) for j in grp if 128 * (j + 1) > TOPK]
              while gens:
                  for g in list(gens):
                      try:
                          next(g)
                      except StopIteration:
                          gens.remove(g)
                  yield
            def stage3_gen(grp):
              for j in grp:
                nk, rows, dcols = tile_vars(j)
                (bias, b_bias), (pta, b_pta) = biass[j % 8], ptas[j % 2]
                for kb in range(j + 1):
                    sp_, b_sp = psb[3 + (kb % 2)]
                    P.op("pe", lambda e, sp_=sp_, kb=kb, j=j: e.matmul(sp_[:], lhsT=kT[:, kb, :], rhs=qT[:, j, :, :].rearrange("p h q -> p (h q)"), start=True, stop=False),
                         reads=[b_kT, b_qT], writes=[b_sp], acc=(b_sp, True))
                    P.op("pe", lambda e, sp_=sp_, kb=kb: e.matmul(sp_[:], lhsT=bias[:, kb * 128:(kb + 1) * 128], rhs=id4[:].rearrange("p h q -> p (h q)"), start=False, stop=True),
                         reads=[b_bias, b_id4], writes=[b_sp], acc=(b_sp, False))
                    P.op("act", lambda e, sp_=sp_, kb=kb: e.activation(out=pta[:, kb, :], in_=sp_[:], func=AF.Exp, scale=0.125), reads=[b_sp], writes=[b_pta])
                    yield
                (ybt, b_ybt), (gtt, b_gtt), (yot, b_yot) = yb[j % 2], gt[j % 2], yo[j % 2]
                P.dma(lambda e, gtt=gtt, rows=rows: e.dma_start(out=gtt[:], in_=gates[rows, 0:256]), writes=[b_gtt])
                for h in range(4):
                    op_, b_op = psb[5 + (h % 2)]
                    (ot, b_ot), (rlt, b_rlt) = ost[h % 2], rl[h % 2]
                    for kb in range(j + 1):
                        P.op("pe", lambda e, op_=op_, kb=kb, h=h, j=j: e.matmul(op_[:, 0:65], lhsT=pta[:, kb, h * 128:(h + 1) * 128], rhs=va[:, kb, :], start=(kb == 0), stop=(kb == j)),
                             reads=[b_pta, b_va], writes=[b_op], acc=(b_op, kb == 0))
                    P.op("dve", lambda e, op_=op_, ot=ot: e.tensor_copy(ot[:], op_[:, 0:65]), reads=[b_op], writes=[b_ot])
                    P.op("dve", lambda e, ot=ot, rlt=rlt: e.reciprocal(out=rlt[:], in_=ot[:, 64:65]), reads=[b_ot], writes=[b_rlt])
                    P.op("dve", lambda e, ot=ot, rlt=rlt, ybt=ybt, h=h: e.tensor_scalar(out=ybt[:, h * 64:(h + 1) * 64], in0=ot[:, 0:64], scalar1=rlt[:], scalar2=None, op0=ALU.mult),
                         reads=[b_ot, b_rlt], writes=[b_ybt])
                    yield
                P.op("pool", lambda e, ybt=ybt, gtt=gtt, yot=yot: e.tensor_tensor(out=yot[:], in0=ybt[:], in1=gtt[:], op=ALU.mult), reads=[b_ybt, b_gtt], writes=[b_yot])
                P.dma(lambda e, yot=yot, rows=rows: e.dma_start(out=ymix[rows, 384:640], in_=yot[:]), reads=[b_yot], q="act")

            def chain(*gs):
                for g_ in gs:
                    yield from g_

            def lockstep(gl):
                gl = [g_ for g_ in gl if g_ is not None]
                while gl:
                    for g_ in list(gl):
                        try:
                            next(g_)
                        except StopIteration:
                            gl.remove(g_)
            groups = [list(range(j0, min(TS, j0 + 4))) for j0 in range(0, TS, 4)]
            lockstep([chain(stage1_gen(groups[0]), stage2_gen(groups[0]))])
            for gi in range(len(groups)):
                nxt = chain(stage1_gen(groups[gi + 1]), stage2_gen(groups[gi + 1])) if gi + 1 < len(groups) else None
                lockstep([stage3_gen(groups[gi]), nxt])


def rwkv_phase(k, l, env):
    nc, P = k.nc, k.P
    NB, S, T, NT, TS = env["NB"], env["S"], env["T"], env["NT"], env["TS"]
    zr, ymix = env["zr"], env["ymix"]
    rin = env["rw_in"]
    psb, ident, b_ident, identf, b_identf = k.psb, k.ident, k.b_ident, k.identf, k.b_identf
    CH = 512
    NCH = S // CH
    with ExitStack() as ph:
        def sb(name, shape, dt):
            return ph.enter_context(k.sbt(name, list(shape), dt)), P.buf(name)
        mu, b_mu = sb("r_mu", [128, 13], F32)
        omu, b_omu = sb("r_omu", [128, 13], F32)
        par, b_par = sb("r_par", [128, 5, 3], F32)
        omka, b_omka = sb("r_omka", [128, 3], F32)
        wup, b_wup = sb("r_wup", [128, 384], BF16)
        aup, b_aup = sb("r_aup", [128, 384], BF16)
        lng, b_lng = sb("r_lng", [128, 384], F32)
        lnb, b_lnb = sb("r_lnb", [128, 384], F32)
        P.dma(lambda e: e.dma_start(out=mu[:], in_=rin["mu"][l]), writes=[b_mu])
        P.dma(lambda e: e.dma_start(out=par[:], in_=rin["par"][l]), writes=[b_par])
        P.dma(lambda e: e.dma_start(out=wup[0:64, :], in_=rin["w_up"][l]), writes=[b_wup], q="pool")
        P.dma(lambda e: e.dma_start(out=aup[64:128, :], in_=rin["a_up"][l]), writes=[b_aup], q="pool")
        P.dma(lambda e: e.dma_start(out=lng[:], in_=rin["ln_g"][l].to_broadcast([128, 384])), writes=[b_lng])
        P.dma(lambda e: e.dma_start(out=lnb[:], in_=rin["ln_b"][l].to_broadcast([128, 384])), writes=[b_lnb])
        P.op("dve", lambda e: e.tensor_scalar(out=omu[:], in0=mu[:], scalar1=-1.0, scalar2=1.0, op0=ALU.mult, op1=ALU.add), reads=[b_mu], writes=[b_omu])
        P.op("dve", lambda e: e.tensor_scalar(out=omka[:], in0=par[:, 3, :], scalar1=-1.0, scalar2=1.0, op0=ALU.mult, op1=ALU.add), reads=[b_par], writes=[b_omka])
        bones, b_bones = sb("r_bones", [128, 128], BF16)
        ind2, b_ind2 = sb("r_ind2", [128, 2], BF16)
        rmask, b_rmask = sb("r_rmask", [128, CH], F32)
        cmk, b_cmk = sb("r_cmk", [128, 4, 128], F32)
        rowm, b_rowm = sb("r_rowm", [128, 2], F32)
        P.op("pool", lambda e: e.memset(bones[:], 0.0), writes=[b_bones])
        P.op("pool", lambda e: e.memset(bones[0:64, 0:64], 1.0), writes=[b_bones])
        P.op("pool", lambda e: e.memset(bones[64:128, 64:128], 1.0), writes=[b_bones])
        P.op("pool", lambda e: e.memset(ind2[:], 0.0), writes=[b_ind2])
        P.op("pool", lambda e: e.memset(ind2[0:64, 0:1], 1.0), writes=[b_ind2])
        P.op("pool", lambda e: e.memset(ind2[64:128, 1:2], 1.0), writes=[b_ind2])
        P.op("pool", lambda e: e.memset(rmask[:], 1.0), writes=[b_rmask])
        P.op("pool", lambda e: e.memset(rmask[:].rearrange("p (c t) -> p c t", t=64)[:, :, 0:1], 0.0), writes=[b_rmask])
        P.op("pool", lambda e: e.memset(rowm[:], 0.0), writes=[b_rowm])
        P.op("pool", lambda e: e.memset(rowm[0:64, 0:1], 1.0), writes=[b_rowm])
        P.op("pool", lambda e: e.memset(rowm[64:128, 1:2], 1.0), writes=[b_rowm])
        P.op("pool", lambda e: e.memset(cmk[:], 0.0), writes=[b_cmk])
        for cblk in range(2):
            ps_ = slice(cblk * 64, cblk * 64 + 64)
            for kind in range(3):
                P.op("pool", lambda e, ps_=ps_, kind=kind: e.memset(cmk[ps_, kind, ps_], 1.0), writes=[b_cmk])
        P.op("pool", lambda e: e.affine_select(out=cmk[:, 0, :], in_=cmk[:, 0, :], pattern=[[1, 128]], compare_op=ALU.is_gt, fill=k.fill(0.0), base=0, channel_multiplier=-1), reads=[b_cmk], writes=[b_cmk])
        P.op("pool", lambda e: e.affine_select(out=cmk[:, 1, :], in_=cmk[:, 1, :], pattern=[[1, 128]], compare_op=ALU.is_ge, fill=k.fill(0.0), base=0, channel_multiplier=-1), reads=[b_cmk], writes=[b_cmk])
        P.op("pool", lambda e: e.affine_select(out=cmk[:, 2, :], in_=cmk[:, 2, :], pattern=[[-1, 128]], compare_op=ALU.is_gt, fill=k.fill(0.0), base=0, channel_multiplier=1), reads=[b_cmk], writes=[b_cmk])
        zts = [sb("r_zt%d" % j, [128, RWIN], F32) for j in range(2)]
        zT, b_zT = sb("r_zT", [128, 13, CH + 1], F32)
        zs, b_zs = sb("r_zs", [128, 13, CH], F32)
        W = {}
        for nm in ("lw", "a", "kk", "kp", "be", "cw", "ep", "en", "epv", "ee", "t1", "t2"):
            W[nm] = sb("r_w_" + nm, [128, CH], F32)
        twd, b_twd = sb("r_twd", [128, CH], BF16)
        adb, b_adb = sb("r_adb", [128, CH], BF16)
        kk2, b_kk2 = sb("r_kk2", [128, CH], BF16)
        Oset = []
        for par_ in range(2):
            O_ = {}
            for nm in ("At", "Bt", "Kt", "Rt", "Bh", "Kh", "vb", "sg", "rk"):
                O_[nm] = sb("r_o%d_%s" % (par_, nm), [128, 3, CH], BF16)
            Oset.append(O_)
        wcs = [sb("r_wc%d" % j, [128, 3, CH // 64], F32) for j in range(2)]
        PAIR = []
        for mm in range(3):
            pr = dict(tokT=sb("r_tokT%d" % mm, [128, 4, 128], BF16), khc=sb("r_khc%d" % mm, [128, 2, 128], BF16), bhc=sb("r_bhc%d" % mm, [128, 2, 128], BF16),
                      atc=sb("r_atc%d" % mm, [128, 2, 128], BF16), rtc=sb("r_rtc%d" % mm, [128, 2, 128], BF16), upc=sb("r_upc%d" % mm, [128, 2, 128], BF16),
                      Hs=sb("r_Hs%d" % mm, [128, 64], F32), Hb=sb("r_Hb%d" % mm, [128, 3, 64], BF16))
            for nm in ("atc", "rtc"):
                t_, b_ = pr[nm]
                P.op("pool", lambda e, t_=t_: e.memset(t_[:], 0.0), writes=[b_])
            PAIR.append(pr)
        HEAD = []
        for hi in range(6):
            HEAD.append(dict(M4=sb("r_M4_%d" % hi, [128, 4, 128], BF16), Mrbc=sb("r_Mrbc%d" % hi, [128, 2, 128], BF16),
                             PP=[sb("r_PP%d_%d" % (hi, j), [128, 2, 128], BF16) for j in range(2)], X=sb("r_X%d" % hi, [128, 128], BF16),
                             P1=sb("r_P1_%d" % hi, [128, 64], BF16), Vp=sb("r_Vp%d" % hi, [128, 64], F32), G=sb("r_G%d" % hi, [128, 64], BF16)))
        TAIL = []
        for j in range(2):
            TAIL.append(dict(ytile=sb("r_ytile%d" % j, [128, 384], F32), ysq=sb("r_ysq%d" % j, [128, 64], F32), st1=sb("r_st1%d" % j, [128, 6], F32),
                             st2=sb("r_st2%d" % j, [128, 6], F32), bsc=sb("r_bsc%d" % j, [128, 6], F32), yout=sb("r_yout%d" % j, [128, 384], BF16),
                             vtok=sb("r_vtok%d" % j, [128, 3, 128], BF16), sgtok=sb("r_sgtok%d" % j, [128, 3, 128], BF16),
                             mean=sb("r_mean%d" % j, [128, 6], F32), rstd=sb("r_rstd%d" % j, [128, 6], F32), btmp=sb("r_btmp%d" % j, [128, 384], F32)))

        def w_(nm):
            return W[nm]

        def prep_gen(b, tc):
            r0 = b * S + tc * CH
            Oc = Oset[tc % 2]
            wcc, b_wcc = wcs[tc % 2]
            if tc == 0:
                P.op("pool", lambda e: e.memset(zT[:, :, 0:1], 0.0), writes=[b_zT])
            else:
                P.op("dve", lambda e: e.tensor_copy(zT[:, :, 0:1], zT[:, :, CH:CH + 1]), reads=[b_zT], writes=[b_zT])
            def load(tt):
                P.dma(lambda e: e.dma_start(out=zts[tt % 2][0][:, :], in_=zr[r0 + tt * 128:r0 + (tt + 1) * 128, :]), writes=[zts[tt % 2][1]])
            load(0)
            cnt = 0
            for tt in range(4):
                if tt + 1 < 4:
                    load(tt + 1)
                for c0 in range(0, 13, 2):
                    ncc = min(2, 13 - c0)
                    tb, b_tb = psb[6 + cnt % 2]
                    cnt += 1
                    tbv = tb[:, 256:512].rearrange("p (c t) -> p c t", t=128)
                    for c in range(ncc):
                        P.op("pe", lambda e, tbv=tbv, c=c, c0=c0, tt=tt: e.transpose(out=tbv[:, c, :], in_=zts[tt % 2][0][:, (c0 + c) * 128:(c0 + c + 1) * 128], identity=identf[:]),
                             reads=[zts[tt % 2][1], b_identf], writes=[b_tb])
                    eng = "act" if cnt % 2 else "dve"
                    if eng == "act":
                        P.op("act", lambda e, tbv=tbv, c0=c0, ncc=ncc, tt=tt: e.activation(out=zT[:, c0:c0 + ncc, 1 + tt * 128:1 + (tt + 1) * 128], in_=tbv[:, 0:ncc, :], func=AF.Copy),
                             reads=[b_tb], writes=[b_zT])
                    else:
                        P.op("dve", lambda e, tbv=tbv, c0=c0, ncc=ncc, tt=tt: e.tensor_copy(zT[:, c0:c0 + ncc, 1 + tt * 128:1 + (tt + 1) * 128], tbv[:, 0:ncc, :]),
                             reads=[b_tb], writes=[b_zT])
                    yield
            for c in range(13):
                if c % 2:
                    P.op("dve", lambda e, c=c: e.tensor_scalar(out=zs[:, c, :], in0=zT[:, c, 1:CH + 1], scalar1=omu[:, c:c + 1], scalar2=None, op0=ALU.mult),
                         reads=[b_zT, b_omu], writes=[b_zs])
                else:
                    P.op("act", lambda e, c=c: e.activation(out=zs[:, c, :], in_=zT[:, c, 1:CH + 1], func=AF.Copy, scale=omu[:, c:c + 1]),
                         reads=[b_zT, b_omu], writes=[b_zs])
                P.op("dve", lambda e, c=c: e.scalar_tensor_tensor(out=zs[:, c, :], in0=zT[:, c, 0:CH], scalar=mu[:, c:c + 1], in1=zs[:, c, :], op0=ALU.mult, op1=ALU.add),
                     reads=[b_zT, b_mu, b_zs], writes=[b_zs])
                yield
            P.op("act", lambda e: e.activation(out=twd[0:64, :], in_=zs[0:64, 12, :], func=AF.Tanh), reads=[b_zs], writes=[b_twd])
            P.op("dve", lambda e: e.tensor_copy(adb[64:128, :], zs[64:128, 12, :]), reads=[b_zs], writes=[b_adb])
            for m in range(3):
                fs = slice(m * 128, (m + 1) * 128)
                (lw, b_lw), (a_, b_a), (kk, b_kk), (kp, b_kp), (be, b_be), (cw, b_cw) = w_("lw"), w_("a"), w_("kk"), w_("kp"), w_("be"), w_("cw")
                (ep, b_ep), (en, b_en), (epv, b_epv), (ee, b_ee), (t1, b_t1), (t2, b_t2) = w_("ep"), w_("en"), w_("epv"), w_("ee"), w_("t1"), w_("t2")
                r_v, k_v, v_v, g_v = zs[:, m, :], zs[:, 3 + m, :], zs[:, 6 + m, :], zs[:, 9 + m, :]
                for hfx in range(2):
                    hs = slice(hfx * 256, (hfx + 1) * 256)
                    pw, b_pw = psb[6]
                    pa, b_pa = psb[7]
                    P.op("pe", lambda e, fs=fs, pw=pw, hs=hs: e.matmul(pw[:, 256:512], lhsT=wup[0:64, fs], rhs=twd[0:64, hs], start=True, stop=True), reads=[b_wup, b_twd], writes=[b_pw], rows="lo")
                    P.op("pe", lambda e, fs=fs, pa=pa, hs=hs: e.matmul(pa[:, 256:512], lhsT=aup[64:128, fs], rhs=adb[64:128, hs], start=True, stop=True), reads=[b_aup, b_adb], writes=[b_pa], rows="hi")
                    P.op("act", lambda e, lw=lw, pw=pw, m=m, hs=hs: e.activation(out=lw[:, hs], in_=pw[:, 256:512], func=AF.Sigmoid, bias=par[:, 0, m:m + 1]), reads=[b_pw, b_par], writes=[b_lw])
                    P.op("act", lambda e, a_=a_, pa=pa, m=m, hs=hs: e.activation(out=a_[:, hs], in_=pa[:, 256:512], func=AF.Sigmoid, bias=par[:, 1, m:m + 1]), reads=[b_pa, b_par], writes=[b_a])
                    yield
                P.op("act", lambda e, lw=lw: e.activation(out=lw[:], in_=lw[:], func=AF.Copy, scale=-DECAY_SCALE), reads=[b_lw], writes=[b_lw])
                yield
                P.op("dve", lambda e, kk=kk, k_v=k_v, m=m: e.tensor_scalar(out=kk[:], in0=k_v, scalar1=par[:, 2, m:m + 1], scalar2=None, op0=ALU.mult), reads=[b_zs, b_par], writes=[b_kk])
                P.op("dve", lambda e, kk=kk: e.tensor_tensor(out=kk2[:], in0=kk[:], in1=kk[:], op=ALU.mult), reads=[b_kk], writes=[b_kk2])
                yield
                for hfx in range(2):
                    hs = slice(hfx * 256, (hfx + 1) * 256)
                    pn, b_pn = psb[6 + hfx]
                    P.op("pe", lambda e, pn=pn, hs=hs: e.matmul(pn[:, 256:512], lhsT=bones[:], rhs=kk2[:, hs], start=True, stop=True), reads=[b_bones, b_kk2], writes=[b_pn])
                    P.op("act", lambda e, t1=t1, pn=pn, hs=hs: e.activation(out=t1[:, hs], in_=pn[:, 256:512], func=AF.Sqrt), reads=[b_pn], writes=[b_t1])
                yield
                P.op("dve", lambda e, t1=t1: e.tensor_scalar(out=t1[:], in0=t1[:], scalar1=1e-12, scalar2=None, op0=ALU.max), reads=[b_t1], writes=[b_t1])
                P.op("dve", lambda e, t1=t1: e.reciprocal(out=t1[:], in_=t1[:]), reads=[b_t1], writes=[b_t1])
                P.op("dve", lambda e, kk=kk, t1=t1: e.tensor_tensor(out=kk[:], in0=kk[:], in1=t1[:], op=ALU.mult), reads=[b_kk, b_t1], writes=[b_kk])
                yield
                P.op("dve", lambda e, kp=kp, a_=a_, m=m: e.tensor_scalar(out=kp[:], in0=a_[:], scalar1=par[:, 3, m:m + 1], scalar2=omka[:, m:m + 1], op0=ALU.mult, op1=ALU.add),
                     reads=[b_a, b_par, b_omka], writes=[b_kp])
                P.op("dve", lambda e, kp=kp, k_v=k_v: e.tensor_tensor(out=kp[:], in0=kp[:], in1=k_v, op=ALU.mult), reads=[b_kp, b_zs], writes=[b_kp])
                P.op("dve", lambda e, be=be, kk=kk, a_=a_: e.tensor_tensor(out=be[:], in0=kk[:], in1=a_[:], op=ALU.mult), reads=[b_kk, b_a], writes=[b_be])
                yield
                P.op("dve", lambda e, cw=cw, lw=lw: e.tensor_tensor_scan(out=cw[:], data0=rmask[:], data1=lw[:], initial=0.0, op0=ALU.mult, op1=ALU.add),
                     reads=[b_rmask, b_lw], writes=[b_cw])
                yield
                P.op("act", lambda e, ep=ep, cw=cw: e.activation(out=ep[:], in_=cw[:], func=AF.Exp), reads=[b_cw], writes=[b_ep])
                P.op("act", lambda e, en=en, cw=cw: e.activation(out=en[:], in_=cw[:], func=AF.Exp, scale=-1.0), reads=[b_cw], writes=[b_en])
                P.op("dve", lambda e, t2=t2, cw=cw, lw=lw: e.tensor_tensor(out=t2[:], in0=cw[:], in1=lw[:], op=ALU.subtract), reads=[b_cw, b_lw], writes=[b_t2])
                P.op("act", lambda e, epv=epv, t2=t2: e.activation(out=epv[:], in_=t2[:], func=AF.Exp), reads=[b_t2], writes=[b_epv])
                yield
                cw3 = cw[:].rearrange("p (c t) -> p c t", t=64)
                P.op("dve", lambda e, t2=t2, cw3=cw3: e.tensor_tensor(out=t2[:].rearrange("p (c t) -> p c t", t=64), in0=cw3[:, :, 63:64].to_broadcast([128, CH // 64, 64]), in1=cw3, op=ALU.subtract),
                     reads=[b_cw], writes=[b_t2])
                P.op("act", lambda e, ee=ee, t2=t2: e.activation(out=ee[:], in_=t2[:], func=AF.Exp), reads=[b_t2], writes=[b_ee])
                yield
                P.op("dve", lambda e, ep=ep, m=m: e.tensor_copy(wcc[:, m, :], ep[:].rearrange("p (c t) -> p c t", t=64)[:, :, 63]), reads=[b_ep], writes=[b_wcc])
                P.op("dve", lambda e, kk=kk, epv=epv, m=m: e.scalar_tensor_tensor(out=Oc["At"][0][:, m, :], in0=kk[:], scalar=-1.0, in1=epv[:], op0=ALU.mult, op1=ALU.mult),
                     reads=[b_kk, b_epv], writes=[Oc["At"][1]])
                P.op("dve", lambda e, be=be, en=en, m=m: e.tensor_tensor(out=Oc["Bt"][0][:, m, :], in0=be[:], in1=en[:], op=ALU.mult), reads=[b_be, b_en], writes=[Oc["Bt"][1]])
                P.op("dve", lambda e, kp=kp, en=en, m=m: e.tensor_tensor(out=Oc["Kt"][0][:, m, :], in0=kp[:], in1=en[:], op=ALU.mult), reads=[b_kp, b_en], writes=[Oc["Kt"][1]])
                yield
                P.op("dve", lambda e, r_v=r_v, ep=ep, m=m: e.tensor_tensor(out=Oc["Rt"][0][:, m, :], in0=r_v, in1=ep[:], op=ALU.mult), reads=[b_zs, b_ep], writes=[Oc["Rt"][1]])
                P.op("dve", lambda e, be=be, ee=ee, m=m: e.tensor_tensor(out=Oc["Bh"][0][:, m, :], in0=be[:], in1=ee[:], op=ALU.mult), reads=[b_be, b_ee], writes=[Oc["Bh"][1]])
                yield
                P.op("dve", lambda e, kp=kp, ee=ee, m=m: e.tensor_tensor(out=Oc["Kh"][0][:, m, :], in0=kp[:], in1=ee[:], op=ALU.mult), reads=[b_kp, b_ee], writes=[Oc["Kh"][1]])
                P.op("act", lambda e, v_v=v_v, m=m: e.activation(out=Oc["vb"][0][:, m, :], in_=v_v, func=AF.Copy), reads=[b_zs], writes=[Oc["vb"][1]])
                yield
                P.op("act", lambda e, g_v=g_v, m=m: e.activation(out=Oc["sg"][0][:, m, :], in_=g_v, func=AF.Silu), reads=[b_zs], writes=[Oc["sg"][1]])
                P.op("dve", lambda e, t1=t1, r_v=r_v, kp=kp: e.tensor_tensor(out=t1[:], in0=r_v, in1=kp[:], op=ALU.mult), reads=[b_zs, b_kp], writes=[b_t1])
                P.op("dve", lambda e, t1=t1, m=m: e.tensor_scalar(out=Oc["rk"][0][:, m, :], in0=t1[:], scalar1=par[:, 4, m:m + 1], scalar2=None, op0=ALU.mult),
                     reads=[b_t1, b_par], writes=[Oc["rk"][1]])

        for b in range(NB):
            for pr in PAIR:
                P.op("pool", lambda e, t_=pr["Hs"][0]: e.memset(t_[:], 0.0), writes=[pr["Hs"][1]])
                P.op("pool", lambda e, t_=pr["Hb"][0]: e.memset(t_[:], 0.0), writes=[pr["Hb"][1]])
            spi = 0
            for _ in prep_gen(b, 0):
                pass
            for tc in range(NCH):
                r0 = b * S + tc * CH
                O = Oset[tc % 2]
                wc, b_wc = wcs[tc % 2]
                bg = [prep_gen(b, tc + 1)] if tc + 1 < NCH else []
                Lc = locals()
                fns = {}
                for sp in range(CH // 128):
                    fns[sp] = RWKV_SPAN_FNS(k, Lc, sp, spi + sp)
                def lockstep(gl):
                    while gl:
                        for g in list(gl) + list(bg):
                            try:
                                next(g)
                            except StopIteration:
                                (gl if g in gl else bg).remove(g)
                for sp in range(CH // 128):
                    gl = [fns[sp][0](0, 0), fns[sp][0](1, 1)]
                    if sp > 0:
                        gl.append(fns[sp - 1][1](2, 0))
                    lockstep(gl)
                    if sp > 0:
                        fns[sp - 1][2]()
                    lockstep([fns[sp][0](2, 0), fns[sp][1](0, 0), fns[sp][1](1, 1)])
                lockstep([fns[CH // 128 - 1][1](2, 0)])
                fns[CH // 128 - 1][2]()
                for g in bg:
                    for _ in g:
                        pass
                spi += CH // 128


def RWKV_SPAN_FNS(k, L, sp, spi):
    nc, P = k.nc, k.P
    psb, ident, b_ident = k.psb, k.ident, k.b_ident
    O, wc, b_wc = L["O"], L["wc"], L["b_wc"]
    cmk, b_cmk, rowm, b_rowm, ind2, b_ind2 = L["cmk"], L["b_cmk"], L["rowm"], L["b_rowm"], L["ind2"], L["b_ind2"]
    lng, b_lng, lnb, b_lnb = L["lng"], L["b_lng"], L["lnb"], L["b_lnb"]
    PAIR, HEAD, TAIL = L["PAIR"], L["HEAD"], L["TAIL"]
    ymix = L["ymix"]
    r0 = L["r0"]
    ts = slice(sp * 128, (sp + 1) * 128)
    rows = slice(r0 + sp * 128, r0 + (sp + 1) * 128)
    T_ = TAIL[spi % 2]
    (ytile, b_ytile), (ysq, b_ysq), (st1, b_st1), (st2, b_st2), (bsc, b_bsc) = T_["ytile"], T_["ysq"], T_["st1"], T_["st2"], T_["bsc"]
    (yout, b_yout), (vtok, b_vtok), (sgtok, b_sgtok), (mean, b_mean), (rstd, b_rstd), (btmp, b_btmp) = T_["yout"], T_["vtok"], T_["sgtok"], T_["mean"], T_["rstd"], T_["btmp"]
    s_in, s_mid, s_out = (2 * spi) % 3, (2 * spi + 1) % 3, (2 * spi + 2) % 3

    def fm(nm, m, hb=slice(0, 128)):
        t_, b_ = O[nm]
        return t_[hb, m, ts], b_

    def head_pre(m, hh, slot):
        pr = PAIR[m]
        (tokT, b_tokT) = pr["tokT"]
        hb = slice(hh * 64, hh * 64 + 64)
        rg = "hi" if hh else "lo"
        d_ = HEAD[2 * m + hh]
        At, b_At = fm("At", m, hb)
        Bt, b_Bt = fm("Bt", m, hb)
        Kt, b_Kt = fm("Kt", m, hb)
        Rt, b_Rt = fm("Rt", m, hb)
        bk, b_bk = psb[1 + 2 * slot + hh]
        bkv = bk[:].rearrange("p (c t) -> p c t", t=128)
        for i_, (lh, rh, bl, br) in enumerate(((Bt, At, b_Bt, b_At), (Kt, At, b_Kt, b_At), (Bt, Rt, b_Bt, b_Rt), (At, Bt, b_At, b_Bt))):
            P.op("pe", lambda e, i_=i_, lh=lh, rh=rh: e.matmul(bkv[:, i_, :], lhsT=lh, rhs=rh, start=True, stop=True), reads=[bl, br], writes=[b_bk], rows=rg)
        yield
        (M4t, b_M4t) = d_["M4"]
        (pp0, b_pp0) = d_["PP"][0]
        P.op("dve", lambda e: e.tensor_tensor(out=M4t[:, 0:2, :], in0=bkv[:, 0:2, :], in1=cmk[:, 0:1, :].to_broadcast([128, 2, 128]), op=ALU.mult),
             reads=[b_bk, b_cmk], writes=[b_M4t])
        P.op("dve", lambda e: e.tensor_tensor(out=pp0[:, 1, :], in0=bkv[:, 3, :], in1=cmk[:, 2, :], op=ALU.mult), reads=[b_bk, b_cmk], writes=[b_pp0])
        P.op("dve", lambda e: e.tensor_tensor(out=M4t[:, 2, :], in0=bkv[:, 2, :], in1=cmk[:, 1, :], op=ALU.mult), reads=[b_bk, b_cmk], writes=[b_M4t])
        N0, Mak, Mrb, Mrk = M4t[:, 0, :], M4t[:, 1, :], M4t[:, 2, :], M4t[:, 3, :]
        yield
        P.op("dve", lambda e: e.tensor_copy(pp0[:, 0, :], N0), reads=[b_M4t], writes=[b_pp0])
        (X, b_X) = d_["X"]
        P.op("dve", lambda e: e.tensor_tensor(out=X[:], in0=N0, in1=ident[:], op=ALU.add), reads=[b_M4t, b_ident], writes=[b_X])
        P.op("pe", lambda e: e.matmul(bk[:, 384:512], lhsT=Kt, rhs=Rt, start=True, stop=True), reads=[b_Kt, b_Rt], writes=[b_bk], rows=rg)
        (mrbc, b_mrbc) = d_["Mrbc"]
        for c in range(2):
            P.op("dve", lambda e, c=c: e.tensor_scalar(out=mrbc[:, c, :], in0=Mrb, scalar1=rowm[:, c:c + 1], scalar2=None, op0=ALU.mult),
                 reads=[b_M4t, b_rowm], writes=[b_mrbc])
        yield
        P.op("dve", lambda e: e.tensor_tensor(out=M4t[:, 3, :], in0=bk[:, 384:512], in1=cmk[:, 1, :], op=ALU.mult), reads=[b_bk, b_cmk], writes=[b_M4t])
        for i_ in range(5):
            (pc, b_pc) = d_["PP"][i_ % 2]
            (pn_, b_pn) = d_["PP"][(i_ + 1) % 2]
            p3v = bk[:, 0:256].rearrange("p (c t) -> p c t", t=128)
            P.op("pe", lambda e, pc=pc: e.matmul(p3v[:, 0, :], lhsT=pc[:, 1, :], rhs=pc[:, 0, :], start=True, stop=True), reads=[b_pc], writes=[b_bk])
            P.op("pe", lambda e, pc=pc: e.matmul(p3v[:, 1, :], lhsT=pc[:, 0, :], rhs=pc[:, 1, :], start=True, stop=True), reads=[b_pc], writes=[b_bk])
            yield
            P.op("act", lambda e, pn_=pn_: e.activation(out=pn_[:], in_=p3v, func=AF.Copy), reads=[b_bk], writes=[b_pn])
            yield
            P.op("pe", lambda e, pn_=pn_: e.matmul(bk[:, 256:384], lhsT=pn_[:, 1, :], rhs=X[:], start=True, stop=True), reads=[b_pn, b_X], writes=[b_bk])
            yield
            P.op("dve", lambda e: e.tensor_tensor(out=X[:], in0=bk[:, 256:384], in1=X[:], op=ALU.add), reads=[b_bk, b_X], writes=[b_X])
            yield
        (P1, b_P1), (Vp, b_Vp) = d_["P1"], d_["Vp"]
        P.op("pe", lambda e: e.matmul(bk[:, 0:64], lhsT=Mak, rhs=tokT[:, 0, hb], start=True, stop=True), reads=[b_M4t, b_tokT], writes=[b_bk])
        yield
        P.op("act", lambda e: e.activation(out=P1[:], in_=bk[:, 0:64], func=AF.Copy), reads=[b_bk], writes=[b_P1])
        yield
        P.op("pe", lambda e: e.matmul(bk[:, 64:128], lhsT=X[:], rhs=P1[:], start=True, stop=True), reads=[b_X, b_P1], writes=[b_bk])
        yield
        P.op("act", lambda e: e.activation(out=Vp[:], in_=bk[:, 64:128], func=AF.Copy), reads=[b_bk], writes=[b_Vp])

    def pre_gen(m, slot):
        pr = PAIR[m]
        (tokT, b_tokT), (khc, b_khc), (bhc, b_bhc), (atc, b_atc), (rtc, b_rtc) = pr["tokT"], pr["khc"], pr["bhc"], pr["atc"], pr["rtc"]
        (upc, b_upc), (Hs, b_Hs), (Hb, b_Hb) = pr["upc"], pr["Hs"], pr["Hb"]
        tb, b_tb = psb[0]
        tbv = tb[:].bitcast(BF16).rearrange("p (c t) -> p c t", t=128)[:, 4 * slot:4 * slot + 4, :]
        for i_, nm in enumerate(("vb", "Kh", "Bh", "sg")):
            src, b_src = fm(nm, m)
            P.op("pe", lambda e, i_=i_, src=src: e.transpose(out=tbv[:, i_, :], in_=src, identity=ident[:]), reads=[b_src, b_ident], writes=[b_tb])
        rk_src, b_rk = fm("rk", m)
        pbs, b_pbs = psb[1 + 2 * slot]
        P.op("pe", lambda e: e.matmul(pbs[:, 384:386], lhsT=rk_src, rhs=ind2[:], start=True, stop=True), reads=[b_rk, b_ind2], writes=[b_pbs])
        yield
        P.op("act", lambda e: e.activation(out=tokT[:], in_=tbv, func=AF.Copy), reads=[b_tb], writes=[b_tokT])
        P.op("dve", lambda e: e.tensor_copy(bsc[:, 2 * m:2 * m + 2], pbs[:, 384:386]), reads=[b_pbs], writes=[b_bsc])
        yield
        P.op("dve", lambda e: e.tensor_copy(vtok[:, m, :], tokT[:, 0, :]), reads=[b_tokT], writes=[b_vtok])
        P.op("dve", lambda e: e.tensor_copy(sgtok[:, m, :], tokT[:, 3, :]), reads=[b_tokT], writes=[b_sgtok])
        at_src, b_at = fm("At", m)
        rt_src, b_rt = fm("Rt", m)
        for c in range(2):
            P.op("dve", lambda e, c=c: e.tensor_scalar(out=khc[:, c, :], in0=tokT[:, 1, :], scalar1=rowm[:, c:c + 1], scalar2=None, op0=ALU.mult),
                 reads=[b_tokT, b_rowm], writes=[b_khc])
            P.op("dve", lambda e, c=c: e.tensor_scalar(out=bhc[:, c, :], in0=tokT[:, 2, :], scalar1=rowm[:, c:c + 1], scalar2=None, op0=ALU.mult),
                 reads=[b_tokT, b_rowm], writes=[b_bhc])
            cs = slice(c * 64, (c + 1) * 64)
            P.op("dve", lambda e, c=c, cs=cs: e.tensor_copy(atc[:, c, cs], at_src[:, cs]), reads=[b_at], writes=[b_atc])
            P.op("dve", lambda e, c=c, cs=cs: e.tensor_copy(rtc[:, c, cs], rt_src[:, cs]), reads=[b_rt], writes=[b_rtc])
        hg = [head_pre(m, 0, slot), head_pre(m, 1, slot)]
        while hg:
            for g in list(hg):
                try:
                    next(g)
                except StopIteration:
                    hg.remove(g)
            yield

    def serial_gen(m, slot):
        b5 = slot * 256
        b67 = slot * 128
        pr = PAIR[m]
        (tokT, b_tokT), (khc, b_khc), (bhc, b_bhc), (atc, b_atc), (rtc, b_rtc) = pr["tokT"], pr["khc"], pr["bhc"], pr["atc"], pr["rtc"]
        (upc, b_upc), (Hs, b_Hs), (Hb, b_Hb) = pr["upc"], pr["Hs"], pr["Hb"]
        slots = (s_in, s_mid, s_out)
        p5, b_p5 = psb[5]
        for c in range(2):
            cg = sp * 2 + c
            sl_i, sl_o = slots[c], slots[c + 1]
            for hh in range(2):
                hb = slice(hh * 64, hh * 64 + 64)
                gp, b_gp = psb[6 + hh]
                P.op("pe", lambda e, gp=gp, hb=hb, c=c, sl_i=sl_i: e.matmul(gp[:, b67 + 64:b67 + 128], lhsT=atc[hb, c, :], rhs=Hb[hb, sl_i, :], start=True, stop=True),
                     reads=[b_atc, b_Hb], writes=[b_gp], rows=("hi" if hh else "lo"))
            yield
            for hh in range(2):
                (G, b_G) = HEAD[2 * m + hh]["G"]
                gp, b_gp = psb[6 + hh]
                P.op("act", lambda e, G=G, gp=gp: e.activation(out=G[:], in_=gp[:, b67 + 64:b67 + 128], func=AF.Copy), reads=[b_gp], writes=[b_G])
            yield
            for hh in range(2):
                (G, b_G), (X, b_X) = HEAD[2 * m + hh]["G"], HEAD[2 * m + hh]["X"]
                P.op("pe", lambda e, hh=hh, X=X, G=G: e.matmul(p5[:, b5 + hh * 64:b5 + (hh + 1) * 64], lhsT=X[:], rhs=G[:], start=True, stop=True), reads=[b_X, b_G], writes=[b_p5])
            yield
            for hh in range(2):
                (Vp, b_Vp) = HEAD[2 * m + hh]["Vp"]
                P.op("dve", lambda e, hh=hh, Vp=Vp, c=c: e.tensor_tensor(out=upc[:, c, hh * 64:(hh + 1) * 64], in0=p5[:, b5 + hh * 64:b5 + (hh + 1) * 64], in1=Vp[:], op=ALU.add),
                     reads=[b_p5, b_Vp], writes=[b_upc])
            yield
            P.op("pe", lambda e, c=c: e.matmul(p5[:, b5 + 128:b5 + 256], lhsT=khc[:, c, :], rhs=tokT[:, 0, :], start=True, stop=False), reads=[b_khc, b_tokT], writes=[b_p5], acc=(b_p5, True))
            P.op("pe", lambda e, c=c: e.matmul(p5[:, b5 + 128:b5 + 256], lhsT=bhc[:, c, :], rhs=upc[:, c, :], start=False, stop=True), reads=[b_bhc, b_upc], writes=[b_p5], acc=(b_p5, False))
            yield
            for hh in range(2):
                hb = slice(hh * 64, hh * 64 + 64)
                P.op("dve", lambda e, hb=hb, hh=hh, cg=cg: e.scalar_tensor_tensor(
                    out=Hs[hb, :], in0=Hs[hb, :], scalar=wc[hb, m, cg:cg + 1], in1=p5[hb, b5 + 128 + hh * 64:b5 + 128 + (hh + 1) * 64], op0=ALU.mult, op1=ALU.add),
                    reads=[b_Hs, b_wc, b_p5], writes=[b_Hs])
            yield
            P.op("act", lambda e, sl_o=sl_o: e.activation(out=Hb[:, sl_o, :], in_=Hs[:], func=AF.Copy), reads=[b_Hs], writes=[b_Hb])
            yield
        for hh in range(2):
            hb = slice(hh * 64, hh * 64 + 64)
            idx = 2 * m + hh
            py, b_py = psb[6 + hh]
            d_ = HEAD[idx]
            (M4t, b_M4t), (mrbc, b_mrbc) = d_["M4"], d_["Mrbc"]
            P.op("pe", lambda e, py=py, M4t=M4t, hb=hb: e.matmul(py[:, b67:b67 + 64], lhsT=M4t[:, 3, :], rhs=tokT[:, 0, hb], start=True, stop=False), reads=[b_M4t, b_tokT], writes=[b_py], acc=(b_py, True))
            for c in range(2):
                P.op("pe", lambda e, py=py, hb=hb, c=c: e.matmul(py[:, b67:b67 + 64], lhsT=rtc[hb, c, :], rhs=Hb[hb, slots[c], :], start=False, stop=False),
                     reads=[b_rtc, b_Hb], writes=[b_py], acc=(b_py, False), rows=("hi" if hh else "lo"))
                P.op("pe", lambda e, py=py, hb=hb, c=c, mrbc=mrbc: e.matmul(py[:, b67:b67 + 64], lhsT=mrbc[:, c, :], rhs=upc[:, c, hb], start=False, stop=(c == 1)),
                     reads=[b_mrbc, b_upc], writes=[b_py], acc=(b_py, False))
        yield
        for hh in range(2):
            idx = 2 * m + hh
            py, b_py = psb[6 + hh]
            P.op("act", lambda e, py=py, idx=idx: e.activation(out=ytile[:, idx * 64:(idx + 1) * 64], in_=py[:, b67:b67 + 64], func=AF.Copy, accum_out=st1[:, idx:idx + 1]),
                 reads=[b_py], writes=[b_ytile, b_st1])
            P.op("act", lambda e, py=py, idx=idx: e.activation(out=ysq[:], in_=py[:, b67:b67 + 64], func=AF.Square, accum_out=st2[:, idx:idx + 1]),
                 reads=[b_py], writes=[b_ysq, b_st2])

    def tail():
        P.op("dve", lambda e: e.tensor_scalar(out=mean[:], in0=st1[:], scalar1=1.0 / 64, scalar2=None, op0=ALU.mult), reads=[b_st1], writes=[b_mean])
        P.op("dve", lambda e: e.tensor_tensor(out=rstd[:], in0=mean[:], in1=mean[:], op=ALU.mult), reads=[b_mean], writes=[b_rstd])
        P.op("dve", lambda e: e.scalar_tensor_tensor(out=rstd[:], in0=st2[:], scalar=1.0 / 64, in1=rstd[:], op0=ALU.mult, op1=ALU.subtract), reads=[b_st2, b_rstd], writes=[b_rstd])
        P.op("dve", lambda e: e.tensor_scalar(out=rstd[:], in0=rstd[:], scalar1=GN_EPS, scalar2=None, op0=ALU.add), reads=[b_rstd], writes=[b_rstd])
        P.op("act", lambda e: e.activation(out=rstd[:], in_=rstd[:], func=AF.Sqrt), reads=[b_rstd], writes=[b_rstd])
        P.op("dve", lambda e: e.reciprocal(out=rstd[:], in_=rstd[:]), reads=[b_rstd], writes=[b_rstd])
        for idx in range(6):
            P.op("dve", lambda e, idx=idx: e.tensor_scalar(out=ytile[:, idx * 64:(idx + 1) * 64], in0=ytile[:, idx * 64:(idx + 1) * 64], scalar1=mean[:, idx:idx + 1],
                                                           scalar2=rstd[:, idx:idx + 1], op0=ALU.subtract, op1=ALU.mult), reads=[b_ytile, b_mean, b_rstd], writes=[b_ytile])
        P.op("dve", lambda e: e.tensor_tensor(out=ytile[:], in0=ytile[:], in1=lng[:], op=ALU.mult), reads=[b_ytile, b_lng], writes=[b_ytile])
        P.op("dve", lambda e: e.tensor_tensor(out=ytile[:], in0=ytile[:], in1=lnb[:], op=ALU.add), reads=[b_ytile, b_lnb], writes=[b_ytile])
        vt6 = vtok[:].rearrange("p m (h v) -> p (m h) v", h=2)
        P.op("dve", lambda e: e.tensor_tensor(out=btmp[:].rearrange("p (i v) -> p i v", v=64), in0=vt6, in1=bsc[:].unsqueeze(2).to_broadcast([128, 6, 64]), op=ALU.mult),
             reads=[b_vtok, b_bsc], writes=[b_btmp])
        P.op("dve", lambda e: e.tensor_tensor(out=ytile[:], in0=ytile[:], in1=btmp[:], op=ALU.add), reads=[b_ytile, b_btmp], writes=[b_ytile])
        P.op("dve", lambda e: e.tensor_tensor(out=yout[:], in0=ytile[:], in1=sgtok[:].rearrange("p m f -> p (m f)"), op=ALU.mult), reads=[b_ytile, b_sgtok], writes=[b_yout])
        P.dma(lambda e: e.dma_start(out=ymix[rows, 0:384], in_=yout[:]), reads=[b_yout], q="act")

    return pre_gen, serial_gen, tail


_CACHE = {}


def rope_table(S):
    def tab(rot):
        half = rot // 2
        inv = (500000.0 ** (-np.arange(half, dtype=np.float32) / half)).astype(np.float32)
        ang = np.arange(S, dtype=np.float32)[:, None] * inv[None, :]
        return np.cos(ang).astype(np.float32), np.sin(ang).astype(np.float32)
    c64, s64 = tab(16)
    c32, s32 = tab(8)
    return np.ascontiguousarray(np.concatenate([c64, s64, c32, s32], axis=1).astype(np.float32))


def fm8(v):
    L = v.shape[0]
    return np.ascontiguousarray(v.reshape(L, 8, 128).transpose(0, 2, 1))


def make_in_maps(inputs, NB, S, DEPTH, ncores):
    f = lambda a: np.ascontiguousarray(np.asarray(a, dtype=np.float32))
    x = f(inputs["x"])
    p = f(inputs["p"])
    maps = []
    shared = {
        "norm_g": fm8(f(inputs["norm_g"])),
        "w_in": f(inputs["w_in"]),
        "w_out": f(inputs["w_out"]),
        "ple_norm_g": fm8(f(inputs["ple_norm_g"])),
        "ple_w_gate": f(inputs["ple_w_gate"]),
        "ple_w_proj": f(inputs["ple_w_proj"]),
        "final_norm_g": f(inputs["final_norm_g"]).reshape(1, D),
        "rope": rope_table(S),
    }
    shared.update(mixer_inputs(inputs, DEPTH))
    for c in range(ncores):
        m = dict(shared)
        m["x"] = np.ascontiguousarray(x[c * NB:(c + 1) * NB].reshape(NB * S, D))
        m["p"] = np.ascontiguousarray(p[:, c * NB:(c + 1) * NB].reshape(DEPTH, NB * S, 256))
        maps.append(m)
    return maps


def run(inputs, NB, S, DEPTH, ncores, dbg=()):
    key = (NB, S, DEPTH, tuple(dbg))
    if key not in _CACHE:
        _CACHE[key] = build(NB, S, DEPTH, dbg)
    nc = _CACHE[key]
    maps = make_in_maps(inputs, NB, S, DEPTH, ncores)
    res = run_bass_kernel_spmd(nc, maps, core_ids=list(range(ncores)))
    return res.results


def kernel(**inputs):
    B, S, _ = inputs["x"].shape
    DEPTH = inputs["w_in"].shape[0]
    ncores = 8
    NB = B // ncores
    results = run(inputs, NB, S, DEPTH, ncores)
    out = np.concatenate([r["out"].reshape(NB, S, D) for r in results], axis=0)
    return out.astype(np.float32)
```
